# Optimizing a Trainium2 kernel written in Bass

```python
import math
import jax
import jax.numpy as jnp
from jax import lax
import numpy as np

D_MODEL = 1024
BATCH = 8
SEQ = 2048
DEPTH = 2

HEAD_DIM = 64
N_HEADS = D_MODEL // HEAD_DIM
N_MIXERS = 4
GROUP_HEADS = N_HEADS // N_MIXERS
GROUP_WIDTH = GROUP_HEADS * HEAD_DIM
D_FF = 4 * D_MODEL
NORM_EPS = 1e-6
Q_BLOCK = 128
NEG_INF = -1e30
FORCE = 1e30
TINY = 1e-30
N_BUCKETS = 32
MAX_DISTANCE = 128
N_BIAS_HEADS = 3 * GROUP_HEADS
MOBA_BLOCK = 256
MOBA_TOPK = 3
MOBA_Q_CHUNK = 64
NSA_KV_DIM = HEAD_DIM
CMP_LEN = 32
CMP_STRIDE = 16
CMP_HIDDEN = 256
SLC_LEN = 64
SLC_TOPN = 4
WINDOW = 512
DIFF_HALF = HEAD_DIM // 2
SPLIT_SIZES = ((GROUP_WIDTH,) * 3
               + (GROUP_WIDTH,) * 3
               + (GROUP_WIDTH,)
               + (NSA_KV_DIM,) * 6
               + (3 * GROUP_HEADS,)
               + (GROUP_WIDTH,) * 3)
D_IN = sum(SPLIT_SIZES)

kernel_name = "hybrid_sb_moba_nsa_diff_block"


def _rmsnorm(x, g):
    xf = x.astype(jnp.float32)
    y = xf * lax.rsqrt(jnp.mean(xf * xf, axis=-1, keepdims=True) + NORM_EPS)
    return (y * g.astype(jnp.float32)).astype(x.dtype)


def _masked_softmax(logits, mask):
    s = jnp.where(mask, logits, NEG_INF)
    m = jnp.max(s, axis=-1, keepdims=True)
    e = jnp.where(mask, jnp.exp(s - m), 0.0)
    return e / jnp.maximum(jnp.sum(e, axis=-1, keepdims=True), TINY)


def _t5_bucket(dist):
    n = jnp.maximum(dist, 0)
    max_exact = N_BUCKETS // 2
    nf = jnp.maximum(n, 1).astype(jnp.float32)
    large = max_exact + (jnp.log(nf / max_exact) / math.log(MAX_DISTANCE / max_exact)
                         * (N_BUCKETS - max_exact)).astype(jnp.int32)
    large = jnp.minimum(large, N_BUCKETS - 1)
    return jnp.where(n < max_exact, n, large)


def _rel_bias(dist, table):
    return jnp.moveaxis(table[_t5_bucket(dist)], -1, 0).astype(jnp.float32)


def _rel_bias_per_head(dist, table):
    h = table.shape[1]
    h_idx = jnp.arange(h).reshape((1, h) + (1,) * (dist.ndim - 2))
    return table.T[h_idx, _t5_bucket(dist)].astype(jnp.float32)


def _heads(t, n_heads):
    b, s, _ = t.shape
    return t.reshape(b, s, n_heads, -1).transpose(0, 2, 1, 3)


def _merge_heads(t):
    b, h, s, d = t.shape
    return t.transpose(0, 2, 1, 3).reshape(b, s, h * d)


def _q_blocks(t, block):
    b, h, s = t.shape[:3]
    t = t.reshape((b, h, s // block, block) + t.shape[3:])
    return jnp.moveaxis(t, 2, 0)


def _unblock(t):
    t = jnp.moveaxis(t, 0, 2)
    b, h, n, blk = t.shape[:4]
    return t.reshape((b, h, n * blk) + t.shape[4:])


def stick_breaking_attention(q, k, v):
    b, h, s, d = q.shape
    scale = d ** -0.5
    kpos = jnp.arange(s)

    def one_block(args):
        qb, i = args
        qpos = i * Q_BLOCK + jnp.arange(Q_BLOCK)
        z = jnp.einsum('bhqd,bhkd->bhqk', qb, k).astype(jnp.float32) * scale
        mask = kpos[None, :] < qpos[:, None]
        log_keep = jnp.where(mask, -jax.nn.softplus(z), 0.0)
        suffix = lax.cumsum(log_keep, axis=3, reverse=True) - log_keep
        a = jnp.where(mask, jnp.exp(jax.nn.log_sigmoid(z) + suffix), 0.0)
        return jnp.einsum('bhqk,bhkd->bhqd', a.astype(v.dtype), v)

    out = lax.map(one_block, (_q_blocks(q, Q_BLOCK), jnp.arange(s // Q_BLOCK)))
    return _unblock(out)


def moba_attention(q, k, v, table):
    b, h, s, d = q.shape
    scale = d ** -0.5
    n_blk = -(-s // MOBA_BLOCK)
    pad = n_blk * MOBA_BLOCK - s
    k_blk = jnp.pad(k, ((0, 0), (0, 0), (0, pad), (0, 0))).reshape(b, h, n_blk, MOBA_BLOCK, d)
    v_blk = jnp.pad(v, ((0, 0), (0, 0), (0, pad), (0, 0))).reshape(b, h, n_blk, MOBA_BLOCK, d)
    k_mean = jnp.mean(k_blk.astype(jnp.float32), axis=3)
    n_sel = min(MOBA_TOPK, n_blk - 1)
    blk_ids = jnp.arange(n_blk)
    in_blk = jnp.arange(MOBA_BLOCK)
    b_idx = jnp.arange(b)[:, None, None, None]
    h_idx = jnp.arange(h)[None, :, None, None]

    def one_chunk(args):
        qc, i = args
        qpos = i * MOBA_Q_CHUNK + jnp.arange(MOBA_Q_CHUNK)
        own = (i * MOBA_Q_CHUNK) // MOBA_BLOCK
        k_own = lax.dynamic_index_in_dim(k_blk, own, axis=2, keepdims=False)
        v_own = lax.dynamic_index_in_dim(v_blk, own, axis=2, keepdims=False)
        dist_own = qpos[:, None] - (own * MOBA_BLOCK + in_blk)[None, :]
        s_own = (jnp.einsum('bhqd,bhkd->bhqk', qc, k_own).astype(jnp.float32) * scale
                 + _rel_bias(dist_own, table)[None])
        m_own = jnp.broadcast_to(dist_own >= 0, s_own.shape)
        if n_sel == 0:
            p = _masked_softmax(s_own, m_own)
            return jnp.einsum('bhqk,bhkd->bhqd', p.astype(v.dtype), v_own)
        gate = jnp.einsum('bhqd,bhnd->bhqn', qc.astype(jnp.float32), k_mean)
        gate = jnp.where(blk_ids < own, gate, NEG_INF)
        _, sel = lax.top_k(gate, n_sel)
        k_sel = k_blk[b_idx, h_idx, sel]
        v_sel = v_blk[b_idx, h_idx, sel]
        sel_pos = sel[..., None] * MOBA_BLOCK + in_blk
        dist_sel = qpos[:, None, None] - sel_pos
        s_sel = (jnp.einsum('bhqd,bhqnkd->bhqnk', qc, k_sel).astype(jnp.float32) * scale
                 + _rel_bias_per_head(dist_sel, table))
        m_sel = jnp.broadcast_to((sel < own)[..., None], s_sel.shape)
        n_flat = n_sel * MOBA_BLOCK
        s_all = jnp.concatenate([s_sel.reshape(b, h, MOBA_Q_CHUNK, n_flat), s_own], axis=-1)
        m_all = jnp.concatenate([m_sel.reshape(b, h, MOBA_Q_CHUNK, n_flat), m_own], axis=-1)
        p = _masked_softmax(s_all, m_all)
        p_sel = p[..., :n_flat].reshape(s_sel.shape)
        p_own = p[..., n_flat:]
        return (jnp.einsum('bhqnk,bhqnkd->bhqd', p_sel.astype(v.dtype), v_sel)
                + jnp.einsum('bhqk,bhkd->bhqd', p_own.astype(v.dtype), v_own))

    out = lax.map(one_chunk, (_q_blocks(q, MOBA_Q_CHUNK), jnp.arange(s // MOBA_Q_CHUNK)))
    return _unblock(out)


def nsa_attention(q, k_c, v_c, k_s, v_s, k_w, v_w, gates, pos_k, pos_v, wk1, wk2, wv1, wv2, table):
    b, h, s, d = q.shape
    scale = d ** -0.5
    tpos = jnp.arange(s)
    n_cmp = (s - CMP_LEN) // CMP_STRIDE + 1
    cmp_idx = np.arange(n_cmp)[:, None] * CMP_STRIDE + np.arange(CMP_LEN)[None, :]
    cmp_end = jnp.asarray(cmp_idx[:, -1])

    def compress(t, pos, w1, w2):
        blocks = t[:, cmp_idx] + pos
        return jax.nn.gelu(blocks.reshape(b, n_cmp, CMP_LEN * d) @ w1) @ w2

    kc = compress(k_c, pos_k, wk1, wk2)
    vc = compress(v_c, pos_v, wv1, wv2)
    dist_c = tpos[:, None] - cmp_end[None, :]
    s_c = (jnp.einsum('bhtd,bcd->bhtc', q, kc).astype(jnp.float32) * scale
           + _rel_bias(dist_c, table)[None])
    p_c = _masked_softmax(s_c, dist_c >= 0)
    o_cmp = jnp.einsum('bhtc,bcd->bhtd', p_c.astype(vc.dtype), vc)
    n_slc = s // SLC_LEN
    n_top = min(SLC_TOPN, n_slc)
    s_start = np.arange(n_slc) * SLC_LEN
    cover = np.clip(np.minimum(cmp_idx[:, -1][:, None], (s_start + SLC_LEN - 1)[None, :])
                    - np.maximum(cmp_idx[:, 0][:, None], s_start[None, :]) + 1, 0, None) / CMP_LEN
    importance = jnp.einsum('btc,cj->btj', jnp.sum(p_c, axis=1), jnp.asarray(cover, jnp.float32))
    own = tpos // SLC_LEN
    blk = jnp.arange(n_slc)
    imp = jnp.where(blk[None, :] == own[:, None], FORCE,
                    jnp.where(blk[None, :] < own[:, None], importance, NEG_INF))
    _, sel = lax.top_k(imp, n_top)
    n_qb = s // Q_BLOCK
    sel_b = sel.reshape(b, n_qb, Q_BLOCK, n_top).transpose(1, 0, 2, 3)
    ks_blk = k_s.reshape(b, n_slc, SLC_LEN, d)
    vs_blk = v_s.reshape(b, n_slc, SLC_LEN, d)
    kw_pad = jnp.pad(k_w, ((0, 0), (WINDOW, 0), (0, 0)))
    vw_pad = jnp.pad(v_w, ((0, 0), (WINDOW, 0), (0, 0)))
    b_idx = jnp.arange(b)[:, None, None]
    in_slc = jnp.arange(SLC_LEN)
    in_win = jnp.arange(WINDOW + Q_BLOCK)

    def one_block(args):
        qb, selc, i = args
        qpos = i * Q_BLOCK + jnp.arange(Q_BLOCK)
        k_sel = ks_blk[b_idx, selc]
        v_sel = vs_blk[b_idx, selc]
        dist_s = qpos[:, None, None] - (selc[..., None] * SLC_LEN + in_slc)
        s_s = (jnp.einsum('bhqd,bqnkd->bhqnk', qb, k_sel).astype(jnp.float32) * scale
               + jnp.moveaxis(table[_t5_bucket(dist_s)], -1, 1).astype(jnp.float32))
        m_s = jnp.broadcast_to((dist_s >= 0)[:, None], s_s.shape)
        flat = (b, h, Q_BLOCK, n_top * SLC_LEN)
        p_s = _masked_softmax(s_s.reshape(flat), m_s.reshape(flat)).reshape(s_s.shape)
        o_s = jnp.einsum('bhqnk,bqnkd->bhqd', p_s.astype(v_sel.dtype), v_sel)
        kw = lax.dynamic_slice_in_dim(kw_pad, i * Q_BLOCK, WINDOW + Q_BLOCK, axis=1)
        vw = lax.dynamic_slice_in_dim(vw_pad, i * Q_BLOCK, WINDOW + Q_BLOCK, axis=1)
        kpos = i * Q_BLOCK - WINDOW + in_win
        dist_w = qpos[:, None] - kpos[None, :]
        m_w = (dist_w >= 0) & (dist_w < WINDOW) & (kpos[None, :] >= 0)
        s_w = (jnp.einsum('bhqd,bkd->bhqk', qb, kw).astype(jnp.float32) * scale
               + _rel_bias(dist_w, table)[None])
        p_w = _masked_softmax(s_w, m_w)
        o_w = jnp.einsum('bhqk,bkd->bhqd', p_w.astype(vw.dtype), vw)
        return o_s, o_w

    o_slc, o_win = lax.map(one_block, (_q_blocks(q, Q_BLOCK), sel_b, jnp.arange(n_qb)))
    o_slc, o_win = _unblock(o_slc), _unblock(o_win)
    g = jax.nn.sigmoid(gates.astype(jnp.float32)).reshape(b, s, 3, h).transpose(2, 0, 3, 1)[..., None]
    g = g.astype(q.dtype)
    return g[0] * o_cmp + g[1] * o_slc + g[2] * o_win


def diff_attention(q, k, v, lam, table):
    s = q.shape[2]
    scale = q.shape[-1] ** -0.5
    kpos = jnp.arange(s)

    def one_block(args):
        qb, i = args
        qpos = i * Q_BLOCK + jnp.arange(Q_BLOCK)
        dist = qpos[:, None] - kpos[None, :]
        sc = (jnp.einsum('bhqcd,bhkcd->bhcqk', qb, k).astype(jnp.float32) * scale
              + _rel_bias(dist, table)[None, :, None])
        p = _masked_softmax(sc, dist >= 0)
        w = p[:, :, 0] - lam * p[:, :, 1]
        return jnp.einsum('bhqk,bhkd->bhqd', w.astype(v.dtype), v)

    out = lax.map(one_block, (_q_blocks(q, Q_BLOCK), jnp.arange(s // Q_BLOCK)))
    return _unblock(out)


def setup_inputs(seed: int = 0) -> dict:
    key = jax.random.key(seed)
    ks = jax.random.split(key, 17)
    f32 = jnp.float32

    def nrm(k, shape, scale):
        return jax.random.normal(k, shape, f32) * scale

    return {
        "x": nrm(ks[0], (BATCH, SEQ, D_MODEL), 1.0),
        "w_in": nrm(ks[1], (DEPTH, D_MODEL, D_IN), D_MODEL ** -0.5),
        "w_out": nrm(ks[2], (DEPTH, D_MODEL, D_MODEL), D_MODEL ** -0.5),
        "w_up": nrm(ks[3], (DEPTH, D_MODEL, D_FF), D_MODEL ** -0.5),
        "w_down": nrm(ks[4], (DEPTH, D_FF, D_MODEL), D_FF ** -0.5),
        "norm_attn": 1.0 + nrm(ks[5], (DEPTH, D_MODEL), 0.05),
        "norm_mlp": 1.0 + nrm(ks[6], (DEPTH, D_MODEL), 0.05),
        "cmp_pos_k": nrm(ks[7], (DEPTH, CMP_LEN, HEAD_DIM), 0.1),
        "cmp_pos_v": nrm(ks[8], (DEPTH, CMP_LEN, HEAD_DIM), 0.1),
        "cmp_k_w1": nrm(ks[9], (DEPTH, CMP_LEN * HEAD_DIM, CMP_HIDDEN), (CMP_LEN * HEAD_DIM) ** -0.5),
        "cmp_k_w2": nrm(ks[10], (DEPTH, CMP_HIDDEN, HEAD_DIM), CMP_HIDDEN ** -0.5),
        "cmp_v_w1": nrm(ks[11], (DEPTH, CMP_LEN * HEAD_DIM, CMP_HIDDEN), (CMP_LEN * HEAD_DIM) ** -0.5),
        "cmp_v_w2": nrm(ks[12], (DEPTH, CMP_HIDDEN, HEAD_DIM), CMP_HIDDEN ** -0.5),
        "diff_lambda": nrm(ks[13], (DEPTH, 4, DIFF_HALF), 0.1),
        "diff_subln": 1.0 + nrm(ks[14], (DEPTH, HEAD_DIM), 0.05),
        "rel_bias": nrm(ks[15], (N_BUCKETS, N_BIAS_HEADS), 0.2),
        "final_norm": 1.0 + nrm(ks[16], (D_MODEL,), 0.05),
    }


def reference(x, w_in, w_out, w_up, w_down, norm_attn, norm_mlp, cmp_pos_k, cmp_pos_v,
              cmp_k_w1, cmp_k_w2, cmp_v_w1, cmp_v_w2, diff_lambda, diff_subln, rel_bias, final_norm):
    bias_moba = rel_bias[:, :GROUP_HEADS]
    bias_nsa = rel_bias[:, GROUP_HEADS:2 * GROUP_HEADS]
    bias_diff = rel_bias[:, 2 * GROUP_HEADS:]
    split_at = np.cumsum(SPLIT_SIZES)[:-1].tolist()
    b, s, _ = x.shape
    for layer in range(DEPTH):
        h = _rmsnorm(x, norm_attn[layer])
        proj = h @ w_in[layer]
        (sb_q, sb_k, sb_v, mb_q, mb_k, mb_v, ns_q, ns_kc, ns_vc, ns_ks, ns_vs, ns_kw, ns_vw,
         ns_g, df_q, df_k, df_v) = jnp.split(proj, split_at, axis=-1)
        o_sb = stick_breaking_attention(_heads(sb_q, GROUP_HEADS), _heads(sb_k, GROUP_HEADS),
                                        _heads(sb_v, GROUP_HEADS))
        o_mb = moba_attention(_heads(mb_q, GROUP_HEADS), _heads(mb_k, GROUP_HEADS),
                              _heads(mb_v, GROUP_HEADS), bias_moba)
        o_ns = nsa_attention(_heads(ns_q, GROUP_HEADS), ns_kc, ns_vc, ns_ks, ns_vs, ns_kw, ns_vw, ns_g,
                             cmp_pos_k[layer], cmp_pos_v[layer], cmp_k_w1[layer], cmp_k_w2[layer],
                             cmp_v_w1[layer], cmp_v_w2[layer], bias_nsa)
        lambda_init = 0.8 - 0.6 * math.exp(-0.3 * layer)
        lv = diff_lambda[layer].astype(jnp.float32)
        lam = jnp.exp(jnp.sum(lv[0] * lv[1])) - jnp.exp(jnp.sum(lv[2] * lv[3])) + lambda_init
        dq = df_q.reshape(b, s, GROUP_HEADS, 2, DIFF_HALF).transpose(0, 2, 1, 3, 4)
        dk = df_k.reshape(b, s, GROUP_HEADS, 2, DIFF_HALF).transpose(0, 2, 1, 3, 4)
        o_df = diff_attention(dq, dk, _heads(df_v, GROUP_HEADS), lam, bias_diff)
        o_df = _rmsnorm(o_df, diff_subln[layer]) * (1.0 - lambda_init)
        mixed = jnp.concatenate([_merge_heads(o_sb), _merge_heads(o_mb),
                                 _merge_heads(o_ns), _merge_heads(o_df)], axis=-1)
        x = x + mixed @ w_out[layer]
        h2 = _rmsnorm(x, norm_mlp[layer])
        x = x + jnp.square(jax.nn.relu(h2 @ w_up[layer])) @ w_down[layer]
    return _rmsnorm(x, final_norm)
```

```python
import numpy as np
import concourse.bass as bass
import concourse.mybir as mybir
from concourse.bass_utils import run_bass_kernel_spmd

F32 = mybir.dt.float32
BF16 = mybir.dt.bfloat16
AF = mybir.ActivationFunctionType
ALU = mybir.AluOpType
AX = mybir.AxisListType

D_MODEL = 1024
SEQ = 2048
DEPTH = 2
D_FF = 4096
D_IN = 2956
NCORES = 8
EPS = 1e-6

_DTSIZE = {F32: 4, BF16: 2, mybir.dt.int32: 4, mybir.dt.uint32: 4, mybir.dt.uint16: 2,
           mybir.dt.int16: 2, mybir.dt.uint8: 1, mybir.dt.int8: 1, mybir.dt.float32r: 4,
           mybir.dt.float16: 2}


def _prod(xs):
    r = 1
    for v in xs:
        r *= int(v)
    return r


def region(ap):
    t = ap.tensor
    name = t.name
    esz = _DTSIZE[ap.dtype]
    off = int(ap.offset)
    dims = ap.ap
    space = str(ap.space)
    if 'DRAM' in space.upper() or 'HBM' in space.upper() or type(t).__name__.startswith('DRam'):
        lo = off
        hi = off
        for (st, cnt) in dims:
            if st >= 0:
                hi += (cnt - 1) * st
            else:
                lo += (cnt - 1) * st
        return (name, 0, 1, lo * esz, (hi + 1) * esz)
    tsz = _DTSIZE[t.dtype]
    pstride = _prod(list(t.shape)[1:]) * tsz // esz
    p0 = off // pstride
    f0 = off % pstride
    (pst, pcnt) = dims[0]
    assert pst % pstride == 0 or pcnt == 1, (name, dims, pstride)
    pstep = max(1, pst // pstride)
    p1 = p0 + (pcnt - 1) * pstep + 1
    lo = f0
    hi = f0
    for (st, cnt) in dims[1:]:
        if st >= 0:
            hi += (cnt - 1) * st
        else:
            lo += (cnt - 1) * st
    return (name, p0, p1, lo * esz, (hi + 1) * esz)


def _overlap(a, b):
    return a[1] < b[2] and b[1] < a[2] and a[3] < b[4] and b[3] < a[4]


def _covers(a, b):
    return a[1] <= b[1] and a[2] >= b[2] and a[3] <= b[3] and a[4] >= b[4]


_CACHE = {}


class _Op:
    __slots__ = ('idx', 'eng', 'fn', 'dma', 'semkey', 'deps', 'ordinal', 'waits', 'sig', 'semval', 'pe_group', 'phase')


class Prog:
    ENGS = ('pe', 'act', 'dve', 'pool', 'sp')

    def __init__(self, nc):
        self.nc = nc
        self.ops = []
        self.acc = {}
        self.final_dma_keys = []
        self.phase = ''

    def add(self, eng, fn, reads=(), writes=(), dma=False, semkey=None):
        op = _Op()
        op.idx = len(self.ops)
        op.eng = eng
        op.fn = fn
        op.dma = dma
        op.deps = set()
        op.phase = self.phase
        rregs = [region(a) for a in reads]
        wregs = [region(a) for a in writes]
        def _banks(r):
            return [(r[0], 0, 128, bk * 2048, (bk + 1) * 2048) for bk in range(r[3] // 2048, (r[4] - 1) // 2048 + 1)]
        ps_r = [x for r in rregs if r[0].startswith('PS') for x in _banks(r)]
        rregs = [r for r in rregs if not r[0].startswith('PS')]
        wregs = [x for w in wregs for x in (_banks(w) if w[0].startswith('PS') else [w])] + ps_r
        if dma:
            op.semkey = semkey if semkey is not None else ('dma_' + wregs[0][0])
        else:
            op.semkey = None
        stream = op.semkey if dma else eng
        for r in rregs:
            lst = self.acc.get(r[0], [])
            for rec in lst:
                if rec[1] and _overlap(rec[0], r):
                    op.deps.update(rec[2].values())
        for w in wregs:
            lst = self.acc.get(w[0], [])
            for rec in lst:
                if _overlap(rec[0], w):
                    op.deps.update(rec[2].values())
        for w in wregs:
            lst = self.acc.setdefault(w[0], [])
            lst[:] = [rec for rec in lst if not _covers(w, rec[0])]
            lst.append([w, True, {stream: op.idx}])
        for r in rregs:
            lst = self.acc.setdefault(r[0], [])
            done = False
            for rec in lst:
                if (not rec[1]) and rec[0] == r:
                    rec[2][stream] = op.idx
                    done = True
                    break
            if not done:
                lst.append([r, False, {stream: op.idx}])
        op.deps.discard(op.idx)
        self.ops.append(op)
        return op

    def dma(self, q, out, in_, semkey=None):
        return self.add(q, lambda e: e.dma_start(out=out, in_=in_), reads=[in_], writes=[out],
                        dma=True, semkey=semkey)

    def emit(self):
        nc = self.nc
        ops = self.ops
        cnt = {}
        for op in ops:
            s = op.semkey if op.dma else op.eng
            cnt[s] = cnt.get(s, 0) + 1
            op.ordinal = cnt[s]
            op.sig = op.dma
            op.waits = []
        waited = {e: {} for e in self.ENGS}
        import bisect
        dma_idx = {}
        for op in ops:
            if op.dma:
                dma_idx.setdefault(op.semkey, []).append(op.idx)
        for op in ops:
            need = {}
            for d in op.deps:
                a = ops[d]
                s = a.semkey if a.dma else a.eng
                if (not a.dma) and a.eng == op.eng and op.eng == 'pe' and not op.dma:
                    continue
                o = a.ordinal
                if a.dma:
                    o = bisect.bisect_left(dma_idx[s], op.idx)
                if o > need.get(s, 0):
                    need[s] = o
            w = waited[op.eng]
            for s, o in need.items():
                if w.get(s, 0) >= o:
                    continue
                w[s] = o
                op.waits.append((s, o))
        needed = set()
        for op in ops:
            for so in op.waits:
                needed.add(so)
        semvals = {}
        run = {}
        for op in ops:
            s = op.semkey if op.dma else op.eng
            if op.dma:
                run[s] = run.get(s, 0) + 16
                semvals[(s, op.ordinal)] = run[s]
            else:
                if (s, op.ordinal) in needed:
                    op.sig = True
                    run[s] = run.get(s, 0) + 1
                    semvals[(s, op.ordinal)] = run[s]
        final_vals = dict(run)
        streams = sorted(run.keys())
        from contextlib import ExitStack
        with ExitStack() as es:
            sems = {}
            for s in streams:
                sems[s] = es.enter_context(nc.semaphore('s_' + s))
            block = es.enter_context(nc.Block())
            per_eng = {e: [op for op in ops if op.eng == e] for e in self.ENGS}
            final_keys = list(self.final_dma_keys)

            def body(e, eops, is_last_waiter):
                for op in eops:
                    for (s, o) in op.waits:
                        e.wait_ge(sems[s], semvals[(s, o)])
                    ins = op.fn(e)
                    if op.sig:
                        s = op.semkey if op.dma else op.eng
                        ins.then_inc(sems[s], 16 if op.dma else 1)
                if is_last_waiter:
                    for k in final_keys:
                        e.wait_ge(sems[k], final_vals[k])

            @block.tensor
            def _(e):
                body(e, per_eng['pe'], False)

            @block.scalar
            def _(e):
                body(e, per_eng['act'], False)

            @block.vector
            def _(e):
                body(e, per_eng['dve'], False)

            @block.gpsimd
            def _(e):
                body(e, per_eng['pool'], False)

            @block.sync
            def _(e):
                body(e, per_eng['sp'], True)
        return len(ops)


import math
from contextlib import ExitStack

NEGV = -30000.0
BIGM = 240000.0
WINC = 784
GELU_C = math.sqrt(2.0 / math.pi)

PK_NA = 0
PK_NM = 16
PK_NF = 32
PK_SUBLN = 40
PK_TAB = 44
PK_LAM = 64
PK_POS = 320
NPK = 384

L_N, L_W, L_C = 768, 1152, 4096
OFF_N, OFF_W, OFF_C = 127, 127, 2063


def _t5_bucket(n):
    n = np.maximum(n, 0)
    nf = np.maximum(n, 1).astype(np.float32)
    large = 16 + (np.log(nf / np.float32(16)) / np.float32(math.log(128 / 16)) * np.float32(16)).astype(np.int32)
    large = np.minimum(large, 31)
    return np.where(n < 16, n, large)


def _onehot(L, off, win):
    oh = np.zeros((33, L), np.float32)
    d = np.arange(L) - off
    masked = d < 0
    if win:
        masked = masked | (d >= 512)
    b = _t5_bucket(d)
    for i in range(L):
        if masked[i]:
            oh[32, i] = 1.0
        else:
            oh[b[i], i] = 1.0
    return oh


def _cover():
    cmp_idx = np.arange(127)[:, None] * 16 + np.arange(32)[None, :]
    s_start = np.arange(32) * 64
    cover = np.clip(np.minimum(cmp_idx[:, -1][:, None], (s_start + 63)[None, :])
                    - np.maximum(cmp_idx[:, 0][:, None], s_start[None, :]) + 1, 0, None) / 32.0
    cv = np.zeros((128, 33), np.float32)
    cv[:127, :32] = cover
    cv[:127, 32] = 1.0
    return cv


def host_consts():
    return {'ohn': _onehot(L_N, OFF_N, False), 'ohw': _onehot(L_W, OFF_W, True),
            'ohc': _onehot(L_C, OFF_C, False), 'cover': _cover()}


def pack_small(inputs):
    pk = np.zeros((128, NPK), np.float32)
    for l in range(DEPTH):
        pk[:, PK_NA + l * 8:PK_NA + l * 8 + 8] = inputs['norm_attn'][l].reshape(8, 128).T
        pk[:, PK_NM + l * 8:PK_NM + l * 8 + 8] = inputs['norm_mlp'][l].reshape(8, 128).T
        pk[:, PK_SUBLN + l] = np.tile(inputs['diff_subln'][l], 2)
        pk[0, PK_LAM + l * 128:PK_LAM + (l + 1) * 128] = inputs['diff_lambda'][l].reshape(-1)
        pk[0:64, PK_POS + l * 32:PK_POS + (l + 1) * 32] = inputs['cmp_pos_k'][l].T
        pk[64:128, PK_POS + l * 32:PK_POS + (l + 1) * 32] = inputs['cmp_pos_v'][l].T
    pk[:, PK_NF:PK_NF + 8] = inputs['final_norm'].reshape(8, 128).T
    pk[0:32, PK_TAB:PK_TAB + 12] = inputs['rel_bias']
    return pk


def permute_w_in(w_in):
    L = w_in.shape[0]
    out = np.zeros((L, 4, D_MODEL, WINC), np.float32)
    out[:, 0, :, 0:768] = w_in[:, :, 0:768]
    out[:, 1, :, 0:768] = w_in[:, :, 768:1536]
    o = 1536
    ns = out[:, 2]
    ns[:, :, 0:256] = w_in[:, :, o:o + 256]
    ns[:, :, 256:320] = w_in[:, :, o + 256:o + 320]
    ns[:, :, 320:384] = w_in[:, :, o + 320:o + 384]
    ns[:, :, 384:448] = w_in[:, :, o + 384:o + 448]
    ns[:, :, 448:512] = w_in[:, :, o + 384:o + 448]
    ns[:, :, 512:576] = w_in[:, :, o + 512:o + 576]
    ns[:, :, 576:640] = w_in[:, :, o + 512:o + 576]
    ns[:, :, 640:652] = w_in[:, :, o + 640:o + 652]
    ns[:, :, 652:716] = w_in[:, :, o + 448:o + 512]
    ns[:, :, 716:780] = w_in[:, :, o + 576:o + 640]
    out[:, 3, :, 0:768] = w_in[:, :, 2188:2956]
    return out


def build(cfg=None):
    cfg = cfg or {}
    mixers = cfg.get('mixers', (0, 1, 2, 3))
    depth = cfg.get('depth', DEPTH)
    do_mlp = cfg.get('mlp', True)
    dbg = cfg.get('dbg', None)
    stop = cfg.get('stop', 99)
    filler = cfg.get('filler', 0)
    probe_wait = cfg.get('probe_wait', 0)
    pk = cfg.get('probe_k', 128)
    pb = cfg.get('probe_b', 0)
    nc = bass.Bass("TRN2", target_bir_lowering=False)

    def din(name, shape, dt=F32):
        return nc.dram_tensor(name, shape, dt, kind="ExternalInput").ap()
    xT_d = din("xT", [D_MODEL, SEQ])
    pk_d = din("pk", [128, NPK])
    w_in_d = din("w_in", [DEPTH, 4, D_MODEL, WINC])
    w_out_d = din("w_out", [DEPTH, D_MODEL, D_MODEL])
    w_up_d = din("w_up", [DEPTH, D_MODEL, D_FF])
    w_down_d = din("w_down", [DEPTH, D_FF, D_MODEL])
    ck1_d = din("cmp_k_w1", [DEPTH, 2048, 256])
    ck2_d = din("cmp_k_w2", [DEPTH, 256, 64])
    cv1_d = din("cmp_v_w1", [DEPTH, 2048, 256])
    cv2_d = din("cmp_v_w2", [DEPTH, 256, 64])
    ohn_d = din("ohn", [33, L_N])
    ohw_d = din("ohw", [33, L_W])
    ohc_d = din("ohc", [33, L_C])
    cover_d = din("cover", [128, 33])
    outT_d = nc.dram_tensor("outT", [D_MODEL, SEQ], F32, kind="ExternalOutput").ap()
    dbg_d = nc.dram_tensor("dbg", [256, SEQ], BF16, kind="ExternalOutput").ap() if dbg is not None else None
    scr_n = [nc.dram_tensor("scr_n%d" % h, [128 * (L_N + 1) + 8], F32, kind="Internal").ap() for h in range(12)]
    scr_w = [nc.dram_tensor("scr_w%d" % h, [128 * (L_W + 1) + 8], F32, kind="Internal").ap() for h in range(4)]
    scr_c = [nc.dram_tensor("scr_c%d" % h, [128 * (L_C + 16) + 8], F32, kind="Internal").ap() for h in range(4)]

    with ExitStack() as es:
        def sb(name, shape, dt):
            return es.enter_context(nc.sbuf_tensor(name, shape, dt))

        XT = sb("XT", [128, 8, SEQ], F32)
        ARENA_B = 106 * 1024
        ARENA = sb("ARENA", [128, ARENA_B // 2], BF16)

        def av(off, shape, dt):
            nbytes = _prod(shape) * _DTSIZE[dt]
            assert off % 4 == 0 and off + nbytes <= ARENA_B, (off, shape)
            a = ARENA[:, off // 2:(off + nbytes) // 2]
            if dt != BF16:
                a = a.bitcast(dt)
            if len(shape) == 2:
                a = a.rearrange("p (a b) -> p a b", a=shape[0])
            elif len(shape) == 3:
                a = a.rearrange("p (a b c) -> p a b c", a=shape[0], b=shape[1])
            return a
        KB = 1024
        HT = av(0, [8, SEQ], BF16)
        QT = av(32 * KB, [2, SEQ], BF16)
        KT = av(40 * KB, [2, SEQ], BF16)
        GTF = av(40 * KB, [SEQ], F32)
        KX = av(48 * KB, [2, SEQ], BF16)
        VT = av(56 * KB, [16, 4, 128], BF16)
        WIN = av(72 * KB, [8, WINC], BF16)
        W1 = av(72 * KB, [32, 256], BF16)
        WO = av(85 * KB, [2, D_MODEL], BF16)
        MIXM = av(89 * KB, [2, SEQ], BF16)
        GREG = 97 * KB
        WUP = [av(32 * KB + i * 8 * KB, [8, 512], BF16) for i in range(2)]
        WDN = [av(48 * KB + i * 8 * KB, [4, D_MODEL], BF16) for i in range(2)]
        AT = [av(64 * KB + i * 16 * KB, [4, SEQ], BF16) for i in range(2)]
        OHC = av(32 * KB, [L_C], F32)
        OHN = av(48 * KB, [L_N], F32)
        OHW = av(52 * KB, [L_W], F32)
        TB = av(57 * KB, [12 * 128], F32)
        RREP = av(64 * KB, [L_C], F32)

        PK = sb("PK", [128, NPK], F32)
        ONESM = sb("ONESM", [128, 128], BF16)
        ONES64 = sb("ONES64", [128, 64], BF16)
        ONE1 = sb("ONE1", [128, 128], BF16)
        IDENT = sb("IDENT", [128, 128], BF16)
        BIGI = sb("BIGI", [128, 128], BF16)
        NEGU = sb("NEGU", [128, 128], BF16)
        OHS = sb("OHS", [128, SEQ], BF16)
        MBT = sb("MBT", [128, SEQ], BF16)
        CBH = sb("CBH", [128, 12], F32)
        COVER = sb("COVER", [128, 33], BF16)
        RSTD = sb("RSTD", [128, 512], F32)
        EPSC = sb("EPSC", [128, 1], F32)
        ONEC = sb("ONEC", [128, 1], F32)
        TINYC = sb("TINYC", [128, 1], F32)
        LAMC = sb("LAMC", [64, 2], F32)
        LTMP = sb("LTMP", [1, 256], F32)
        SQ = [sb("SQ%d" % i, [128, 512], BF16) for i in range(2)]
        PT = [sb("PT%d" % i, [128, 512], BF16) for i in range(3)]
        FT = [sb("FT%d" % i, [128, 512], F32) for i in range(4)]
        SPB = [sb("SPB%d" % i, [128, 512], BF16) for i in range(2)]
        RL = FT[0:2]
        OUTB = FT[2:4]
        SELB = sb("SELB", [12, 12, 64], BF16)
        ONER = sb("ONER", [1, 64], F32)
        KM = sb("KM", [128, 2, 8], F32)
        KMB = sb("KMB", [128, 2, 8], BF16)
        GATE = sb("GATE", [128, 32], F32)
        TOP8 = sb("TOP8", [128, 8], F32)
        MB = sb("MB", [128, 32], BF16)
        MB8 = sb("MB8", [128, 8], BF16)
        GS_EXTRA = [(sb("MB8_%d" % i, [128, 8], BF16), sb("MB_%d" % i, [128, 32], BF16), sb("GATE_%d" % i, [128, 32], F32), sb("TOP8_%d" % i, [128, 8], F32)) for i in range(3)]
        IMP = sb("IMP", [128, 16, 32], F32)
        IMR = sb("IMR", [128, 1], F32)
        POSB = sb("POSB", [128, 32], BF16)
        B1 = sb("B1", [128, 4], F32)
        W2 = sb("W2", [128, 2, 2, 128], BF16)
        GEL = sb("GEL", [128, 4, 128], BF16)
        DUM = sb("DUM", [128, 8], F32)
        ZLH = sb("ZLH", [128, 128], BF16)
        KC = sb("KC", [128, 2, 128], BF16)
        VC = sb("VC", [128, 128], BF16)
        PS = [es.enter_context(nc.psum_tensor("PS%d" % i, [128, 512], F32)) for i in range(8)]

        P = Prog(nc)

        def A(eng, fn, reads, writes):
            return P.add(eng, fn, reads=reads, writes=writes)

        for c in range(8):
            P.dma('sp', XT[:, c, :], xT_d[c * 128:(c + 1) * 128, :])
        P.dma('sp', PK[:, :], pk_d[:, :])
        A('pool', lambda e: e.memset(ONESM[:, :], 1.0 / 1024.0), [], [ONESM[:, :]])
        A('pool', lambda e: e.memset(ONES64[:, :], 1.0 / 64.0), [], [ONES64[:, :]])
        A('pool', lambda e: e.memset(ONE1[:, :], 1.0), [], [ONE1[:, :]])
        A('pool', lambda e: e.memset(EPSC[:, :], EPS), [], [EPSC[:, :]])
        A('pool', lambda e: e.memset(ONEC[:, :], 1.0), [], [ONEC[:, :]])
        A('pool', lambda e: e.memset(ZLH[:, :], 0.0), [], [ZLH[:, :]])
        A('pool', lambda e: e.memset(TINYC[:, :], 1e-30), [], [TINYC[:, :]])
        A('pool', lambda e: e.affine_select(out=IDENT[:, :], in_=ONE1[:, :], pattern=[[1, 128]], compare_op=ALU.is_equal,
                                            fill=0.0, base=0, channel_multiplier=-1), [ONE1[:, :]], [IDENT[:, :]])
        A('pool', lambda e: e.tensor_scalar(out=BIGI[:, :], in0=IDENT[:, :], scalar1=BIGM, scalar2=None, op0=ALU.mult),
          [IDENT[:, :]], [BIGI[:, :]])
        A('pool', lambda e: e.memset(NEGU[:, :], -1.0), [], [NEGU[:, :]])
        A('pool', lambda e: e.affine_select(out=NEGU[:, :], in_=NEGU[:, :], pattern=[[-1, 128]], compare_op=ALU.is_gt,
                                            fill=0.0, base=0, channel_multiplier=1), [NEGU[:, :]], [NEGU[:, :]])
        OHTMP = av(80 * KB, [SEQ], BF16)
        for (T, w) in ((OHTMP[0:32, :], 64),):
            A('pool', lambda e, T=T: e.memset(T, 1.0), [], [T])
            A('pool', lambda e, T=T, w=w: e.affine_select(out=T, in_=T, pattern=[[1, SEQ]],
                                                          compare_op=ALU.is_ge, fill=0.0, base=0, channel_multiplier=-w),
              [T], [T])
            A('pool', lambda e, T=T, w=w: e.affine_select(out=T, in_=T, pattern=[[-1, SEQ]],
                                                          compare_op=ALU.is_ge, fill=0.0, base=w - 1, channel_multiplier=w),
              [T], [T])
        A('act', lambda e: e.activation(out=OHS[64:96, :], in_=OHTMP[0:32, :], func=AF.Copy), [OHTMP[0:32, :]], [OHS[64:96, :]])
        A('dve', lambda e: e.memset(OHS[96:128, :], 0.0), [], [OHS[96:128, :]])
        A('pool', lambda e: e.memset(VC[:, 64:128], 1.0), [], [VC[:, 64:128]])
        A('pool', lambda e: e.memset(SELB[:, :, :], 1.0), [], [SELB[:, :, :]])
        A('pool', lambda e: e.affine_select(out=SELB[:, :, :], in_=SELB[:, :, :], pattern=[[-1, 12], [0, 64]],
                                            compare_op=ALU.is_equal, fill=0.0, base=0, channel_multiplier=1),
          [SELB[:, :, :]], [SELB[:, :, :]])
        P.dma('pool', COVER[:, :], cover_d[:, :])
        A('pool', lambda e: e.memset(ONER[:, :], 1.0), [], [ONER[:, :]])

        P.dma('sp', OHN[0:33, :], ohn_d[:, :], semkey='dma_ohn')
        P.dma('sp', OHW[0:33, :], ohw_d[:, :], semkey='dma_ohw')
        P.dma('sp', OHC[0:33, :], ohc_d[:, :], semkey='dma_ohc')
        A('pool', lambda e: e.memset(TB[32:33, :], NEGV), [], [TB[32:33, :]])
        for h in range(12):
            A('dve', lambda e, h=h: e.tensor_copy(out=TB[0:32, h * 128:(h + 1) * 128],
                                                  in_=PK[0:32, PK_TAB + h:PK_TAB + h + 1].to_broadcast([32, 128])),
              [PK[0:32, PK_TAB + h:PK_TAB + h + 1]], [TB[0:32, h * 128:(h + 1) * 128]])
        psr = [0]

        def nps():
            p = PS[psr[0] % 8]
            psr[0] += 1
            return p

        def gen_bias(h, OH, L, scr, sk, inv_scale, want_cb):
            for j in range(0, L, 512):
                n = min(512, L - j)
                ps = nps()
                A('pe', lambda e, ps=ps, j=j, n=n: e.matmul(ps[:, 0:n], lhsT=TB[0:33, h * 128:(h + 1) * 128],
                                                            rhs=OH[0:33, j:j + n], start=True, stop=True),
                  [TB[0:33, h * 128:(h + 1) * 128], OH[0:33, j:j + n]], [ps[:, 0:n]])
                A('act', lambda e, ps=ps, j=j, n=n: e.activation(out=RREP[:, j:j + n], in_=ps[:, 0:n], func=AF.Copy,
                                                                 scale=inv_scale),
                  [ps[:, 0:n]], [RREP[:, j:j + n]])
                if want_cb and j == 0:
                    A('dve', lambda e, ps=ps: e.tensor_copy(out=CBH[:, h:h + 1], in_=ps[:, 400:401]),
                      [ps[:, 400:401]], [CBH[:, h:h + 1]])
            dst = bass.AP(scr.tensor, 0, [[L + sk, 128], [1, L]])
            P.add('sp', lambda e, dst=dst: e.dma_start(out=dst, in_=RREP[:, 0:L]), reads=[RREP[:, 0:L]], writes=[scr[:]],
                  dma=True, semkey='dma_' + scr.tensor.name)

        for h in range(12):
            gen_bias(h, OHN, L_N, scr_n[h], 1, (1.0 if h < 8 else math.sqrt(32.0)), True)
        for h in range(4):
            gen_bias(4 + h, OHW, L_W, scr_w[h], 1, 1.0, False)
            gen_bias(4 + h, OHC, L_C, scr_c[h], 16, 1.0, False)

        def load_g(kind, h, dst, slot=0):
            if kind == 'n':
                src = bass.AP(scr_n[h].tensor, OFF_N, [[L_N, 128], [1, 640]])
                full = scr_n[h]
            elif kind == 'w':
                src = bass.AP(scr_w[h].tensor, OFF_W, [[L_W, 128], [1, 1024]])
                full = scr_w[h]
            else:
                src = bass.AP(scr_c[h].tensor, 2032, [[L_C, 128], [1, 2048]])
                full = scr_c[h]
            P.add('pool', lambda e: e.dma_start(out=dst, in_=src), reads=[full[:]], writes=[dst], dma=True,
                  semkey='dma_greg%d' % slot)

        sqi = [0]

        def rmsnorm_tile(tt, gcol0, dst_fn):
            ts = slice(tt * 512, (tt + 1) * 512)
            ps = nps()
            for c in range(8):
                sq = SQ[sqi[0] % 2]
                sqi[0] += 1
                A('act', lambda e, sq=sq, c=c: e.activation(out=sq[:, :], in_=XT[:, c, ts], func=AF.Square),
                  [XT[:, c, ts]], [sq[:, :]])
                A('pe', lambda e, sq=sq, c=c: e.matmul(ps[:, :], lhsT=ONESM[:, :], rhs=sq[:, :], start=(c == 0), stop=(c == 7)),
                  [ONESM[:, :], sq[:, :]], [ps[:, :]])
            A('act', lambda e: e.activation(out=RSTD[:, :], in_=ps[:, :], func=AF.Sqrt, bias=EPSC[:, :]),
              [ps[:, :], EPSC[:, :]], [RSTD[:, :]])
            A('dve', lambda e: e.reciprocal(out=RSTD[:, :], in_=RSTD[:, :]), [RSTD[:, :]], [RSTD[:, :]])
            for c in range(8):
                dst, post = dst_fn(c)
                A('dve', lambda e, dst=dst, c=c: e.scalar_tensor_tensor(
                    out=dst, in0=XT[:, c, ts], scalar=PK[:, gcol0 + c:gcol0 + c + 1], in1=RSTD[:, :],
                    op0=ALU.mult, op1=ALU.mult),
                  [XT[:, c, ts], PK[:, gcol0 + c:gcol0 + c + 1], RSTD[:, :]], [dst])
                if post:
                    post()

        def proj_fm(col0, ncols, dst_fn, scale=1.0):
            for tt in range(4):
                ts = slice(tt * 512, (tt + 1) * 512)
                ps = nps()
                for k in range(8):
                    A('pe', lambda e, ps=ps, k=k, ts=ts: e.matmul(ps[0:ncols, :], lhsT=WIN[:, k, col0:col0 + ncols],
                                                                  rhs=HT[:, k, ts], start=(k == 0), stop=(k == 7)),
                      [WIN[:, k, col0:col0 + ncols], HT[:, k, ts]], [ps[0:ncols, :]])
                dst = dst_fn(tt)
                A('act', lambda e, ps=ps, dst=dst: e.activation(out=dst, in_=ps[0:ncols, :], func=AF.Copy, scale=scale),
                  [ps[0:ncols, :]], [dst])

        def proj_tm(col0, ncols, h0):
            nh = ncols // 64
            for tb in range(16):
                ps = nps()
                for k in range(8):
                    A('pe', lambda e, ps=ps, k=k, tb=tb: e.matmul(ps[:, 0:ncols], lhsT=HT[:, k, tb * 128:(tb + 1) * 128],
                                                                  rhs=WIN[:, k, col0:col0 + ncols], start=(k == 0), stop=(k == 7)),
                      [HT[:, k, tb * 128:(tb + 1) * 128], WIN[:, k, col0:col0 + ncols]], [ps[:, 0:ncols]])
                A('dve', lambda e, ps=ps, tb=tb: e.tensor_copy(out=VT[:, tb, h0:h0 + nh, 0:64],
                                                               in_=ps[:, 0:ncols].rearrange("p (h d) -> p h d", h=nh)),
                  [ps[:, 0:ncols]], [VT[:, tb, h0:h0 + nh, 0:64]])

        KZ = [KT[:, 0, :], KT[:, 1, :], KX[:, 0, :], KX[:, 1, :]]

        def proj_k_padded(colbase):
            for hh in range(4):
                ob = 64 - (hh % 2) * 64
                A('dve', lambda e, hh=hh, ob=ob: e.memset(KZ[hh][ob:ob + 64, :], 0.0), [], [KZ[hh][ob:ob + 64, :]])
            for c in range(2):
                for tt in range(4):
                    ts = slice(tt * 512, (tt + 1) * 512)
                    ps = nps()
                    for k in range(8):
                        A('pe', lambda e, ps=ps, k=k, ts=ts, c=c: e.matmul(ps[:, :], lhsT=WIN[:, k, colbase + c * 128:colbase + (c + 1) * 128],
                                                                            rhs=HT[:, k, ts], start=(k == 0), stop=(k == 7)),
                          [WIN[:, k, colbase + c * 128:colbase + (c + 1) * 128], HT[:, k, ts]], [ps[:, :]])
                    A('act', lambda e, ps=ps, c=c, ts=ts: e.activation(out=KZ[2 * c][0:64, ts], in_=ps[0:64, :], func=AF.Copy),
                      [ps[0:64, :]], [KZ[2 * c][0:64, ts]])
                    A('dve', lambda e, ps=ps, c=c, ts=ts: e.tensor_copy(out=KZ[2 * c + 1][64:128, ts], in_=ps[64:128, :]),
                      [ps[64:128, :]], [KZ[2 * c + 1][64:128, ts]])

        def vones(tb, c0):
            return VT[:, tb, c0 // 64, :]

        class Pipe:
            def __init__(self):
                self.e1 = None
                self.p0 = None
                self.eps = []

            def defer(self, fn, n=3):
                self.eps.append([n, fn])

            def _tick(self):
                for it in self.eps:
                    it[0] -= 1
                while self.eps and self.eps[0][0] <= 0:
                    self.eps.pop(0)[1]()

            def push(self, s, e, pv):
                self._tick()
                s()
                if self.e1:
                    self.e1[0]()
                if self.p0:
                    self.p0()
                self.p0 = self.e1[1] if self.e1 else None
                self.e1 = (e, pv)

            def flush(self):
                if self.e1:
                    self.e1[0]()
                if self.p0:
                    self.p0()
                if self.e1:
                    self.e1[1]()
                self.e1 = None
                self.p0 = None
                while self.eps:
                    self.eps.pop(0)[1]()

        zi = [0]
        pti = [0]
        FILL = [0]
        EPDEF = [3]
        ZB = [PS[0], PS[1], PS[2]]
        OB = [PS[3], PS[4], PS[5]]
        misc = [0]

        MISC = [[PS[6], PS[7]]]

        def mps():
            lst = MISC[0]
            p = lst[misc[0] % len(lst)]
            misc[0] += 1
            return p

        def recip_act(dst, den):
            A('act', lambda e: e.activation(out=dst, in_=den, func=AF.Ln, bias=TINYC[64:128, :]), [den, TINYC[64:128, :]], [dst])
            A('act', lambda e: e.activation(out=dst, in_=dst, func=AF.Exp, scale=-1.0), [dst], [dst])

        def sm_tile(pipe, terms, q0, N, act_scale, cbias, v_lhsT, o_ps, first, last, nk=128, extra_pv=None, epilogue=None):
            z = ZB[zi[0] % 3]
            zi[0] += 1
            pt = PT[pti[0] % 3]
            pti[0] += 1
            zs = z[0:nk, q0:q0 + N]
            pts = pt[0:nk, q0:q0 + N]

            def s():
                nf = FILL[0]
                if nf:
                    A('pe', lambda e: e.matmul(z[:, 0:nf], lhsT=ONE1[:, :], rhs=HT[:, 0, 0:nf], start=True, stop=True),
                      [ONE1[:, :], HT[:, 0, 0:nf]], [z[:, 0:nf]])
                for i, (a, b) in enumerate(terms):
                    A('pe', lambda e, a=a, b=b, i=i: e.matmul(zs, lhsT=a, rhs=b, start=(i == 0), stop=(i == len(terms) - 1)),
                      [a, b], [zs])

            def ex():
                if cbias is None:
                    A('act', lambda e: e.activation(out=pts, in_=zs, func=AF.Exp, scale=act_scale), [zs], [pts])
                else:
                    A('act', lambda e: e.activation(out=pts, in_=zs, func=AF.Exp, scale=act_scale, bias=cbias),
                      [zs, cbias], [pts])

            def pv():
                M = v_lhsT.shape[-1] if len(v_lhsT.shape) == 2 else 128
                osl = o_ps[0:M, q0:q0 + N]
                A('pe', lambda e: e.matmul(osl, lhsT=v_lhsT, rhs=pts, start=first, stop=last), [v_lhsT, pts], [osl])
                if extra_pv:
                    extra_pv(pt)
                if epilogue:
                    pipe.defer(epilogue, EPDEF[0])
            pipe.push(s, ex, pv)

        def causal_blocks(qt):
            out = []
            for kb in range(4 * qt + 4):
                i = kb - 4 * qt
                if i <= 0:
                    out.append((kb, 0, 512, (4 * qt - kb) * 128))
                else:
                    out.append((kb, 128 * i, 512 - 128 * i, 0))
            return out

        def bias_terms(G, bh, q0, N, D):
            if D >= 256:
                return [], CBH[:, bh:bh + 1]
            return [(IDENT[:, :], G[:, D:D + N])], None

        def write_mix(dst, src, scale=1.0):
            A('act', lambda e: e.activation(out=dst, in_=src, func=AF.Copy, scale=scale), [src], [dst])

        def apply_wout(l, mixer):
            for c in range(8):
                for tt in range(4):
                    ts = slice(tt * 512, (tt + 1) * 512)
                    ps = nps()
                    for m in range(2):
                        A('pe', lambda e, ps=ps, m=m, c=c, ts=ts: e.matmul(ps[:, :], lhsT=WO[:, m, c * 128:(c + 1) * 128],
                                                                            rhs=MIXM[:, m, ts], start=(m == 0), stop=(m == 1)),
                          [WO[:, m, c * 128:(c + 1) * 128], MIXM[:, m, ts]], [ps[:, :]])
                    A('dve', lambda e, ps=ps, c=c, ts=ts: e.tensor_tensor(out=XT[:, c, ts], in0=XT[:, c, ts], in1=ps[:, :], op=ALU.add),
                      [XT[:, c, ts], ps[:, :]], [XT[:, c, ts]])

        def mix_dst(h, ts):
            return MIXM[(h % 2) * 64:(h % 2) * 64 + 64, h // 2, ts]

        def mixer_diff(l):
            lam_init = 0.8 - 0.6 * math.exp(-0.3 * l)
            lv = PK[0:1, PK_LAM + l * 128:PK_LAM + (l + 1) * 128]
            A('dve', lambda e: e.tensor_tensor(out=LTMP[:, 0:32], in0=PK[0:1, PK_LAM + l * 128:PK_LAM + l * 128 + 32],
                                               in1=PK[0:1, PK_LAM + l * 128 + 32:PK_LAM + l * 128 + 64], op=ALU.mult),
              [lv], [LTMP[:, 0:32]])
            A('dve', lambda e: e.tensor_tensor(out=LTMP[:, 32:64], in0=PK[0:1, PK_LAM + l * 128 + 64:PK_LAM + l * 128 + 96],
                                               in1=PK[0:1, PK_LAM + l * 128 + 96:PK_LAM + l * 128 + 128], op=ALU.mult),
              [lv], [LTMP[:, 32:64]])
            A('dve', lambda e: e.reduce_sum(out=LTMP[:, 64:66], in_=LTMP[:, 0:64].rearrange("p (a b) -> p a b", a=2), axis=AX.X),
              [LTMP[:, 0:64]], [LTMP[:, 64:66]])
            A('act', lambda e: e.activation(out=LTMP[:, 66:68], in_=LTMP[:, 64:66], func=AF.Exp), [LTMP[:, 64:66]], [LTMP[:, 66:68]])
            A('dve', lambda e: e.scalar_tensor_tensor(out=LTMP[:, 68:69], in0=LTMP[:, 67:68], scalar=-lam_init, in1=LTMP[:, 66:67],
                                                      op0=ALU.add, op1=ALU.subtract),
              [LTMP[:, 66:68]], [LTMP[:, 68:69]])
            ps = mps()
            A('pe', lambda e: e.matmul(ps[0:64, 0:1], lhsT=ONER[0:1, 0:64], rhs=LTMP[0:1, 68:69], start=True, stop=True),
              [ONER[0:1, 0:64], LTMP[0:1, 68:69]], [ps[0:64, 0:1]])
            A('dve', lambda e: e.tensor_copy(out=LAMC[:, l:l + 1], in_=ps[0:64, 0:1]), [ps[0:64, 0:1]], [LAMC[:, l:l + 1]])
            if stop <= 1:
                return
            if stop <= 2:
                return
            A('dve', lambda e: e.memset(KT[:, :, :], 0.0), [], [KT[:, :, :]])
            for c in range(2):
                for tt in range(4):
                    ts = slice(tt * 512, (tt + 1) * 512)
                    ps = nps()
                    for k in range(8):
                        A('pe', lambda e, ps=ps, k=k, ts=ts, c=c: e.matmul(ps[:, :], lhsT=WIN[:, k, 256 + c * 128:256 + (c + 1) * 128],
                                                                            rhs=HT[:, k, ts], start=(k == 0), stop=(k == 7)),
                          [WIN[:, k, 256 + c * 128:256 + (c + 1) * 128], HT[:, k, ts]], [ps[:, :]])
                    for b0 in (0, 64):
                        A('act', lambda e, ps=ps, b0=b0, c=c, ts=ts: e.activation(out=KT[b0:b0 + 32, c, ts], in_=ps[b0:b0 + 32, :], func=AF.Copy),
                          [ps[b0:b0 + 32, :]], [KT[b0:b0 + 32, c, ts]])
                    A('dve', lambda e, ps=ps, c=c, ts=ts: e.tensor_copy(out=KX[:, c, ts], in_=ps[:, :]), [ps[:, :]], [KX[:, c, ts]])
                    for b0 in (0, 64):
                        A('dve', lambda e, b0=b0, c=c, ts=ts: e.memset(KX[b0:b0 + 32, c, ts], 0.0), [], [KX[b0:b0 + 32, c, ts]])
            if stop <= 3:
                return
            proj_tm(512, 256, 0)
            QW = av(72 * KB, [2, SEQ], BF16)
            QZ = [QT[:, 0, :], QT[:, 1, :], QW[:, 0, :], QW[:, 1, :]]
            for c in range(2):
                psl = []
                for tt in range(4):
                    ts = slice(tt * 512, (tt + 1) * 512)
                    ps = nps()
                    psl.append(ps)
                    for k in range(8):
                        A('pe', lambda e, ps=ps, k=k, ts=ts, c=c: e.matmul(ps[:, :], lhsT=WIN[:, k, c * 128:(c + 1) * 128], rhs=HT[:, k, ts],
                                                                            start=(k == 0), stop=(k == 7)),
                          [WIN[:, k, c * 128:(c + 1) * 128], HT[:, k, ts]], [ps[:, :]])
                for tt in range(4):
                    ts = slice(tt * 512, (tt + 1) * 512)
                    ps = psl[tt]
                    A('act', lambda e, ps=ps, ts=ts, c=c: e.activation(out=QZ[2 * c][0:64, ts], in_=ps[0:64, :], func=AF.Copy),
                      [ps[0:64, :]], [QZ[2 * c][0:64, ts]])
                    A('dve', lambda e, ps=ps, ts=ts, c=c: e.tensor_copy(out=QZ[2 * c + 1][64:128, ts], in_=ps[64:128, :]),
                      [ps[64:128, :]], [QZ[2 * c + 1][64:128, ts]])
            for hh in range(4):
                ob = 64 - (hh % 2) * 64
                A('dve', lambda e, hh=hh, ob=ob: e.memset(QZ[hh][ob:ob + 64, :], 0.0), [], [QZ[hh][ob:ob + 64, :]])
            after_proj()
            if stop <= 4:
                return
            G = [av(GREG + i * 1280, [640], BF16) for i in range(4)]
            for h in range(4):
                load_g('n', 8 + h, G[h][:, :], slot=h)
            if stop <= 5:
                return
            P.phase = P.phase.split('/')[0] + '/attn'
            MISC[0] = [PS[7]]
            EPDEF[0] = 1
            FILL[0] = 512 if filler else 0
            pipe = Pipe()
            sc = 1.0 / math.sqrt(32.0)
            for h in range(4):
                b0 = (h % 2) * 64
                for qt in range(4):
                    t0 = qt * 512
                    blocks = causal_blocks(qt)
                    gi = h * 4 + qt
                    O1, O2 = (PS[3], PS[4]) if gi % 2 == 0 else (PS[5], PS[6])
                    for half, (Kt, O) in enumerate(((KT, O1), (KX, O2))):
                        for bi, (kb, q0, N, D) in enumerate(blocks):
                            kT = Kt[:, h // 2, kb * 128:(kb + 1) * 128]
                            qT = QZ[h][:, t0 + q0:t0 + q0 + N]
                            ext, cb = bias_terms(G[h], 8 + h, q0, N, D)
                            last = (bi == len(blocks) - 1)
                            ep = None
                            if last and half == 1:
                                def ep(h=h, qt=qt, O1=O1, O2=O2):
                                    ts = slice(qt * 512, (qt + 1) * 512)
                                    r1, r2, o1, o2 = FT[0], FT[1], FT[2], FT[3]
                                    A('dve', lambda e: e.reciprocal(out=r1[0:64, :], in_=O1[64:128, :]), [O1[64:128, :]], [r1[0:64, :]])
                                    A('dve', lambda e: e.tensor_tensor(out=o1[0:64, :], in0=O1[0:64, :], in1=r1[0:64, :], op=ALU.mult),
                                      [O1[0:64, :], r1[0:64, :]], [o1[0:64, :]])
                                    recip_act(r2[0:64, :], O2[64:128, :])
                                    A('dve', lambda e: e.tensor_tensor(out=o2[0:64, :], in0=O2[0:64, :], in1=r2[0:64, :], op=ALU.mult),
                                      [O2[0:64, :], r2[0:64, :]], [o2[0:64, :]])
                                    A('dve', lambda e: e.scalar_tensor_tensor(out=o1[0:64, :], in0=o2[0:64, :], scalar=LAMC[:, l:l + 1],
                                                                              in1=o1[0:64, :], op0=ALU.mult, op1=ALU.add),
                                      [o2[0:64, :], LAMC[:, l:l + 1], o1[0:64, :]], [o1[0:64, :]])
                                    sq = SQ[sqi[0] % 2]
                                    sqi[0] += 1
                                    A('pool', lambda e: e.tensor_tensor(out=sq[0:64, :], in0=o1[0:64, :], in1=o1[0:64, :], op=ALU.mult), [o1[0:64, :]], [sq[0:64, :]])

                                    def ep2():
                                        ps = mps()
                                        A('pe', lambda e: e.matmul(ps[0:64, :], lhsT=ONES64[0:64, :], rhs=sq[0:64, :], start=True, stop=True),
                                          [ONES64[0:64, :], sq[0:64, :]], [ps[0:64, :]])
                                        A('act', lambda e: e.activation(out=r1[0:64, :], in_=ps[0:64, :], func=AF.Ln, bias=EPSC[0:64, :]),
                                          [ps[0:64, :], EPSC[0:64, :]], [r1[0:64, :]])
                                        A('act', lambda e: e.activation(out=r1[0:64, :], in_=r1[0:64, :], func=AF.Exp, scale=-0.5), [r1[0:64, :]], [r1[0:64, :]])
                                        A('dve', lambda e: e.scalar_tensor_tensor(out=o2[0:64, :], in0=o1[0:64, :], scalar=PK[0:64, PK_SUBLN + l:PK_SUBLN + l + 1],
                                                                                  in1=r1[0:64, :], op0=ALU.mult, op1=ALU.mult),
                                          [o1[0:64, :], PK[0:64, PK_SUBLN + l:PK_SUBLN + l + 1], r1[0:64, :]], [o2[0:64, :]])
                                        write_mix(mix_dst(h, ts), o2[0:64, :], scale=(1.0 - lam_init))
                                    pipe.defer(ep2, 9)
                            sm_tile(pipe, [(kT, qT)] + ext, q0, N, sc, cb, vones(kb, h * 64), O, bi == 0, last, epilogue=ep)
            pipe.flush()
            FILL[0] = 0
            MISC[0] = [PS[6], PS[7]]

        def mixer_moba(l):
            for c in range(2):
                proj_fm(c * 128, 128, lambda tt, c=c: QT[:, c, tt * 512:(tt + 1) * 512], scale=0.125)
            proj_k_padded(256)
            A('dve', lambda e: e.memset(MBT[96:128, :], 0.0), [], [MBT[96:128, :]])
            for hh in range(4):
                bb = (hh % 2) * 64
                A('dve', lambda e, hh=hh, bb=bb: e.reduce_sum(out=KM[bb:bb + 64, hh // 2, :],
                                                               in_=KZ[hh][bb:bb + 64, :].rearrange("p (b s) -> p b s", b=8), axis=AX.X),
                  [KZ[hh][bb:bb + 64, :]], [KM[bb:bb + 64, hh // 2, :]])
            for c in range(2):
                A('act', lambda e, c=c: e.activation(out=KMB[:, c, :], in_=KM[:, c, :], func=AF.Copy, scale=1.0 / 256.0),
                  [KM[:, c, :]], [KMB[:, c, :]])
            G = [av(GREG + i * 1280, [640], BF16) for i in range(4)]
            for h in range(4):
                load_g('n', h, G[h][:, :], slot=h)
            P.phase = P.phase.split('/')[0] + '/attn'
            EPDEF[0] = 3
            pipe = Pipe()
            gsets = [(MB8, MB, GATE, TOP8)] + GS_EXTRA
            gcnt = [0]

            def gating(h, tb):
                b0 = (h % 2) * 64
                own = tb // 2
                mb8, mb, gate, top8 = gsets[gcnt[0] % 4]
                gcnt[0] += 1
                tsl = slice(tb * 128, (tb + 1) * 128)
                msl = slice((h % 2) * 1024 + tb * 128 - 1024, (h % 2) * 1024 + (tb + 1) * 128 - 1024)
                A('pool', lambda e: e.memset(mb8[:, 0:8], -1.0), [], [mb8[:, 0:8]])
                A('pool', lambda e: e.memset(mb8[:, own:own + 1], 0.0), [], [mb8[:, own:own + 1]])
                ps = mps()
                A('pe', lambda e: e.matmul(ps[:, 0:8], lhsT=QT[b0:b0 + 64, h // 2, tsl], rhs=KMB[b0:b0 + 64, h // 2, :], start=True, stop=True),
                  [QT[b0:b0 + 64, h // 2, tsl], KMB[b0:b0 + 64, h // 2, :]], [ps[:, 0:8]])
                A('pool', lambda e: e.memset(gate[:, 0:8], -1e30), [], [gate[:, 0:8]])
                A('dve', lambda e: e.tensor_copy(out=gate[:, 0:own], in_=ps[:, 0:own]), [ps[:, 0:own]], [gate[:, 0:own]])
                A('dve', lambda e: e.max(out=top8[:, :], in_=gate[:, 0:8]), [gate[:, 0:8]], [top8[:, :]])
                A('dve', lambda e: e.tensor_scalar(out=mb8[:, 0:own], in0=gate[:, 0:own], scalar1=top8[:, 2:3], scalar2=-1.0,
                                                   op0=ALU.is_ge, op1=ALU.add),
                  [gate[:, 0:own], top8[:, 2:3]], [mb8[:, 0:own]])
                for r4 in range(4):
                    A('pool', lambda e, r4=r4: e.tensor_copy(out=mb[:, r4:32:4], in_=mb8[:, 0:8]), [mb8[:, 0:8]], [mb[:, r4:32:4]])

                def part2():
                    ps2 = mps()
                    A('pe', lambda e: e.matmul(ps2[0:32, 0:128], lhsT=mb[:, 0:32], rhs=BIGI[:, :], start=True, stop=True),
                      [mb[:, 0:32], BIGI[:, :]], [ps2[0:32, 0:128]])
                    A('dve', lambda e: e.tensor_copy(out=MBT[64:96, msl], in_=ps2[0:32, 0:128]), [ps2[0:32, 0:128]], [MBT[64:96, msl]])
                return part2

            def qcopy(hh):
                bb = (hh % 2) * 64
                dst = MBT[0:64, (hh % 2) * 1024:(hh % 2 + 1) * 1024]
                src = QT[bb:bb + 64, hh // 2, 1024:2048]
                A('dve', lambda e: e.tensor_copy(out=dst, in_=src), [src], [dst])

            def kcopy(hh):
                bb = (hh % 2) * 64
                src = KZ[hh][bb:bb + 64, :]
                A('dve', lambda e: e.tensor_copy(out=OHS[0:64, :], in_=src), [src], [OHS[0:64, :]])
            ogc = [0]
            qcopy(0)
            kcopy(0)
            pending = [(0, tb) for tb in range(8, 16)]
            p2s = []
            for (hh, tb) in pending:
                p2s.append(gating(hh, tb))
                if len(p2s) > 2:
                    p2s.pop(0)()
            while p2s:
                p2s.pop(0)()
            proj_tm(512, 256, 0)
            after_proj()
            for h in range(4):
                b0 = (h % 2) * 64
                pending = [(h + 1, tb) for tb in range(8, 16)] if h < 3 else []
                tcount = 0
                if h < 3:
                    qcopy(h + 1)
                for qt in (2, 3, 0, 1):
                    if qt == 0 and h < 3:
                        kcopy(h + 1)
                    t0 = qt * 512
                    blocks = causal_blocks(qt)
                    O = OB[ogc[0] % 3]
                    ogc[0] += 1
                    for bi, (kb, q0, N, D) in enumerate(blocks):
                        kT = KZ[h][:, kb * 128:(kb + 1) * 128]
                        qT = QT[:, h // 2, t0 + q0:t0 + q0 + N]
                        ext, cb = bias_terms(G[h], h, q0, N, D)
                        if qt >= 2:
                            m0 = (h % 2) * 1024 + t0 + q0 - 1024
                            kT = OHS[:, kb * 128:(kb + 1) * 128]
                            qT = MBT[:, m0:m0 + N]
                        last = (bi == len(blocks) - 1)
                        ep = None
                        if last:
                            def ep(h=h, qt=qt, O=O):
                                ts = slice(qt * 512, (qt + 1) * 512)
                                r1, o1 = FT[0], FT[2]
                                A('dve', lambda e: e.reciprocal(out=r1[0:64, :], in_=O[64:128, :]), [O[64:128, :]], [r1[0:64, :]])
                                A('dve', lambda e: e.tensor_tensor(out=o1[0:64, :], in0=O[0:64, :], in1=r1[0:64, :], op=ALU.mult),
                                  [O[0:64, :], r1[0:64, :]], [o1[0:64, :]])
                                write_mix(mix_dst(h, ts), o1[0:64, :])
                        sm_tile(pipe, [(kT, qT)] + ext, q0, N, 1.0, cb, vones(kb, h * 64), O, bi == 0, last, epilogue=ep)
                        tcount += 1
                        if tcount % 2 == 1 and (len(p2s) > 2 or (p2s and not pending)):
                            p2s.pop(0)()
                        if pending and tcount % 2 == 0:
                            hh, tb = pending.pop(0)
                            p2s.append(gating(hh, tb))
                while pending or p2s:
                    if pending:
                        hh, tb = pending.pop(0)
                        p2s.append(gating(hh, tb))
                    if p2s:
                        p2s.pop(0)()
            pipe.flush()
            FILL[0] = 0

        def mixer_sb(l):
            for c in range(2):
                proj_fm(c * 128, 128, lambda tt, c=c: QT[:, c, tt * 512:(tt + 1) * 512], scale=0.125)
            proj_k_padded(256)
            proj_tm(512, 256, 0)
            after_proj()
            P.phase = P.phase.split('/')[0] + '/attn'
            ZA = [PS[0], PS[1], PS[7]]
            WB = [PS[2], PS[3]]
            CBK = PS[4]
            OBK = [PS[5], PS[6]]
            SPB3 = [SPB[0], SPB[1], SQ[0], SQ[1]]
            FTS = [(FT[0], FT[1]), (FT[2], FT[3]), (MBT[:, 0:1024].bitcast(F32), MBT[:, 1024:2048].bitcast(F32))]
            blks = []
            for h in range(4):
                for qt in range(4):
                    blocks = list(reversed(causal_blocks(qt)))
                    for bi, (kb, q0, N, D) in enumerate(blocks):
                        blks.append((h, qt, bi, len(blocks), kb, q0, N))
            stA, stB1, stC1, stB2, stC2 = [], [], [], [], []
            for i, (h, qt, bi, nb, kb, q0, N) in enumerate(blks):
                b0 = (h % 2) * 64
                t0 = qt * 512
                O = OBK[(h * 4 + qt) % 2]
                diag = (kb >= 4 * qt)
                first = (bi == 0)
                last = (bi == nb - 1)
                za = ZA[i % 3][:, q0:q0 + N]
                wb = WB[i % 2][:, q0:q0 + N]
                cb = CBK[:, q0:q0 + N]
                t1t, t2t = FTS[i % 3]
                t1 = t1t[:, q0:q0 + N]
                t2 = t2t[:, q0:q0 + N]
                arg = t1
                spb_t = SPB3[i % 4]
                spb = spb_t[:, q0:q0 + N]
                pt_t = PT[i % 3]
                pts = pt_t[:, q0:q0 + N]
                kT = KZ[h][:, kb * 128:(kb + 1) * 128]
                qT = QT[:, h // 2, t0 + q0:t0 + q0 + N]

                def fA(za=za, kT=kT, qT=qT, t1=t1, t2=t2, diag=diag, spb_t=spb_t, t2t=t2t, q0=q0, N=N, spb=spb):
                    A('pe', lambda e: e.matmul(za, lhsT=kT, rhs=qT, start=True, stop=True), [kT, qT], [za])
                    A('act', lambda e: e.activation(out=t1, in_=za, func=AF.Exp), [za], [t1])
                    A('act', lambda e: e.activation(out=t2, in_=t1, func=AF.Ln, bias=ONEC[:, :]), [t1, ONEC[:, :]], [t2])
                    if diag:
                        d0 = spb_t[:, q0:q0 + 128]
                        s0 = t2t[:, q0:q0 + 128]
                        A('pool', lambda e: e.affine_select(out=d0, in_=s0, pattern=[[1, 128]], compare_op=ALU.is_gt,
                                                            fill=0.0, base=0, channel_multiplier=-1), [s0], [d0])
                        if N > 128:
                            d1 = spb_t[:, q0 + 128:q0 + N]
                            s1 = t2t[:, q0 + 128:q0 + N]
                            A('pool', lambda e: e.tensor_copy(out=d1, in_=s1), [s1], [d1])
                    else:
                        A('pool', lambda e: e.tensor_copy(out=spb, in_=t2), [t2], [spb])

                def fB1(wb=wb, za=za, spb=spb, t2=t2, arg=arg):
                    A('pe', lambda e: e.matmul(wb, lhsT=NEGU[:, :], rhs=spb, start=True, stop=True), [NEGU[:, :], spb], [wb])
                    A('dve', lambda e: e.tensor_tensor(out=arg, in0=za, in1=t2, op=ALU.subtract), [za, t2], [arg])
                    A('dve', lambda e: e.tensor_tensor(out=arg, in0=arg, in1=wb, op=ALU.add), [arg, wb], [arg])

                def fC1(cb=cb, spb=spb, first=first, last=last):
                    if first:
                        A('pe', lambda e: e.matmul(CBK[:, :], lhsT=ZLH[:, :], rhs=HT[:, 0, 0:512], start=True, stop=True, skip_group_check=True),
                          [ZLH[:, :], HT[:, 0, 0:512]], [CBK[:, :]])
                    if not last:
                        A('pe', lambda e: e.matmul(cb, lhsT=ONE1[:, :], rhs=spb, start=False, stop=True, skip_group_check=True), [ONE1[:, :], spb], [cb])

                def fB2(diag=diag, q0=q0, first=first, t1t=t1t, arg=arg, pts=pts, pt_t=pt_t):
                    c0 = q0 + 128 if diag else q0
                    if not first and c0 < 512:
                        cbs = CBK[:, c0:512]
                        args = t1t[:, c0:512]
                        A('dve', lambda e: e.tensor_tensor(out=args, in0=args, in1=cbs, op=ALU.subtract), [args, cbs], [args])
                    A('act', lambda e: e.activation(out=pts, in_=arg, func=AF.Exp), [arg], [pts])
                    if diag:
                        a0 = pt_t[:, q0:q0 + 128]
                        A('pool', lambda e: e.affine_select(out=a0, in_=a0, pattern=[[1, 128]], compare_op=ALU.is_gt,
                                                            fill=0.0, base=0, channel_multiplier=-1), [a0], [a0])

                def fC2(O=O, q0=q0, N=N, kb=kb, h=h, qt=qt, pts=pts, first=first, last=last):
                    vl = VT[:, kb, h, :]
                    osl = O[:, q0:q0 + N]
                    if first:
                        A('pe', lambda e: e.matmul(O[:, :], lhsT=ZLH[:, :], rhs=HT[:, 0, 0:512], start=True, stop=True, skip_group_check=True),
                          [ZLH[:, :], HT[:, 0, 0:512]], [O[:, :]])
                    A('pe', lambda e: e.matmul(osl, lhsT=vl, rhs=pts, start=False, stop=True, skip_group_check=True), [vl, pts], [osl])
                    if last:
                        ts = slice(qt * 512, (qt + 1) * 512)
                        write_mix(mix_dst(h, ts), O[0:64, :])
                stA.append(fA)
                stB1.append(fB1)
                stC1.append(fC1)
                stB2.append(fB2)
                stC2.append(fC2)
            nblk = len(blks)
            for step in range(-3, nblk):
                if 0 <= step + 3 < nblk:
                    stA[step + 3]()
                if 0 <= step + 1 < nblk:
                    stB1[step + 1]()
                if 0 <= step < nblk:
                    stC1[step]()
                if 0 <= step + 1 < nblk:
                    stB2[step + 1]()
                if 0 <= step < nblk:
                    stC2[step]()

        def mixer_nsa(l):
            for c in range(2):
                proj_fm(c * 128, 128, lambda tt, c=c: QT[:, c, tt * 512:(tt + 1) * 512], scale=0.125)
            proj_fm(256, 128, lambda tt: KX[:, 0, tt * 512:(tt + 1) * 512])
            proj_fm(384, 64, lambda tt: OHS[0:64, tt * 512:(tt + 1) * 512])
            A('dve', lambda e: e.memset(KT[64:128, 0, :], 0.0), [], [KT[64:128, 0, :]])
            A('dve', lambda e: e.memset(KX[0:64, 1, :], 0.0), [], [KX[0:64, 1, :]])
            A('dve', lambda e: e.memset(MBT[96:128, :], 0.0), [], [MBT[96:128, :]])
            A('dve', lambda e: e.memset(KC[:, :, :], 0.0), [], [KC[:, :, :]])
            for tt in range(4):
                ts = slice(tt * 512, (tt + 1) * 512)
                ps = nps()
                for k in range(8):
                    A('pe', lambda e, ps=ps, k=k, ts=ts: e.matmul(ps[:, :], lhsT=WIN[:, k, 512:640], rhs=HT[:, k, ts], start=(k == 0), stop=(k == 7)),
                      [WIN[:, k, 512:640], HT[:, k, ts]], [ps[:, :]])
                A('act', lambda e, ps=ps, ts=ts: e.activation(out=KT[0:64, 0, ts], in_=ps[0:64, :], func=AF.Copy), [ps[0:64, :]], [KT[0:64, 0, ts]])
                A('dve', lambda e, ps=ps, ts=ts: e.tensor_copy(out=KX[64:128, 1, ts], in_=ps[64:128, :]), [ps[64:128, :]], [KX[64:128, 1, ts]])
            SGB = KT[0:12, 1, :]
            for tt in range(4):
                ts = slice(tt * 512, (tt + 1) * 512)
                ps = nps()
                for k in range(8):
                    A('pe', lambda e, ps=ps, k=k, ts=ts: e.matmul(ps[0:12, :], lhsT=WIN[:, k, 640:652], rhs=HT[:, k, ts],
                                                                  start=(k == 0), stop=(k == 7)),
                      [WIN[:, k, 640:652], HT[:, k, ts]], [ps[0:12, :]])
                A('act', lambda e, ps=ps, ts=ts: e.activation(out=SGB[:, ts], in_=ps[0:12, :], func=AF.Sigmoid), [ps[0:12, :]], [SGB[:, ts]])
            proj_tm(652, 128, 0)
            P.phase = P.phase.split('/')[0] + '/cmp'
            P.dma('pool', W1[0:64, :, :], ck1_d[l].rearrange("(l d) h -> d l h", d=64), semkey='dma_w1')
            P.dma('pool', W1[64:128, :, :], cv1_d[l].rearrange("(l d) h -> d l h", d=64), semkey='dma_w1')
            for kv, w2d in enumerate((ck2_d, cv2_d)):
                for dup in range(2):
                    P.dma('pool', W2[:, kv, :, dup * 64:(dup + 1) * 64], w2d[l].rearrange("(a p) d -> p a d", p=128), semkey='dma_w2')
            A('dve', lambda e: e.tensor_copy(out=POSB[:, :], in_=PK[:, PK_POS + l * 32:PK_POS + (l + 1) * 32]),
              [PK[:, PK_POS + l * 32:PK_POS + (l + 1) * 32]], [POSB[:, :]])
            for kv in range(2):
                b0 = 64 * kv
                for half in range(2):
                    ps = nps()
                    ps2 = nps()
                    for li in range(32):
                        lw = W1[b0:b0 + 64, li, half * 128:(half + 1) * 128]
                        xs = KX[b0:b0 + 64, 0, li:li + 16 * 126 + 1:16]
                        A('pe', lambda e, ps=ps, lw=lw, xs=xs, li=li: e.matmul(ps[:, 0:127], lhsT=lw, rhs=xs, start=(li == 0), stop=(li == 31)),
                          [lw, xs], [ps[:, 0:127]])
                    for li in range(32):
                        lw = W1[b0:b0 + 64, li, half * 128:(half + 1) * 128]
                        pb = POSB[b0:b0 + 64, li:li + 1]
                        A('pe', lambda e, ps2=ps2, lw=lw, pb=pb, li=li: e.matmul(ps2[:, 0:1], lhsT=lw, rhs=pb, start=(li == 0), stop=(li == 31)),
                          [lw, pb], [ps2[:, 0:1]])
                    bc = B1[:, kv * 2 + half:kv * 2 + half + 1]
                    A('dve', lambda e, ps2=ps2, bc=bc: e.tensor_copy(out=bc, in_=ps2[:, 0:1]), [ps2[:, 0:1]], [bc])
                    x, u, v = FT[0][:, 0:127], FT[1][:, 0:127], FT[2][:, 0:127]
                    A('act', lambda e, ps=ps, bc=bc, x=x: e.activation(out=x, in_=ps[:, 0:127], func=AF.Identity, bias=bc), [ps[:, 0:127], bc], [x])
                    A('dve', lambda e, x=x, u=u: e.tensor_tensor(out=u, in0=x, in1=x, op=ALU.mult), [x], [u])
                    A('dve', lambda e, u=u: e.tensor_scalar(out=u, in0=u, scalar1=0.044715, scalar2=1.0, op0=ALU.mult, op1=ALU.add), [u], [u])
                    A('dve', lambda e, x=x, u=u, v=v: e.tensor_tensor(out=v, in0=u, in1=x, op=ALU.mult), [u, x], [v])
                    A('act', lambda e, v=v, u=u: e.activation(out=u, in_=v, func=AF.Tanh, scale=GELU_C), [v], [u])
                    A('dve', lambda e, u=u: e.tensor_scalar(out=u, in0=u, scalar1=1.0, scalar2=0.5, op0=ALU.add, op1=ALU.mult), [u], [u])
                    gl = GEL[:, kv * 2 + half, 0:127]
                    A('dve', lambda e, x=x, u=u, gl=gl: e.tensor_tensor(out=gl, in0=u, in1=x, op=ALU.mult), [u, x], [gl])
            ps = nps()
            for half in range(2):
                A('pe', lambda e, ps=ps, half=half: e.matmul(ps[:, 0:127], lhsT=W2[:, 0, half, :], rhs=GEL[:, half, 0:127], start=(half == 0), stop=(half == 1)),
                  [W2[:, 0, half, :], GEL[:, half, 0:127]], [ps[:, 0:127]])
            A('act', lambda e, ps=ps: e.activation(out=KC[0:64, 0, 0:127], in_=ps[0:64, 0:127], func=AF.Copy), [ps[0:64, 0:127]], [KC[0:64, 0, 0:127]])
            A('dve', lambda e, ps=ps: e.tensor_copy(out=KC[64:128, 1, 0:127], in_=ps[64:128, 0:127]), [ps[64:128, 0:127]], [KC[64:128, 1, 0:127]])
            ps = nps()
            for half in range(2):
                A('pe', lambda e, ps=ps, half=half: e.matmul(ps[0:127, 0:64], lhsT=GEL[:, 2 + half, 0:127], rhs=W2[:, 1, half, 0:64], start=(half == 0), stop=(half == 1)),
                  [GEL[:, 2 + half, 0:127], W2[:, 1, half, 0:64]], [ps[0:127, 0:64]])
            A('dve', lambda e, ps=ps: e.tensor_copy(out=VC[0:127, 0:64], in_=ps[0:127, 0:64]), [ps[0:127, 0:64]], [VC[0:127, 0:64]])
            after_proj()
            P.dma('pool', WO[:, :, :], w_out_d[l, 512:768, :].rearrange("(m p) c -> p m c", p=128), semkey='dma_wo')
            GC = av(GREG, [SEQ], BF16)
            GWs = [av(GREG + i * 2048, [1024], BF16) for i in range(2)]
            GNs = [av(GREG + 4096 + i * 1280, [640], BF16) for i in range(2)]
            IMPB = IMP[:, :, :].rearrange("p a b -> p (a b)").bitcast(BF16)
            GCs = [IMPB[:, i * 512:(i + 1) * 512] for i in range(2)]

            def cmp_terms(h, ts, gc=None):
                b0 = (h % 2) * 64
                g = GC[0:127, ts] if gc is None else gc[0:127, :]
                return [(KC[:, h % 2, 0:127], QT[:, h // 2, ts]), (IDENT[0:127, 0:127], g)]
            P.phase = P.phase.split('/')[0] + '/pass1'
            for h in range(4):
                load_g('c', h, GC[:, :], slot=0)
                for qt in range(4):
                    ts = slice(qt * 512, (qt + 1) * 512)
                    z = ZB[zi[0] % 3]
                    zi[0] += 1
                    pt = PT[pti[0] % 3]
                    pti[0] += 1
                    terms = cmp_terms(h, ts)
                    for i, (a, bb) in enumerate(terms):
                        A('pe', lambda e, a=a, bb=bb, i=i, z=z: e.matmul(z[0:127, :], lhsT=a, rhs=bb, start=(i == 0), stop=(i == 1)), [a, bb], [z[0:127, :]])
                    A('act', lambda e, z=z, pt=pt: e.activation(out=pt[0:127, :], in_=z[0:127, :], func=AF.Exp), [z[0:127, :]], [pt[0:127, :]])
                    for t4 in range(4):
                        tb = qt * 4 + t4
                        ps = mps()
                        A('pe', lambda e, ps=ps, pt=pt, t4=t4: e.matmul(ps[:, 0:33], lhsT=pt[0:127, t4 * 128:(t4 + 1) * 128], rhs=COVER[0:127, 0:33], start=True, stop=True),
                          [pt[0:127, t4 * 128:(t4 + 1) * 128], COVER[0:127, 0:33]], [ps[:, 0:33]])
                        A('dve', lambda e, ps=ps: e.tensor_scalar(out=IMR[:, :], in0=ps[:, 32:33], scalar1=1e-30, scalar2=None, op0=ALU.max), [ps[:, 32:33]], [IMR[:, :]])
                        A('dve', lambda e: e.reciprocal(out=IMR[:, :], in_=IMR[:, :]), [IMR[:, :]], [IMR[:, :]])
                        if h == 0:
                            A('dve', lambda e, ps=ps, tb=tb: e.tensor_scalar(out=IMP[:, tb, :], in0=ps[:, 0:32], scalar1=IMR[:, 0:1], scalar2=None, op0=ALU.mult),
                              [ps[:, 0:32], IMR[:, :]], [IMP[:, tb, :]])
                        else:
                            A('dve', lambda e, ps=ps, tb=tb: e.scalar_tensor_tensor(out=IMP[:, tb, :], in0=ps[:, 0:32], scalar=IMR[:, 0:1], in1=IMP[:, tb, :],
                                                                                    op0=ALU.mult, op1=ALU.add),
                              [ps[:, 0:32], IMR[:, :], IMP[:, tb, :]], [IMP[:, tb, :]])
            P.phase = P.phase.split('/')[0] + '/sel'
            for tb in range(16):
                tsl = slice(tb * 128, (tb + 1) * 128)
                oa, ob = 2 * tb, 2 * tb + 1
                A('pool', lambda e: e.memset(GATE[:, :], -1e30), [], [GATE[:, :]])
                if oa > 0:
                    A('dve', lambda e, tb=tb, oa=oa: e.tensor_copy(out=GATE[0:64, 0:oa], in_=IMP[0:64, tb, 0:oa]), [IMP[0:64, tb, 0:oa]], [GATE[0:64, 0:oa]])
                A('dve', lambda e, tb=tb, ob=ob: e.tensor_copy(out=GATE[64:128, 0:ob], in_=IMP[64:128, tb, 0:ob]), [IMP[64:128, tb, 0:ob]], [GATE[64:128, 0:ob]])
                A('dve', lambda e: e.max(out=TOP8[:, :], in_=GATE[:, :]), [GATE[:, :]], [TOP8[:, :]])
                A('dve', lambda e: e.tensor_scalar(out=MB[:, :], in0=GATE[:, :], scalar1=TOP8[:, 2:3], scalar2=-1.0, op0=ALU.is_ge, op1=ALU.add),
                  [GATE[:, :], TOP8[:, 2:3]], [MB[:, :]])
                A('dve', lambda e, oa=oa: e.memset(MB[0:64, oa:32], -1.0), [], [MB[0:64, oa:32]])
                A('dve', lambda e, oa=oa: e.memset(MB[0:64, oa:oa + 1], 0.0), [], [MB[0:64, oa:oa + 1]])
                A('dve', lambda e, ob=ob: e.memset(MB[64:128, ob:32], -1.0), [], [MB[64:128, ob:32]])
                A('dve', lambda e, ob=ob: e.memset(MB[64:128, ob:ob + 1], 0.0), [], [MB[64:128, ob:ob + 1]])
                ps = mps()
                A('pe', lambda e, ps=ps: e.matmul(ps[0:32, 0:128], lhsT=MB[:, :], rhs=BIGI[:, :], start=True, stop=True), [MB[:, :], BIGI[:, :]], [ps[0:32, 0:128]])
                A('act', lambda e, ps=ps, tsl=tsl: e.activation(out=MBT[64:96, tsl], in_=ps[0:32, 0:128], func=AF.Copy), [ps[0:32, 0:128]], [MBT[64:96, tsl]])
            P.phase = P.phase.split('/')[0] + '/pass2'
            EPDEF[0] = 1
            pipe = Pipe()
            obi = [0]

            def branch_ep(h, ts, br, O):
                def ep():
                    r, r2, t, acc = FT[0], FT[1], FT[2], FT[3]
                    recip_act(r[0:64, :], O[64:128, :])
                    ps = mps()
                    A('pe', lambda e: e.matmul(ps[0:64, :], lhsT=SELB[0:12, br * 4 + h, :], rhs=SGB[:, ts], start=True, stop=True),
                      [SELB[0:12, br * 4 + h, :], SGB[:, ts]], [ps[0:64, :]])
                    A('dve', lambda e: e.tensor_tensor(out=r2[0:64, :], in0=ps[0:64, :], in1=r[0:64, :], op=ALU.mult),
                      [ps[0:64, :], r[0:64, :]], [r2[0:64, :]])
                    dst = acc if br == 0 else t
                    A('dve', lambda e: e.tensor_tensor(out=dst[0:64, :], in0=O[0:64, :], in1=r2[0:64, :], op=ALU.mult),
                      [O[0:64, :], r2[0:64, :]], [dst[0:64, :]])
                    if br > 0:
                        A('pool', lambda e: e.tensor_tensor(out=acc[0:64, :], in0=acc[0:64, :], in1=t[0:64, :], op=ALU.add),
                          [acc[0:64, :], t[0:64, :]], [acc[0:64, :]])
                    if br == 2:
                        write_mix(mix_dst(h, ts), acc[0:64, :])
                return ep

            def next_o():
                o = OB[obi[0] % 3]
                obi[0] += 1
                return o
            def load_gc_slice(h, qt, dst, slot):
                src = bass.AP(scr_c[h].tensor, 2032 + qt * 512, [[L_C, 128], [1, 512]])
                P.add('pool', lambda e: e.dma_start(out=dst, in_=src), reads=[scr_c[h][:]], writes=[dst], dma=True,
                      semkey='dma_gcs%d' % slot)

            def load_head(h):
                load_g('n', 4 + h, GNs[h % 2][:, :], slot=2 + (h % 2))
                load_g('w', h, GWs[h % 2][:, :], slot=4 + (h % 2))
            load_head(0)
            gci = [0]
            load_gc_slice(0, 0, GCs[0], 0)
            for h in range(4):
                b0 = (h % 2) * 64
                GN = GNs[h % 2]
                GW = GWs[h % 2]
                if h == 0:
                    A('dve', lambda e: e.tensor_copy(out=MBT[0:64, :], in_=QT[0:64, 0, :]), [QT[0:64, 0, :]], [MBT[0:64, :]])
                if h + 1 < 4:
                    load_head(h + 1)
                for qt in range(4):
                    t0 = qt * 512
                    ts = slice(t0, t0 + 512)
                    gcs = GCs[gci[0] % 2]
                    gci[0] += 1
                    nh, nq = (h, qt + 1) if qt < 3 else (h + 1, 0)
                    if nh < 4:
                        load_gc_slice(nh, nq, GCs[gci[0] % 2], gci[0] % 2)
                    Oc = next_o()
                    sm_tile(pipe, cmp_terms(h, ts, gcs), 0, 512, 1.0, None, VC[0:127, :], Oc, True, True, nk=127,
                            epilogue=branch_ep(h, ts, 0, Oc))
                    blocks = causal_blocks(qt)
                    Os = next_o()
                    for bi, (kb, q0, N, D) in enumerate(blocks):
                        kT = OHS[:, kb * 128:(kb + 1) * 128]
                        qT = MBT[:, t0 + q0:t0 + q0 + N]
                        ext, cb = bias_terms(GN, 4 + h, q0, N, D)
                        last = (bi == len(blocks) - 1)
                        sm_tile(pipe, [(kT, qT)] + ext, q0, N, 1.0, cb, VT[:, kb, 0, :], Os, bi == 0, last,
                                epilogue=(branch_ep(h, ts, 1, Os) if last else None))
                    if qt == 3 and h < 3:
                        nb0 = ((h + 1) % 2) * 64
                        A('dve', lambda e, nb0=nb0, h=h: e.tensor_copy(out=MBT[0:64, :], in_=QT[nb0:nb0 + 64, (h + 1) // 2, :]),
                          [QT[nb0:nb0 + 64, (h + 1) // 2, :]], [MBT[0:64, :]])
                    wblocks = [(kb, q0, N, D) for (kb, q0, N, D) in blocks if D <= 512]
                    Ow = next_o()
                    for bi, (kb, q0, N, D) in enumerate(wblocks):
                        kT = (KT[:, 0, kb * 128:(kb + 1) * 128] if h % 2 == 0 else KX[:, 1, kb * 128:(kb + 1) * 128])
                        qT = QT[:, h // 2, t0 + q0:t0 + q0 + N]
                        last = (bi == len(wblocks) - 1)
                        sm_tile(pipe, [(kT, qT), (IDENT[:, :], GW[:, D:D + N])], q0, N, 1.0, None, VT[:, kb, 1, :], Ow, bi == 0, last,
                                epilogue=(branch_ep(h, ts, 2, Ow) if last else None))
                pipe.flush()
            FILL[0] = 0

        AFTER_PROJ = [None]

        def after_proj():
            if AFTER_PROJ[0]:
                AFTER_PROJ[0]()
                AFTER_PROJ[0] = None

        MIXERS = {0: mixer_sb, 1: mixer_moba, 2: mixer_nsa, 3: mixer_diff}

        for l in range(depth):
            P.phase = 'L%d norm' % l
            A('pool', lambda e: e.memset(VT[:, :, :, 64:128], 1.0), [], [VT[:, :, :, 64:128]])
            if mixers:
                for tt in range(4):
                    rmsnorm_tile(tt, PK_NA + l * 8, lambda c, tt=tt: (HT[:, c, tt * 512:(tt + 1) * 512], None))
            for mixer in mixers:
                P.phase = 'L%d mixer%d' % (l, mixer)
                mi = list(mixers).index(mixer)
                if mi == 0:
                    P.dma('pool', WIN[:, :, :], w_in_d[l, mixer].rearrange("(c p) f -> p c f", p=128), semkey='dma_win')
                if mi + 1 < len(mixers):
                    nxt = mixers[mi + 1]
                    AFTER_PROJ[0] = (lambda l=l, nxt=nxt: P.dma('pool', WIN[:, :, :], w_in_d[l, nxt].rearrange("(c p) f -> p c f", p=128), semkey='dma_win'))
                else:
                    AFTER_PROJ[0] = None
                if mixer != 2:
                    P.dma('pool', WO[:, :, :], w_out_d[l, mixer * 256:(mixer + 1) * 256, :].rearrange("(m p) c -> p m c", p=128), semkey='dma_wo')
                MIXERS[mixer](l)
                if dbg == (l, mixer) and cfg.get('dump'):
                    for nm in cfg['dump']:
                        src = {'QT': QT, 'KT': KT, 'KX': KX, 'HT0': HT[:, 0:2, :], 'HT1': HT[:, 2:4, :]}[nm]
                        dd = nc.dram_tensor("dump_" + nm, [128, 2 * SEQ] if nm != 'MBT' else [64, SEQ], BF16, kind="ExternalOutput").ap()
                        if nm == 'MBT':
                            P.dma('sp', dd[:, :], src[:, :], semkey='dma_out')
                        else:
                            P.dma('sp', dd.rearrange("p (a b) -> p a b", a=2), src, semkey='dma_out')
                if dbg == (l, mixer):
                    for m in range(2):
                        P.dma('sp', dbg_d[m * 128:(m + 1) * 128, :], MIXM[:, m, :], semkey='dma_out')
                P.phase = 'L%d wout%d' % (l, mixer)
                apply_wout(l, mixer)
            if do_mlp:
                P.phase = 'L%d mlp' % l
                for tt in range(4):
                    rmsnorm_tile(tt, PK_NM + l * 8, lambda c, tt=tt: (HT[:, c, tt * 512:(tt + 1) * 512], None))
                rli = 0
                for fg in range(8):
                    wu, wd, at = WUP[fg % 2], WDN[fg % 2], AT[fg % 2]
                    P.dma('pool', wu[:, :, :], w_up_d[l, :, fg * 512:(fg + 1) * 512].rearrange("(c p) f -> p c f", p=128), semkey='dma_wu%d' % (fg % 2))
                    P.dma('pool', wd[:, :, :], w_down_d[l, fg * 512:(fg + 1) * 512, :].rearrange("(m p) c -> p m c", p=128), semkey='dma_wd%d' % (fg % 2))
                    for tt in range(4):
                        ts = slice(tt * 512, (tt + 1) * 512)
                        for m in range(4):
                            ps = nps()
                            for k in range(8):
                                extra = []
                                if probe_wait:
                                    dd = DUM[:, k:k + 1]
                                    A('dve', lambda e, dd=dd: e.memset(dd, 0.0), [], [dd])
                                    extra = [dd]
                                pb0 = pb if (k % 2 == 0) else 0
                                A('pe', lambda e, ps=ps, k=k, m=m, wu=wu, ts=ts, pb0=pb0: e.matmul(
                                    ps[:, :], lhsT=wu[pb0:pb0 + pk, k, m * 128:(m + 1) * 128], rhs=HT[pb0:pb0 + pk, k, ts], start=(k == 0), stop=(k == 7)),
                                  [wu[:, k, m * 128:(m + 1) * 128], HT[:, k, ts]] + extra, [ps[:, :]])
                            rl = RL[rli % 2]
                            rli += 1
                            A('act', lambda e, ps=ps, rl=rl: e.activation(out=rl[:, :], in_=ps[:, :], func=AF.Relu), [ps[:, :]], [rl[:, :]])
                            A('pool', lambda e, rl=rl, at=at, m=m, ts=ts: e.tensor_tensor(out=at[:, m, ts], in0=rl[:, :], in1=rl[:, :], op=ALU.mult),
                              [rl[:, :]], [at[:, m, ts]])
                    for c in range(8):
                        for tt in range(4):
                            ts = slice(tt * 512, (tt + 1) * 512)
                            ps = nps()
                            for m in range(4):
                                A('pe', lambda e, ps=ps, m=m, c=c, wd=wd, at=at, ts=ts: e.matmul(
                                    ps[:, :], lhsT=wd[:, m, c * 128:(c + 1) * 128], rhs=at[:, m, ts], start=(m == 0), stop=(m == 3)),
                                  [wd[:, m, c * 128:(c + 1) * 128], at[:, m, ts]], [ps[:, :]])
                            A('dve', lambda e, ps=ps, c=c, ts=ts: e.tensor_tensor(out=XT[:, c, ts], in0=XT[:, c, ts], in1=ps[:, :], op=ALU.add),
                              [XT[:, c, ts], ps[:, :]], [XT[:, c, ts]])

        P.phase = 'final'
        oi = [0]
        for tt in range(4):
            ts = slice(tt * 512, (tt + 1) * 512)

            def dstf(c, ts=ts):
                ob = OUTB[oi[0] % 2]
                oi[0] += 1

                def post(ob=ob, c=c, ts=ts):
                    P.dma('sp', outT_d[c * 128:(c + 1) * 128, ts], ob[:, :], semkey='dma_out')
                return ob[:, :], post
            rmsnorm_tile(tt, PK_NF, dstf)
        P.final_dma_keys.append('dma_out')
        n = P.emit()
    _CACHE['prog'] = P
    return nc, n


def make_in_maps(inputs, nb=None):
    x = np.asarray(inputs['x'], np.float32)
    B = x.shape[0] if nb is None else nb
    pk = pack_small(inputs)
    hc = host_consts()
    shared = {
        'pk': pk,
        'w_in': permute_w_in(np.asarray(inputs['w_in'], np.float32)),
        'w_out': np.ascontiguousarray(inputs['w_out'], np.float32),
        'w_up': np.ascontiguousarray(inputs['w_up'], np.float32),
        'w_down': np.ascontiguousarray(inputs['w_down'], np.float32),
        'cmp_k_w1': np.ascontiguousarray(inputs['cmp_k_w1'], np.float32),
        'cmp_k_w2': np.ascontiguousarray(inputs['cmp_k_w2'], np.float32),
        'cmp_v_w1': np.ascontiguousarray(inputs['cmp_v_w1'], np.float32),
        'cmp_v_w2': np.ascontiguousarray(inputs['cmp_v_w2'], np.float32),
    }
    shared.update(hc)
    in_maps = []
    for b in range(B):
        m = dict(shared)
        m['xT'] = np.ascontiguousarray(x[b].T)
        in_maps.append(m)
    return in_maps


def kernel(**inputs):
    if 'nc' not in _CACHE:
        _CACHE['nc'] = build()[0]
    nc = _CACHE['nc']
    in_maps = make_in_maps(inputs)
    B = len(in_maps)
    res = run_bass_kernel_spmd(nc, in_maps, core_ids=list(range(B)))
    out = np.stack([np.ascontiguousarray(r['outT'].T) for r in res.results], axis=0)
    return out.astype(np.float32)
```

```python
import numpy as np
import concourse.bass as bass
import concourse.mybir as mybir
from concourse.bass_utils import run_bass_kernel_spmd

F32 = mybir.dt.float32
BF16 = mybir.dt.bfloat16
AF = mybir.ActivationFunctionType
ALU = mybir.AluOpType
AX = mybir.AxisListType

D_MODEL = 1024
SEQ = 2048
DEPTH = 2
D_FF = 4096
D_IN = 2956
NCORES = 8
EPS = 1e-6

_DTSIZE = {F32: 4, BF16: 2, mybir.dt.int32: 4, mybir.dt.uint32: 4, mybir.dt.uint16: 2,
           mybir.dt.int16: 2, mybir.dt.uint8: 1, mybir.dt.int8: 1, mybir.dt.float32r: 4,
           mybir.dt.float16: 2}


def _prod(xs):
    r = 1
    for v in xs:
        r *= int(v)
    return r


def region(ap):
    t = ap.tensor
    name = t.name
    esz = _DTSIZE[ap.dtype]
    off = int(ap.offset)
    dims = ap.ap
    space = str(ap.space)
    if 'DRAM' in space.upper() or 'HBM' in space.upper() or type(t).__name__.startswith('DRam'):
        lo = off
        hi = off
        for (st, cnt) in dims:
            if st >= 0:
                hi += (cnt - 1) * st
            else:
                lo += (cnt - 1) * st
        return (name, 0, 1, lo * esz, (hi + 1) * esz)
    tsz = _DTSIZE[t.dtype]
    pstride = _prod(list(t.shape)[1:]) * tsz // esz
    p0 = off // pstride
    f0 = off % pstride
    (pst, pcnt) = dims[0]
    assert pst % pstride == 0 or pcnt == 1, (name, dims, pstride)
    pstep = max(1, pst // pstride)
    p1 = p0 + (pcnt - 1) * pstep + 1
    lo = f0
    hi = f0
    for (st, cnt) in dims[1:]:
        if st >= 0:
            hi += (cnt - 1) * st
        else:
            lo += (cnt - 1) * st
    return (name, p0, p1, lo * esz, (hi + 1) * esz)


def _overlap(a, b):
    return a[1] < b[2] and b[1] < a[2] and a[3] < b[4] and b[3] < a[4]


def _covers(a, b):
    return a[1] <= b[1] and a[2] >= b[2] and a[3] <= b[3] and a[4] >= b[4]


_CACHE = {}


class _Op:
    __slots__ = ('idx', 'eng', 'fn', 'dma', 'semkey', 'deps', 'ordinal', 'waits', 'sig', 'semval', 'pe_group', 'phase')


class Prog:
    ENGS = ('pe', 'act', 'dve', 'pool', 'sp')

    def __init__(self, nc):
        self.nc = nc
        self.ops = []
        self.acc = {}
        self.final_dma_keys = []
        self.phase = ''

    def add(self, eng, fn, reads=(), writes=(), dma=False, semkey=None):
        op = _Op()
        op.idx = len(self.ops)
        op.eng = eng
        op.fn = fn
        op.dma = dma
        op.deps = set()
        op.phase = self.phase
        rregs = [region(a) for a in reads]
        wregs = [region(a) for a in writes]
        def _banks(r):
            return [(r[0], 0, 128, bk * 2048, (bk + 1) * 2048) for bk in range(r[3] // 2048, (r[4] - 1) // 2048 + 1)]
        ps_r = [x for r in rregs if r[0].startswith('PS') for x in _banks(r)]
        rregs = [r for r in rregs if not r[0].startswith('PS')]
        wregs = [x for w in wregs for x in (_banks(w) if w[0].startswith('PS') else [w])] + ps_r
        if dma:
            op.semkey = semkey if semkey is not None else ('dma_' + wregs[0][0])
        else:
            op.semkey = None
        stream = op.semkey if dma else eng
        for r in rregs:
            lst = self.acc.get(r[0], [])
            for rec in lst:
                if rec[1] and _overlap(rec[0], r):
                    op.deps.update(rec[2].values())
        for w in wregs:
            lst = self.acc.get(w[0], [])
            for rec in lst:
                if _overlap(rec[0], w):
                    op.deps.update(rec[2].values())
        for w in wregs:
            lst = self.acc.setdefault(w[0], [])
            lst[:] = [rec for rec in lst if not _covers(w, rec[0])]
            lst.append([w, True, {stream: op.idx}])
        for r in rregs:
            lst = self.acc.setdefault(r[0], [])
            done = False
            for rec in lst:
                if (not rec[1]) and rec[0] == r:
                    rec[2][stream] = op.idx
                    done = True
                    break
            if not done:
                lst.append([r, False, {stream: op.idx}])
        op.deps.discard(op.idx)
        self.ops.append(op)
        return op

    def dma(self, q, out, in_, semkey=None):
        return self.add(q, lambda e: e.dma_start(out=out, in_=in_), reads=[in_], writes=[out],
                        dma=True, semkey=semkey)

    def emit(self):
        nc = self.nc
        ops = self.ops
        cnt = {}
        for op in ops:
            s = op.semkey if op.dma else op.eng
            cnt[s] = cnt.get(s, 0) + 1
            op.ordinal = cnt[s]
            op.sig = op.dma
            op.waits = []
        waited = {e: {} for e in self.ENGS}
        import bisect
        dma_idx = {}
        for op in ops:
            if op.dma:
                dma_idx.setdefault(op.semkey, []).append(op.idx)
        for op in ops:
            need = {}
            for d in op.deps:
                a = ops[d]
                s = a.semkey if a.dma else a.eng
                if (not a.dma) and a.eng == op.eng and op.eng == 'pe' and not op.dma:
                    continue
                o = a.ordinal
                if a.dma:
                    o = bisect.bisect_left(dma_idx[s], op.idx)
                if o > need.get(s, 0):
                    need[s] = o
            w = waited[op.eng]
            for s, o in need.items():
                if w.get(s, 0) >= o:
                    continue
                w[s] = o
                op.waits.append((s, o))
        needed = set()
        for op in ops:
            for so in op.waits:
                needed.add(so)
        semvals = {}
        run = {}
        for op in ops:
            s = op.semkey if op.dma else op.eng
            if op.dma:
                run[s] = run.get(s, 0) + 16
                semvals[(s, op.ordinal)] = run[s]
            else:
                if (s, op.ordinal) in needed:
                    op.sig = True
                    run[s] = run.get(s, 0) + 1
                    semvals[(s, op.ordinal)] = run[s]
        final_vals = dict(run)
        streams = sorted(run.keys())
        from contextlib import ExitStack
        with ExitStack() as es:
            sems = {}
            for s in streams:
                sems[s] = es.enter_context(nc.semaphore('s_' + s))
            block = es.enter_context(nc.Block())
            per_eng = {e: [op for op in ops if op.eng == e] for e in self.ENGS}
            final_keys = list(self.final_dma_keys)

            def body(e, eops, is_last_waiter):
                for op in eops:
                    for (s, o) in op.waits:
                        e.wait_ge(sems[s], semvals[(s, o)])
                    ins = op.fn(e)
                    if op.sig:
                        s = op.semkey if op.dma else op.eng
                        ins.then_inc(sems[s], 16 if op.dma else 1)
                if is_last_waiter:
                    for k in final_keys:
                        e.wait_ge(sems[k], final_vals[k])

            @block.tensor
            def _(e):
                body(e, per_eng['pe'], False)

            @block.scalar
            def _(e):
                body(e, per_eng['act'], False)

            @block.vector
            def _(e):
                body(e, per_eng['dve'], False)

            @block.gpsimd
            def _(e):
                body(e, per_eng['pool'], False)

            @block.sync
            def _(e):
                body(e, per_eng['sp'], True)
        return len(ops)


import math
from contextlib import ExitStack

NEGV = -30000.0
BIGM = 240000.0
WINC = 784
GELU_C = math.sqrt(2.0 / math.pi)

PK_NA = 0
PK_NM = 16
PK_NF = 32
PK_SUBLN = 40
PK_TAB = 44
PK_LAM = 64
PK_POS = 320
NPK = 384

L_N, L_W, L_C = 768, 1152, 4096
OFF_N, OFF_W, OFF_C = 127, 127, 2063


def _t5_bucket(n):
    n = np.maximum(n, 0)
    nf = np.maximum(n, 1).astype(np.float32)
    large = 16 + (np.log(nf / np.float32(16)) / np.float32(math.log(128 / 16)) * np.float32(16)).astype(np.int32)
    large = np.minimum(large, 31)
    return np.where(n < 16, n, large)


def _onehot(L, off, win):
    oh = np.zeros((33, L), np.float32)
    d = np.arange(L) - off
    masked = d < 0
    if win:
        masked = masked | (d >= 512)
    b = _t5_bucket(d)
    for i in range(L):
        if masked[i]:
            oh[32, i] = 1.0
        else:
            oh[b[i], i] = 1.0
    return oh


def _cover():
    cmp_idx = np.arange(127)[:, None] * 16 + np.arange(32)[None, :]
    s_start = np.arange(32) * 64
    cover = np.clip(np.minimum(cmp_idx[:, -1][:, None], (s_start + 63)[None, :])
                    - np.maximum(cmp_idx[:, 0][:, None], s_start[None, :]) + 1, 0, None) / 32.0
    cv = np.zeros((128, 33), np.float32)
    cv[:127, :32] = cover
    cv[:127, 32] = 1.0
    return cv


def host_consts():
    return {'ohn': _onehot(L_N, OFF_N, False), 'ohw': _onehot(L_W, OFF_W, True),
            'ohc': _onehot(L_C, OFF_C, False), 'cover': _cover()}


def pack_small(inputs):
    pk = np.zeros((128, NPK), np.float32)
    for l in range(DEPTH):
        pk[:, PK_NA + l * 8:PK_NA + l * 8 + 8] = inputs['norm_attn'][l].reshape(8, 128).T
        pk[:, PK_NM + l * 8:PK_NM + l * 8 + 8] = inputs['norm_mlp'][l].reshape(8, 128).T
        pk[:, PK_SUBLN + l] = np.tile(inputs['diff_subln'][l], 2)
        pk[0, PK_LAM + l * 128:PK_LAM + (l + 1) * 128] = inputs['diff_lambda'][l].reshape(-1)
        pk[0:64, PK_POS + l * 32:PK_POS + (l + 1) * 32] = inputs['cmp_pos_k'][l].T
        pk[64:128, PK_POS + l * 32:PK_POS + (l + 1) * 32] = inputs['cmp_pos_v'][l].T
    pk[:, PK_NF:PK_NF + 8] = inputs['final_norm'].reshape(8, 128).T
    pk[0:32, PK_TAB:PK_TAB + 12] = inputs['rel_bias']
    return pk


def permute_w_in(w_in):
    L = w_in.shape[0]
    out = np.zeros((L, 4, D_MODEL, WINC), np.float32)
    out[:, 0, :, 0:768] = w_in[:, :, 0:768]
    out[:, 1, :, 0:768] = w_in[:, :, 768:1536]
    o = 1536
    ns = out[:, 2]
    ns[:, :, 0:256] = w_in[:, :, o:o + 256]
    ns[:, :, 256:320] = w_in[:, :, o + 256:o + 320]
    ns[:, :, 320:384] = w_in[:, :, o + 320:o + 384]
    ns[:, :, 384:448] = w_in[:, :, o + 384:o + 448]
    ns[:, :, 448:512] = w_in[:, :, o + 384:o + 448]
    ns[:, :, 512:576] = w_in[:, :, o + 512:o + 576]
    ns[:, :, 576:640] = w_in[:, :, o + 512:o + 576]
    ns[:, :, 640:652] = w_in[:, :, o + 640:o + 652]
    ns[:, :, 652:716] = w_in[:, :, o + 448:o + 512]
    ns[:, :, 716:780] = w_in[:, :, o + 576:o + 640]
    out[:, 3, :, 0:768] = w_in[:, :, 2188:2956]
    return out


def build(cfg=None):
    cfg = cfg or {}
    mixers = cfg.get('mixers', (0, 1, 2, 3))
    depth = cfg.get('depth', DEPTH)
    do_mlp = cfg.get('mlp', True)
    dbg = cfg.get('dbg', None)
    stop = cfg.get('stop', 99)
    filler = cfg.get('filler', 0)
    probe_wait = cfg.get('probe_wait', 0)
    pk = cfg.get('probe_k', 128)
    pb = cfg.get('probe_b', 0)
    nc = bass.Bass("TRN2", target_bir_lowering=False)

    def din(name, shape, dt=F32):
        return nc.dram_tensor(name, shape, dt, kind="ExternalInput").ap()
    xT_d = din("xT", [D_MODEL, SEQ])
    pk_d = din("pk", [128, NPK])
    w_in_d = din("w_in", [DEPTH, 4, D_MODEL, WINC])
    w_out_d = din("w_out", [DEPTH, D_MODEL, D_MODEL])
    w_up_d = din("w_up", [DEPTH, D_MODEL, D_FF])
    w_down_d = din("w_down", [DEPTH, D_FF, D_MODEL])
    ck1_d = din("cmp_k_w1", [DEPTH, 2048, 256])
    ck2_d = din("cmp_k_w2", [DEPTH, 256, 64])
    cv1_d = din("cmp_v_w1", [DEPTH, 2048, 256])
    cv2_d = din("cmp_v_w2", [DEPTH, 256, 64])
    ohn_d = din("ohn", [33, L_N])
    ohw_d = din("ohw", [33, L_W])
    ohc_d = din("ohc", [33, L_C])
    cover_d = din("cover", [128, 33])
    outT_d = nc.dram_tensor("outT", [D_MODEL, SEQ], F32, kind="ExternalOutput").ap()
    dbg_d = nc.dram_tensor("dbg", [256, SEQ], BF16, kind="ExternalOutput").ap() if dbg is not None else None
    scr_n = [nc.dram_tensor("scr_n%d" % h, [128 * (L_N + 1) + 8], F32, kind="Internal").ap() for h in range(12)]
    scr_w = [nc.dram_tensor("scr_w%d" % h, [128 * (L_W + 1) + 8], F32, kind="Internal").ap() for h in range(4)]
    scr_c = [nc.dram_tensor("scr_c%d" % h, [128 * (L_C + 16) + 8], F32, kind="Internal").ap() for h in range(4)]

    with ExitStack() as es:
        def sb(name, shape, dt):
            return es.enter_context(nc.sbuf_tensor(name, shape, dt))

        XT = sb("XT", [128, 8, SEQ], F32)
        ARENA_B = 106 * 1024
        ARENA = sb("ARENA", [128, ARENA_B // 2], BF16)

        def av(off, shape, dt):
            nbytes = _prod(shape) * _DTSIZE[dt]
            assert off % 4 == 0 and off + nbytes <= ARENA_B, (off, shape)
            a = ARENA[:, off // 2:(off + nbytes) // 2]
            if dt != BF16:
                a = a.bitcast(dt)
            if len(shape) == 2:
                a = a.rearrange("p (a b) -> p a b", a=shape[0])
            elif len(shape) == 3:
                a = a.rearrange("p (a b c) -> p a b c", a=shape[0], b=shape[1])
            return a
        KB = 1024
        HT = av(0, [8, SEQ], BF16)
        QT = av(32 * KB, [2, SEQ], BF16)
        KT = av(40 * KB, [2, SEQ], BF16)
        GTF = av(40 * KB, [SEQ], F32)
        KX = av(48 * KB, [2, SEQ], BF16)
        VT = av(56 * KB, [16, 4, 128], BF16)
        WIN = av(72 * KB, [8, WINC], BF16)
        W1 = av(72 * KB, [32, 256], BF16)
        WO = av(85 * KB, [2, D_MODEL], BF16)
        MIXM = av(89 * KB, [2, SEQ], BF16)
        GREG = 97 * KB
        WUP = [av(32 * KB + i * 8 * KB, [8, 512], BF16) for i in range(2)]
        WDN = [av(48 * KB + i * 8 * KB, [4, D_MODEL], BF16) for i in range(2)]
        AT = [av(64 * KB + i * 16 * KB, [4, SEQ], BF16) for i in range(2)]
        OHC = av(32 * KB, [L_C], F32)
        OHN = av(48 * KB, [L_N], F32)
        OHW = av(52 * KB, [L_W], F32)
        TB = av(57 * KB, [12 * 128], F32)
        RREP = av(64 * KB, [L_C], F32)

        PK = sb("PK", [128, NPK], F32)
        ONESM = sb("ONESM", [128, 128], BF16)
        ONES64 = sb("ONES64", [128, 64], BF16)
        ONE1 = sb("ONE1", [128, 128], BF16)
        IDENT = sb("IDENT", [128, 128], BF16)
        BIGI = sb("BIGI", [128, 128], BF16)
        NEGU = sb("NEGU", [128, 128], BF16)
        OHS = sb("OHS", [128, SEQ], BF16)
        MBT = sb("MBT", [128, SEQ], BF16)
        CBH = sb("CBH", [128, 12], F32)
        COVER = sb("COVER", [128, 33], BF16)
        RSTD = sb("RSTD", [128, 512], F32)
        EPSC = sb("EPSC", [128, 1], F32)
        ONEC = sb("ONEC", [128, 1], F32)
        TINYC = sb("TINYC", [128, 1], F32)
        LAMC = sb("LAMC", [64, 2], F32)
        LTMP = sb("LTMP", [1, 256], F32)
        SQ = [sb("SQ%d" % i, [128, 512], BF16) for i in range(2)]
        PT = [sb("PT%d" % i, [128, 512], BF16) for i in range(3)]
        FT = [sb("FT%d" % i, [128, 512], F32) for i in range(4)]
        SPB = [sb("SPB%d" % i, [128, 512], BF16) for i in range(2)]
        RL = FT[0:2]
        OUTB = FT[2:4]
        SELB = sb("SELB", [12, 12, 64], BF16)
        ONER = sb("ONER", [1, 64], F32)
        KM = sb("KM", [128, 2, 8], F32)
        KMB = sb("KMB", [128, 2, 8], BF16)
        GATE = sb("GATE", [128, 32], F32)
        TOP8 = sb("TOP8", [128, 8], F32)
        MB = sb("MB", [128, 32], BF16)
        MB8 = sb("MB8", [128, 8], BF16)
        GS_EXTRA = [(sb("MB8_%d" % i, [128, 8], BF16), sb("MB_%d" % i, [128, 32], BF16), sb("GATE_%d" % i, [128, 32], F32), sb("TOP8_%d" % i, [128, 8], F32)) for i in range(3)]
        IMP = sb("IMP", [128, 16, 32], F32)
        IMR = sb("IMR", [128, 1], F32)
        POSB = sb("POSB", [128, 32], BF16)
        B1 = sb("B1", [128, 4], F32)
        W2 = sb("W2", [128, 2, 2, 128], BF16)
        GEL = sb("GEL", [128, 4, 128], BF16)
        DUM = sb("DUM", [128, 8], F32)
        ZLH = sb("ZLH", [128, 128], BF16)
        KC = sb("KC", [128, 2, 128], BF16)
        VC = sb("VC", [128, 128], BF16)
        PS = [es.enter_context(nc.psum_tensor("PS%d" % i, [128, 512], F32)) for i in range(8)]

        P = Prog(nc)

        def A(eng, fn, reads, writes):
            return P.add(eng, fn, reads=reads, writes=writes)

        for c in range(8):
            P.dma('sp', XT[:, c, :], xT_d[c * 128:(c + 1) * 128, :])
        P.dma('sp', PK[:, :], pk_d[:, :])
        A('pool', lambda e: e.memset(ONESM[:, :], 1.0 / 1024.0), [], [ONESM[:, :]])
        A('pool', lambda e: e.memset(ONES64[:, :], 1.0 / 64.0), [], [ONES64[:, :]])
        A('pool', lambda e: e.memset(ONE1[:, :], 1.0), [], [ONE1[:, :]])
        A('pool', lambda e: e.memset(EPSC[:, :], EPS), [], [EPSC[:, :]])
        A('pool', lambda e: e.memset(ONEC[:, :], 1.0), [], [ONEC[:, :]])
        A('pool', lambda e: e.memset(ZLH[:, :], 0.0), [], [ZLH[:, :]])
        A('pool', lambda e: e.memset(TINYC[:, :], 1e-30), [], [TINYC[:, :]])
        A('pool', lambda e: e.affine_select(out=IDENT[:, :], in_=ONE1[:, :], pattern=[[1, 128]], compare_op=ALU.is_equal,
                                            fill=0.0, base=0, channel_multiplier=-1), [ONE1[:, :]], [IDENT[:, :]])
        A('pool', lambda e: e.tensor_scalar(out=BIGI[:, :], in0=IDENT[:, :], scalar1=BIGM, scalar2=None, op0=ALU.mult),
          [IDENT[:, :]], [BIGI[:, :]])
        A('pool', lambda e: e.memset(NEGU[:, :], -1.0), [], [NEGU[:, :]])
        A('pool', lambda e: e.affine_select(out=NEGU[:, :], in_=NEGU[:, :], pattern=[[-1, 128]], compare_op=ALU.is_gt,
                                            fill=0.0, base=0, channel_multiplier=1), [NEGU[:, :]], [NEGU[:, :]])
        OHTMP = av(80 * KB, [SEQ], BF16)
        for (T, w) in ((OHTMP[0:32, :], 64),):
            A('pool', lambda e, T=T: e.memset(T, 1.0), [], [T])
            A('pool', lambda e, T=T, w=w: e.affine_select(out=T, in_=T, pattern=[[1, SEQ]],
                                                          compare_op=ALU.is_ge, fill=0.0, base=0, channel_multiplier=-w),
              [T], [T])
            A('pool', lambda e, T=T, w=w: e.affine_select(out=T, in_=T, pattern=[[-1, SEQ]],
                                                          compare_op=ALU.is_ge, fill=0.0, base=w - 1, channel_multiplier=w),
              [T], [T])
        A('act', lambda e: e.activation(out=OHS[64:96, :], in_=OHTMP[0:32, :], func=AF.Copy), [OHTMP[0:32, :]], [OHS[64:96, :]])
        A('dve', lambda e: e.memset(OHS[96:128, :], 0.0), [], [OHS[96:128, :]])
        A('pool', lambda e: e.memset(VC[:, 64:128], 1.0), [], [VC[:, 64:128]])
        A('pool', lambda e: e.memset(SELB[:, :, :], 1.0), [], [SELB[:, :, :]])
        A('pool', lambda e: e.affine_select(out=SELB[:, :, :], in_=SELB[:, :, :], pattern=[[-1, 12], [0, 64]],
                                            compare_op=ALU.is_equal, fill=0.0, base=0, channel_multiplier=1),
          [SELB[:, :, :]], [SELB[:, :, :]])
        P.dma('pool', COVER[:, :], cover_d[:, :])
        A('pool', lambda e: e.memset(ONER[:, :], 1.0), [], [ONER[:, :]])

        P.dma('sp', OHN[0:33, :], ohn_d[:, :], semkey='dma_ohn')
        P.dma('sp', OHW[0:33, :], ohw_d[:, :], semkey='dma_ohw')
        P.dma('sp', OHC[0:33, :], ohc_d[:, :], semkey='dma_ohc')
        A('pool', lambda e: e.memset(TB[32:33, :], NEGV), [], [TB[32:33, :]])
        for h in range(12):
            A('dve', lambda e, h=h: e.tensor_copy(out=TB[0:32, h * 128:(h + 1) * 128],
                                                  in_=PK[0:32, PK_TAB + h:PK_TAB + h + 1].to_broadcast([32, 128])),
              [PK[0:32, PK_TAB + h:PK_TAB + h + 1]], [TB[0:32, h * 128:(h + 1) * 128]])
        psr = [0]

        def nps():
            p = PS[psr[0] % 8]
            psr[0] += 1
            return p

        def gen_bias(h, OH, L, scr, sk, inv_scale, want_cb):
            for j in range(0, L, 512):
                n = min(512, L - j)
                ps = nps()
                A('pe', lambda e, ps=ps, j=j, n=n: e.matmul(ps[:, 0:n], lhsT=TB[0:33, h * 128:(h + 1) * 128],
                                                            rhs=OH[0:33, j:j + n], start=True, stop=True),
                  [TB[0:33, h * 128:(h + 1) * 128], OH[0:33, j:j + n]], [ps[:, 0:n]])
                A('act', lambda e, ps=ps, j=j, n=n: e.activation(out=RREP[:, j:j + n], in_=ps[:, 0:n], func=AF.Copy,
                                                                 scale=inv_scale),
                  [ps[:, 0:n]], [RREP[:, j:j + n]])
                if want_cb and j == 0:
                    A('dve', lambda e, ps=ps: e.tensor_copy(out=CBH[:, h:h + 1], in_=ps[:, 400:401]),
                      [ps[:, 400:401]], [CBH[:, h:h + 1]])
            dst = bass.AP(scr.tensor, 0, [[L + sk, 128], [1, L]])
            P.add('sp', lambda e, dst=dst: e.dma_start(out=dst, in_=RREP[:, 0:L]), reads=[RREP[:, 0:L]], writes=[scr[:]],
                  dma=True, semkey='dma_' + scr.tensor.name)

        for h in range(12):
            gen_bias(h, OHN, L_N, scr_n[h], 1, (1.0 if h < 8 else math.sqrt(32.0)), True)
        for h in range(4):
            gen_bias(4 + h, OHW, L_W, scr_w[h], 1, 1.0, False)
            gen_bias(4 + h, OHC, L_C, scr_c[h], 16, 1.0, False)

        def load_g(kind, h, dst, slot=0):
            if kind == 'n':
                src = bass.AP(scr_n[h].tensor, OFF_N, [[L_N, 128], [1, 640]])
                full = scr_n[h]
            elif kind == 'w':
                src = bass.AP(scr_w[h].tensor, OFF_W, [[L_W, 128], [1, 1024]])
                full = scr_w[h]
            else:
                src = bass.AP(scr_c[h].tensor, 2032, [[L_C, 128], [1, 2048]])
                full = scr_c[h]
            P.add('pool', lambda e: e.dma_start(out=dst, in_=src), reads=[full[:]], writes=[dst], dma=True,
                  semkey='dma_greg%d' % slot)

        sqi = [0]

        def rmsnorm_tile(tt, gcol0, dst_fn):
            ts = slice(tt * 512, (tt + 1) * 512)
            ps = nps()
            for c in range(8):
                sq = SQ[sqi[0] % 2]
                sqi[0] += 1
                A('act', lambda e, sq=sq, c=c: e.activation(out=sq[:, :], in_=XT[:, c, ts], func=AF.Square),
                  [XT[:, c, ts]], [sq[:, :]])
                A('pe', lambda e, sq=sq, c=c: e.matmul(ps[:, :], lhsT=ONESM[:, :], rhs=sq[:, :], start=(c == 0), stop=(c == 7)),
                  [ONESM[:, :], sq[:, :]], [ps[:, :]])
            A('act', lambda e: e.activation(out=RSTD[:, :], in_=ps[:, :], func=AF.Sqrt, bias=EPSC[:, :]),
              [ps[:, :], EPSC[:, :]], [RSTD[:, :]])
            A('dve', lambda e: e.reciprocal(out=RSTD[:, :], in_=RSTD[:, :]), [RSTD[:, :]], [RSTD[:, :]])
            for c in range(8):
                dst, post = dst_fn(c)
                A('dve', lambda e, dst=dst, c=c: e.scalar_tensor_tensor(
                    out=dst, in0=XT[:, c, ts], scalar=PK[:, gcol0 + c:gcol0 + c + 1], in1=RSTD[:, :],
                    op0=ALU.mult, op1=ALU.mult),
                  [XT[:, c, ts], PK[:, gcol0 + c:gcol0 + c + 1], RSTD[:, :]], [dst])
                if post:
                    post()

        def proj_fm(col0, ncols, dst_fn, scale=1.0):
            for tt in range(4):
                ts = slice(tt * 512, (tt + 1) * 512)
                ps = nps()
                for k in range(8):
                    A('pe', lambda e, ps=ps, k=k, ts=ts: e.matmul(ps[0:ncols, :], lhsT=WIN[:, k, col0:col0 + ncols],
                                                                  rhs=HT[:, k, ts], start=(k == 0), stop=(k == 7)),
                      [WIN[:, k, col0:col0 + ncols], HT[:, k, ts]], [ps[0:ncols, :]])
                dst = dst_fn(tt)
                A('act', lambda e, ps=ps, dst=dst: e.activation(out=dst, in_=ps[0:ncols, :], func=AF.Copy, scale=scale),
                  [ps[0:ncols, :]], [dst])

        def proj_tm(col0, ncols, h0):
            nh = ncols // 64
            for tb in range(16):
                ps = nps()
                for k in range(8):
                    A('pe', lambda e, ps=ps, k=k, tb=tb: e.matmul(ps[:, 0:ncols], lhsT=HT[:, k, tb * 128:(tb + 1) * 128],
                                                                  rhs=WIN[:, k, col0:col0 + ncols], start=(k == 0), stop=(k == 7)),
                      [HT[:, k, tb * 128:(tb + 1) * 128], WIN[:, k, col0:col0 + ncols]], [ps[:, 0:ncols]])
                A('dve', lambda e, ps=ps, tb=tb: e.tensor_copy(out=VT[:, tb, h0:h0 + nh, 0:64],
                                                               in_=ps[:, 0:ncols].rearrange("p (h d) -> p h d", h=nh)),
                  [ps[:, 0:ncols]], [VT[:, tb, h0:h0 + nh, 0:64]])

        KZ = [KT[:, 0, :], KT[:, 1, :], KX[:, 0, :], KX[:, 1, :]]

        def proj_k_padded(colbase):
            for hh in range(4):
                ob = 64 - (hh % 2) * 64
                A('dve', lambda e, hh=hh, ob=ob: e.memset(KZ[hh][ob:ob + 64, :], 0.0), [], [KZ[hh][ob:ob + 64, :]])
            for c in range(2):
                for tt in range(4):
                    ts = slice(tt * 512, (tt + 1) * 512)
                    ps = nps()
                    for k in range(8):
                        A('pe', lambda e, ps=ps, k=k, ts=ts, c=c: e.matmul(ps[:, :], lhsT=WIN[:, k, colbase + c * 128:colbase + (c + 1) * 128],
                                                                            rhs=HT[:, k, ts], start=(k == 0), stop=(k == 7)),
                          [WIN[:, k, colbase + c * 128:colbase + (c + 1) * 128], HT[:, k, ts]], [ps[:, :]])
                    A('act', lambda e, ps=ps, c=c, ts=ts: e.activation(out=KZ[2 * c][0:64, ts], in_=ps[0:64, :], func=AF.Copy),
                      [ps[0:64, :]], [KZ[2 * c][0:64, ts]])
                    A('dve', lambda e, ps=ps, c=c, ts=ts: e.tensor_copy(out=KZ[2 * c + 1][64:128, ts], in_=ps[64:128, :]),
                      [ps[64:128, :]], [KZ[2 * c + 1][64:128, ts]])

        def vones(tb, c0):
            return VT[:, tb, c0 // 64, :]

        class Pipe:
            def __init__(self):
                self.e1 = None
                self.p0 = None
                self.eps = []

            def defer(self, fn, n=3):
                self.eps.append([n, fn])

            def _tick(self):
                for it in self.eps:
                    it[0] -= 1
                while self.eps and self.eps[0][0] <= 0:
                    self.eps.pop(0)[1]()

            def push(self, s, e, pv):
                self._tick()
                s()
                if self.e1:
                    self.e1[0]()
                if self.p0:
                    self.p0()
                self.p0 = self.e1[1] if self.e1 else None
                self.e1 = (e, pv)

            def flush(self):
                if self.e1:
                    self.e1[0]()
                if self.p0:
                    self.p0()
                if self.e1:
                    self.e1[1]()
                self.e1 = None
                self.p0 = None
                while self.eps:
                    self.eps.pop(0)[1]()

        zi = [0]
        pti = [0]
        FILL = [0]
        EPDEF = [3]
        ZB = [PS[0], PS[1], PS[2]]
        OB = [PS[3], PS[4], PS[5]]
        misc = [0]

        MISC = [[PS[6], PS[7]]]

        def mps():
            lst = MISC[0]
            p = lst[misc[0] % len(lst)]
            misc[0] += 1
            return p

        def recip_act(dst, den):
            A('act', lambda e: e.activation(out=dst, in_=den, func=AF.Ln, bias=TINYC[64:128, :]), [den, TINYC[64:128, :]], [dst])
            A('act', lambda e: e.activation(out=dst, in_=dst, func=AF.Exp, scale=-1.0), [dst], [dst])

        def sm_tile(pipe, terms, q0, N, act_scale, cbias, v_lhsT, o_ps, first, last, nk=128, extra_pv=None, epilogue=None):
            z = ZB[zi[0] % 3]
            zi[0] += 1
            pt = PT[pti[0] % 3]
            pti[0] += 1
            zs = z[0:nk, q0:q0 + N]
            pts = pt[0:nk, q0:q0 + N]

            def s():
                nf = FILL[0]
                if nf:
                    A('pe', lambda e: e.matmul(z[:, 0:nf], lhsT=ONE1[:, :], rhs=HT[:, 0, 0:nf], start=True, stop=True),
                      [ONE1[:, :], HT[:, 0, 0:nf]], [z[:, 0:nf]])
                for i, (a, b) in enumerate(terms):
                    A('pe', lambda e, a=a, b=b, i=i: e.matmul(zs, lhsT=a, rhs=b, start=(i == 0), stop=(i == len(terms) - 1)),
                      [a, b], [zs])

            def ex():
                if cbias is None:
                    A('act', lambda e: e.activation(out=pts, in_=zs, func=AF.Exp, scale=act_scale), [zs], [pts])
                else:
                    A('act', lambda e: e.activation(out=pts, in_=zs, func=AF.Exp, scale=act_scale, bias=cbias),
                      [zs, cbias], [pts])

            def pv():
                M = v_lhsT.shape[-1] if len(v_lhsT.shape) == 2 else 128
                osl = o_ps[0:M, q0:q0 + N]
                A('pe', lambda e: e.matmul(osl, lhsT=v_lhsT, rhs=pts, start=first, stop=last), [v_lhsT, pts], [osl])
                if extra_pv:
                    extra_pv(pt)
                if epilogue:
                    pipe.defer(epilogue, EPDEF[0])
            pipe.push(s, ex, pv)

        def causal_blocks(qt):
            out = []
            for kb in range(4 * qt + 4):
                i = kb - 4 * qt
                if i <= 0:
                    out.append((kb, 0, 512, (4 * qt - kb) * 128))
                else:
                    out.append((kb, 128 * i, 512 - 128 * i, 0))
            return out

        def bias_terms(G, bh, q0, N, D):
            if D >= 256:
                return [], CBH[:, bh:bh + 1]
            return [(IDENT[:, :], G[:, D:D + N])], None

        def write_mix(dst, src, scale=1.0):
            A('act', lambda e: e.activation(out=dst, in_=src, func=AF.Copy, scale=scale), [src], [dst])

        def apply_wout(l, mixer):
            for c in range(8):
                for tt in range(4):
                    ts = slice(tt * 512, (tt + 1) * 512)
                    ps = nps()
                    for m in range(2):
                        A('pe', lambda e, ps=ps, m=m, c=c, ts=ts: e.matmul(ps[:, :], lhsT=WO[:, m, c * 128:(c + 1) * 128],
                                                                            rhs=MIXM[:, m, ts], start=(m == 0), stop=(m == 1)),
                          [WO[:, m, c * 128:(c + 1) * 128], MIXM[:, m, ts]], [ps[:, :]])
                    A('dve', lambda e, ps=ps, c=c, ts=ts: e.tensor_tensor(out=XT[:, c, ts], in0=XT[:, c, ts], in1=ps[:, :], op=ALU.add),
                      [XT[:, c, ts], ps[:, :]], [XT[:, c, ts]])

        def mix_dst(h, ts):
            return MIXM[(h % 2) * 64:(h % 2) * 64 + 64, h // 2, ts]

        def mixer_diff(l):
            lam_init = 0.8 - 0.6 * math.exp(-0.3 * l)
            lv = PK[0:1, PK_LAM + l * 128:PK_LAM + (l + 1) * 128]
            A('dve', lambda e: e.tensor_tensor(out=LTMP[:, 0:32], in0=PK[0:1, PK_LAM + l * 128:PK_LAM + l * 128 + 32],
                                               in1=PK[0:1, PK_LAM + l * 128 + 32:PK_LAM + l * 128 + 64], op=ALU.mult),
              [lv], [LTMP[:, 0:32]])
            A('dve', lambda e: e.tensor_tensor(out=LTMP[:, 32:64], in0=PK[0:1, PK_LAM + l * 128 + 64:PK_LAM + l * 128 + 96],
                                               in1=PK[0:1, PK_LAM + l * 128 + 96:PK_LAM + l * 128 + 128], op=ALU.mult),
              [lv], [LTMP[:, 32:64]])
            A('dve', lambda e: e.reduce_sum(out=LTMP[:, 64:66], in_=LTMP[:, 0:64].rearrange("p (a b) -> p a b", a=2), axis=AX.X),
              [LTMP[:, 0:64]], [LTMP[:, 64:66]])
            A('act', lambda e: e.activation(out=LTMP[:, 66:68], in_=LTMP[:, 64:66], func=AF.Exp), [LTMP[:, 64:66]], [LTMP[:, 66:68]])
            A('dve', lambda e: e.scalar_tensor_tensor(out=LTMP[:, 68:69], in0=LTMP[:, 67:68], scalar=-lam_init, in1=LTMP[:, 66:67],
                                                      op0=ALU.add, op1=ALU.subtract),
              [LTMP[:, 66:68]], [LTMP[:, 68:69]])
            ps = mps()
            A('pe', lambda e: e.matmul(ps[0:64, 0:1], lhsT=ONER[0:1, 0:64], rhs=LTMP[0:1, 68:69], start=True, stop=True),
              [ONER[0:1, 0:64], LTMP[0:1, 68:69]], [ps[0:64, 0:1]])
            A('dve', lambda e: e.tensor_copy(out=LAMC[:, l:l + 1], in_=ps[0:64, 0:1]), [ps[0:64, 0:1]], [LAMC[:, l:l + 1]])
            if stop <= 1:
                return
            if stop <= 2:
                return
            A('dve', lambda e: e.memset(KT[:, :, :], 0.0), [], [KT[:, :, :]])
            for c in range(2):
                for tt in range(4):
                    ts = slice(tt * 512, (tt + 1) * 512)
                    ps = nps()
                    for k in range(8):
                        A('pe', lambda e, ps=ps, k=k, ts=ts, c=c: e.matmul(ps[:, :], lhsT=WIN[:, k, 256 + c * 128:256 + (c + 1) * 128],
                                                                            rhs=HT[:, k, ts], start=(k == 0), stop=(k == 7)),
                          [WIN[:, k, 256 + c * 128:256 + (c + 1) * 128], HT[:, k, ts]], [ps[:, :]])
                    for b0 in (0, 64):
                        A('act', lambda e, ps=ps, b0=b0, c=c, ts=ts: e.activation(out=KT[b0:b0 + 32, c, ts], in_=ps[b0:b0 + 32, :], func=AF.Copy),
                          [ps[b0:b0 + 32, :]], [KT[b0:b0 + 32, c, ts]])
                    A('dve', lambda e, ps=ps, c=c, ts=ts: e.tensor_copy(out=KX[:, c, ts], in_=ps[:, :]), [ps[:, :]], [KX[:, c, ts]])
                    for b0 in (0, 64):
                        A('dve', lambda e, b0=b0, c=c, ts=ts: e.memset(KX[b0:b0 + 32, c, ts], 0.0), [], [KX[b0:b0 + 32, c, ts]])
            if stop <= 3:
                return
            proj_tm(512, 256, 0)
            QW = av(72 * KB, [2, SEQ], BF16)
            QZ = [QT[:, 0, :], QT[:, 1, :], QW[:, 0, :], QW[:, 1, :]]
            for c in range(2):
                psl = []
                for tt in range(4):
                    ts = slice(tt * 512, (tt + 1) * 512)
                    ps = nps()
                    psl.append(ps)
                    for k in range(8):
                        A('pe', lambda e, ps=ps, k=k, ts=ts, c=c: e.matmul(ps[:, :], lhsT=WIN[:, k, c * 128:(c + 1) * 128], rhs=HT[:, k, ts],
                                                                            start=(k == 0), stop=(k == 7)),
                          [WIN[:, k, c * 128:(c + 1) * 128], HT[:, k, ts]], [ps[:, :]])
                for tt in range(4):
                    ts = slice(tt * 512, (tt + 1) * 512)
                    ps = psl[tt]
                    A('act', lambda e, ps=ps, ts=ts, c=c: e.activation(out=QZ[2 * c][0:64, ts], in_=ps[0:64, :], func=AF.Copy),
                      [ps[0:64, :]], [QZ[2 * c][0:64, ts]])
                    A('dve', lambda e, ps=ps, ts=ts, c=c: e.tensor_copy(out=QZ[2 * c + 1][64:128, ts], in_=ps[64:128, :]),
                      [ps[64:128, :]], [QZ[2 * c + 1][64:128, ts]])
            for hh in range(4):
                ob = 64 - (hh % 2) * 64
                A('dve', lambda e, hh=hh, ob=ob: e.memset(QZ[hh][ob:ob + 64, :], 0.0), [], [QZ[hh][ob:ob + 64, :]])
            after_proj()
            if stop <= 4:
                return
            G = [av(GREG + i * 1280, [640], BF16) for i in range(4)]
            for h in range(4):
                load_g('n', 8 + h, G[h][:, :], slot=h)
            if stop <= 5:
                return
            P.phase = P.phase.split('/')[0] + '/attn'
            MISC[0] = [PS[7]]
            EPDEF[0] = 1
            FILL[0] = 512 if filler else 0
            pipe = Pipe()
            sc = 1.0 / math.sqrt(32.0)
            for h in range(4):
                b0 = (h % 2) * 64
                for qt in range(4):
                    t0 = qt * 512
                    blocks = causal_blocks(qt)
                    gi = h * 4 + qt
                    O1, O2 = (PS[3], PS[4]) if gi % 2 == 0 else (PS[5], PS[6])
                    for half, (Kt, O) in enumerate(((KT, O1), (KX, O2))):
                        for bi, (kb, q0, N, D) in enumerate(blocks):
                            kT = Kt[:, h // 2, kb * 128:(kb + 1) * 128]
                            qT = QZ[h][:, t0 + q0:t0 + q0 + N]
                            ext, cb = bias_terms(G[h], 8 + h, q0, N, D)
                            last = (bi == len(blocks) - 1)
                            ep = None
                            if last and half == 1:
                                def ep(h=h, qt=qt, O1=O1, O2=O2):
                                    ts = slice(qt * 512, (qt + 1) * 512)
                                    r1, r2, o1, o2 = FT[0], FT[1], FT[2], FT[3]
                                    A('dve', lambda e: e.reciprocal(out=r1[0:64, :], in_=O1[64:128, :]), [O1[64:128, :]], [r1[0:64, :]])
                                    A('dve', lambda e: e.tensor_tensor(out=o1[0:64, :], in0=O1[0:64, :], in1=r1[0:64, :], op=ALU.mult),
                                      [O1[0:64, :], r1[0:64, :]], [o1[0:64, :]])
                                    recip_act(r2[0:64, :], O2[64:128, :])
                                    A('dve', lambda e: e.tensor_tensor(out=o2[0:64, :], in0=O2[0:64, :], in1=r2[0:64, :], op=ALU.mult),
                                      [O2[0:64, :], r2[0:64, :]], [o2[0:64, :]])
                                    A('dve', lambda e: e.scalar_tensor_tensor(out=o1[0:64, :], in0=o2[0:64, :], scalar=LAMC[:, l:l + 1],
                                                                              in1=o1[0:64, :], op0=ALU.mult, op1=ALU.add),
                                      [o2[0:64, :], LAMC[:, l:l + 1], o1[0:64, :]], [o1[0:64, :]])
                                    sq = SQ[sqi[0] % 2]
                                    sqi[0] += 1
                                    A('pool', lambda e: e.tensor_tensor(out=sq[0:64, :], in0=o1[0:64, :], in1=o1[0:64, :], op=ALU.mult), [o1[0:64, :]], [sq[0:64, :]])

                                    def ep2():
                                        ps = mps()
                                        A('pe', lambda e: e.matmul(ps[0:64, :], lhsT=ONES64[0:64, :], rhs=sq[0:64, :], start=True, stop=True),
                                          [ONES64[0:64, :], sq[0:64, :]], [ps[0:64, :]])
                                        A('act', lambda e: e.activation(out=r1[0:64, :], in_=ps[0:64, :], func=AF.Ln, bias=EPSC[0:64, :]),
                                          [ps[0:64, :], EPSC[0:64, :]], [r1[0:64, :]])
                                        A('act', lambda e: e.activation(out=r1[0:64, :], in_=r1[0:64, :], func=AF.Exp, scale=-0.5), [r1[0:64, :]], [r1[0:64, :]])
                                        A('dve', lambda e: e.scalar_tensor_tensor(out=o2[0:64, :], in0=o1[0:64, :], scalar=PK[0:64, PK_SUBLN + l:PK_SUBLN + l + 1],
                                                                                  in1=r1[0:64, :], op0=ALU.mult, op1=ALU.mult),
                                          [o1[0:64, :], PK[0:64, PK_SUBLN + l:PK_SUBLN + l + 1], r1[0:64, :]], [o2[0:64, :]])
                                        write_mix(mix_dst(h, ts), o2[0:64, :], scale=(1.0 - lam_init))
                                    pipe.defer(ep2, 9)
                            sm_tile(pipe, [(kT, qT)] + ext, q0, N, sc, cb, vones(kb, h * 64), O, bi == 0, last, epilogue=ep)
            pipe.flush()
            FILL[0] = 0
            MISC[0] = [PS[6], PS[7]]

        def mixer_moba(l):
            for c in range(2):
                proj_fm(c * 128, 128, lambda tt, c=c: QT[:, c, tt * 512:(tt + 1) * 512], scale=0.125)
            proj_k_padded(256)
            A('dve', lambda e: e.memset(MBT[96:128, :], 0.0), [], [MBT[96:128, :]])
            for hh in range(4):
                bb = (hh % 2) * 64
                A('dve', lambda e, hh=hh, bb=bb: e.reduce_sum(out=KM[bb:bb + 64, hh // 2, :],
                                                               in_=KZ[hh][bb:bb + 64, :].rearrange("p (b s) -> p b s", b=8), axis=AX.X),
                  [KZ[hh][bb:bb + 64, :]], [KM[bb:bb + 64, hh // 2, :]])
            for c in range(2):
                A('act', lambda e, c=c: e.activation(out=KMB[:, c, :], in_=KM[:, c, :], func=AF.Copy, scale=1.0 / 256.0),
                  [KM[:, c, :]], [KMB[:, c, :]])
            G = [av(GREG + i * 1280, [640], BF16) for i in range(4)]
            for h in range(4):
                load_g('n', h, G[h][:, :], slot=h)
            P.phase = P.phase.split('/')[0] + '/attn'
            EPDEF[0] = 3
            pipe = Pipe()
            gsets = [(MB8, MB, GATE, TOP8)] + GS_EXTRA
            gcnt = [0]

            def gating(h, tb):
                b0 = (h % 2) * 64
                own = tb // 2
                mb8, mb, gate, top8 = gsets[gcnt[0] % 4]
                gcnt[0] += 1
                tsl = slice(tb * 128, (tb + 1) * 128)
                msl = slice((h % 2) * 1024 + tb * 128 - 1024, (h % 2) * 1024 + (tb + 1) * 128 - 1024)
                A('pool', lambda e: e.memset(mb8[:, 0:8], -1.0), [], [mb8[:, 0:8]])
                A('pool', lambda e: e.memset(mb8[:, own:own + 1], 0.0), [], [mb8[:, own:own + 1]])
                ps = mps()
                A('pe', lambda e: e.matmul(ps[:, 0:8], lhsT=QT[b0:b0 + 64, h // 2, tsl], rhs=KMB[b0:b0 + 64, h // 2, :], start=True, stop=True),
                  [QT[b0:b0 + 64, h // 2, tsl], KMB[b0:b0 + 64, h // 2, :]], [ps[:, 0:8]])
                A('pool', lambda e: e.memset(gate[:, 0:8], -1e30), [], [gate[:, 0:8]])
                A('dve', lambda e: e.tensor_copy(out=gate[:, 0:own], in_=ps[:, 0:own]), [ps[:, 0:own]], [gate[:, 0:own]])
                A('dve', lambda e: e.max(out=top8[:, :], in_=gate[:, 0:8]), [gate[:, 0:8]], [top8[:, :]])
                A('dve', lambda e: e.tensor_scalar(out=mb8[:, 0:own], in0=gate[:, 0:own], scalar1=top8[:, 2:3], scalar2=-1.0,
                                                   op0=ALU.is_ge, op1=ALU.add),
                  [gate[:, 0:own], top8[:, 2:3]], [mb8[:, 0:own]])
                for r4 in range(4):
                    A('pool', lambda e, r4=r4: e.tensor_copy(out=mb[:, r4:32:4], in_=mb8[:, 0:8]), [mb8[:, 0:8]], [mb[:, r4:32:4]])

                def part2():
                    ps2 = mps()
                    A('pe', lambda e: e.matmul(ps2[0:32, 0:128], lhsT=mb[:, 0:32], rhs=BIGI[:, :], start=True, stop=True),
                      [mb[:, 0:32], BIGI[:, :]], [ps2[0:32, 0:128]])
                    A('dve', lambda e: e.tensor_copy(out=MBT[64:96, msl], in_=ps2[0:32, 0:128]), [ps2[0:32, 0:128]], [MBT[64:96, msl]])
                return part2

            def qcopy(hh):
                bb = (hh % 2) * 64
                dst = MBT[0:64, (hh % 2) * 1024:(hh % 2 + 1) * 1024]
                src = QT[bb:bb + 64, hh // 2, 1024:2048]
                A('dve', lambda e: e.tensor_copy(out=dst, in_=src), [src], [dst])

            def kcopy(hh):
                bb = (hh % 2) * 64
                src = KZ[hh][bb:bb + 64, :]
                A('dve', lambda e: e.tensor_copy(out=OHS[0:64, :], in_=src), [src], [OHS[0:64, :]])
            ogc = [0]
            qcopy(0)
            kcopy(0)
            pending = [(0, tb) for tb in range(8, 16)]
            p2s = []
            for (hh, tb) in pending:
                p2s.append(gating(hh, tb))
                if len(p2s) > 2:
                    p2s.pop(0)()
            while p2s:
                p2s.pop(0)()
            proj_tm(512, 256, 0)
            after_proj()
            for h in range(4):
                b0 = (h % 2) * 64
                pending = [(h + 1, tb) for tb in range(8, 16)] if h < 3 else []
                tcount = 0
                if h < 3:
                    qcopy(h + 1)
                for qt in (2, 3, 0, 1):
                    if qt == 0 and h < 3:
                        kcopy(h + 1)
                    t0 = qt * 512
                    blocks = causal_blocks(qt)
                    O = OB[ogc[0] % 3]
                    ogc[0] += 1
                    for bi, (kb, q0, N, D) in enumerate(blocks):
                        kT = KZ[h][:, kb * 128:(kb + 1) * 128]
                        qT = QT[:, h // 2, t0 + q0:t0 + q0 + N]
                        ext, cb = bias_terms(G[h], h, q0, N, D)
                        if qt >= 2:
                            m0 = (h % 2) * 1024 + t0 + q0 - 1024
                            kT = OHS[:, kb * 128:(kb + 1) * 128]
                            qT = MBT[:, m0:m0 + N]
                        last = (bi == len(blocks) - 1)
                        ep = None
                        if last:
                            def ep(h=h, qt=qt, O=O):
                                ts = slice(qt * 512, (qt + 1) * 512)
                                r1, o1 = FT[0], FT[2]
                                A('dve', lambda e: e.reciprocal(out=r1[0:64, :], in_=O[64:128, :]), [O[64:128, :]], [r1[0:64, :]])
                                A('dve', lambda e: e.tensor_tensor(out=o1[0:64, :], in0=O[0:64, :], in1=r1[0:64, :], op=ALU.mult),
                                  [O[0:64, :], r1[0:64, :]], [o1[0:64, :]])
                                write_mix(mix_dst(h, ts), o1[0:64, :])
                        sm_tile(pipe, [(kT, qT)] + ext, q0, N, 1.0, cb, vones(kb, h * 64), O, bi == 0, last, epilogue=ep)
                        tcount += 1
                        if tcount % 2 == 1 and (len(p2s) > 2 or (p2s and not pending)):
                            p2s.pop(0)()
                        if pending and tcount % 2 == 0:
                            hh, tb = pending.pop(0)
                            p2s.append(gating(hh, tb))
                while pending or p2s:
                    if pending:
                        hh, tb = pending.pop(0)
                        p2s.append(gating(hh, tb))
                    if p2s:
                        p2s.pop(0)()
            pipe.flush()
            FILL[0] = 0

        def mixer_sb(l):
            for c in range(2):
                proj_fm(c * 128, 128, lambda tt, c=c: QT[:, c, tt * 512:(tt + 1) * 512], scale=0.125)
            proj_k_padded(256)
            proj_tm(512, 256, 0)
            after_proj()
            P.phase = P.phase.split('/')[0] + '/attn'
            ZA = [PS[0], PS[1], PS[7]]
            WB = [PS[2], PS[3]]
            CBK = PS[4]
            OBK = [PS[5], PS[6]]
            SPB3 = [SPB[0], SPB[1], SQ[0], SQ[1]]
            FTS = [(FT[0], FT[1]), (FT[2], FT[3]), (MBT[:, 0:1024].bitcast(F32), MBT[:, 1024:2048].bitcast(F32))]
            blks = []
            for h in range(4):
                for qt in range(4):
                    blocks = list(reversed(causal_blocks(qt)))
                    for bi, (kb, q0, N, D) in enumerate(blocks):
                        blks.append((h, qt, bi, len(blocks), kb, q0, N))
            stA, stB1, stC1, stB2, stC2 = [], [], [], [], []
            for i, (h, qt, bi, nb, kb, q0, N) in enumerate(blks):
                b0 = (h % 2) * 64
                t0 = qt * 512
                O = OBK[(h * 4 + qt) % 2]
                diag = (kb >= 4 * qt)
                first = (bi == 0)
                last = (bi == nb - 1)
                za = ZA[i % 3][:, q0:q0 + N]
                wb = WB[i % 2][:, q0:q0 + N]
                cb = CBK[:, q0:q0 + N]
                t1t, t2t = FTS[i % 3]
                t1 = t1t[:, q0:q0 + N]
                t2 = t2t[:, q0:q0 + N]
                arg = t1
                spb_t = SPB3[i % 4]
                spb = spb_t[:, q0:q0 + N]
                pt_t = PT[i % 3]
                pts = pt_t[:, q0:q0 + N]
                kT = KZ[h][:, kb * 128:(kb + 1) * 128]
                qT = QT[:, h // 2, t0 + q0:t0 + q0 + N]

                def fA(za=za, kT=kT, qT=qT, t1=t1, t2=t2, diag=diag, spb_t=spb_t, t2t=t2t, q0=q0, N=N, spb=spb):
                    A('pe', lambda e: e.matmul(za, lhsT=kT, rhs=qT, start=True, stop=True), [kT, qT], [za])
                    A('act', lambda e: e.activation(out=t1, in_=za, func=AF.Exp), [za], [t1])
                    A('act', lambda e: e.activation(out=t2, in_=t1, func=AF.Ln, bias=ONEC[:, :]), [t1, ONEC[:, :]], [t2])
                    if diag:
                        d0 = spb_t[:, q0:q0 + 128]
                        s0 = t2t[:, q0:q0 + 128]
                        A('pool', lambda e: e.affine_select(out=d0, in_=s0, pattern=[[1, 128]], compare_op=ALU.is_gt,
                                                            fill=0.0, base=0, channel_multiplier=-1), [s0], [d0])
                        if N > 128:
                            d1 = spb_t[:, q0 + 128:q0 + N]
                            s1 = t2t[:, q0 + 128:q0 + N]
                            A('pool', lambda e: e.tensor_copy(out=d1, in_=s1), [s1], [d1])
                    else:
                        A('pool', lambda e: e.tensor_copy(out=spb, in_=t2), [t2], [spb])

                def fB1(wb=wb, za=za, spb=spb, t2=t2, arg=arg):
                    A('pe', lambda e: e.matmul(wb, lhsT=NEGU[:, :], rhs=spb, start=True, stop=True), [NEGU[:, :], spb], [wb])
                    A('dve', lambda e: e.tensor_tensor(out=arg, in0=za, in1=t2, op=ALU.subtract), [za, t2], [arg])
                    A('dve', lambda e: e.tensor_tensor(out=arg, in0=arg, in1=wb, op=ALU.add), [arg, wb], [arg])

                def fC1(cb=cb, spb=spb, first=first, last=last):
                    if first:
                        A('pe', lambda e: e.matmul(CBK[:, :], lhsT=ZLH[:, :], rhs=HT[:, 0, 0:512], start=True, stop=True, skip_group_check=True),
                          [ZLH[:, :], HT[:, 0, 0:512]], [CBK[:, :]])
                    if not last:
                        A('pe', lambda e: e.matmul(cb, lhsT=ONE1[:, :], rhs=spb, start=False, stop=True, skip_group_check=True), [ONE1[:, :], spb], [cb])

                def fB2(diag=diag, q0=q0, first=first, t1t=t1t, arg=arg, pts=pts, pt_t=pt_t):
                    c0 = q0 + 128 if diag else q0
                    if not first and c0 < 512:
                        cbs = CBK[:, c0:512]
                        args = t1t[:, c0:512]
                        A('dve', lambda e: e.tensor_tensor(out=args, in0=args, in1=cbs, op=ALU.subtract), [args, cbs], [args])
                    A('act', lambda e: e.activation(out=pts, in_=arg, func=AF.Exp), [arg], [pts])
                    if diag:
                        a0 = pt_t[:, q0:q0 + 128]
                        A('pool', lambda e: e.affine_select(out=a0, in_=a0, pattern=[[1, 128]], compare_op=ALU.is_gt,
                                                            fill=0.0, base=0, channel_multiplier=-1), [a0], [a0])

                def fC2(O=O, q0=q0, N=N, kb=kb, h=h, qt=qt, pts=pts, first=first, last=last):
                    vl = VT[:, kb, h, :]
                    osl = O[:, q0:q0 + N]
                    if first:
                        A('pe', lambda e: e.matmul(O[:, :], lhsT=ZLH[:, :], rhs=HT[:, 0, 0:512], start=True, stop=True, skip_group_check=True),
                          [ZLH[:, :], HT[:, 0, 0:512]], [O[:, :]])
                    A('pe', lambda e: e.matmul(osl, lhsT=vl, rhs=pts, start=False, stop=True, skip_group_check=True), [vl, pts], [osl])
                    if last:
                        ts = slice(qt * 512, (qt + 1) * 512)
                        write_mix(mix_dst(h, ts), O[0:64, :])
                stA.append(fA)
                stB1.append(fB1)
                stC1.append(fC1)
                stB2.append(fB2)
                stC2.append(fC2)
            nblk = len(blks)
            for step in range(-3, nblk):
                if 0 <= step + 3 < nblk:
                    stA[step + 3]()
                if 0 <= step + 1 < nblk:
                    stB1[step + 1]()
                if 0 <= step < nblk:
                    stC1[step]()
                if 0 <= step + 1 < nblk:
                    stB2[step + 1]()
                if 0 <= step < nblk:
                    stC2[step]()

        def mixer_nsa(l):
            for c in range(2):
                proj_fm(c * 128, 128, lambda tt, c=c: QT[:, c, tt * 512:(tt + 1) * 512], scale=0.125)
            proj_fm(256, 128, lambda tt: KX[:, 0, tt * 512:(tt + 1) * 512])
            proj_fm(384, 64, lambda tt: OHS[0:64, tt * 512:(tt + 1) * 512])
            A('dve', lambda e: e.memset(KT[64:128, 0, :], 0.0), [], [KT[64:128, 0, :]])
            A('dve', lambda e: e.memset(KX[0:64, 1, :], 0.0), [], [KX[0:64, 1, :]])
            A('dve', lambda e: e.memset(MBT[96:128, :], 0.0), [], [MBT[96:128, :]])
            A('dve', lambda e: e.memset(KC[:, :, :], 0.0), [], [KC[:, :, :]])
            for tt in range(4):
                ts = slice(tt * 512, (tt + 1) * 512)
                ps = nps()
                for k in range(8):
                    A('pe', lambda e, ps=ps, k=k, ts=ts: e.matmul(ps[:, :], lhsT=WIN[:, k, 512:640], rhs=HT[:, k, ts], start=(k == 0), stop=(k == 7)),
                      [WIN[:, k, 512:640], HT[:, k, ts]], [ps[:, :]])
                A('act', lambda e, ps=ps, ts=ts: e.activation(out=KT[0:64, 0, ts], in_=ps[0:64, :], func=AF.Copy), [ps[0:64, :]], [KT[0:64, 0, ts]])
                A('dve', lambda e, ps=ps, ts=ts: e.tensor_copy(out=KX[64:128, 1, ts], in_=ps[64:128, :]), [ps[64:128, :]], [KX[64:128, 1, ts]])
            SGB = KT[0:12, 1, :]
            for tt in range(4):
                ts = slice(tt * 512, (tt + 1) * 512)
                ps = nps()
                for k in range(8):
                    A('pe', lambda e, ps=ps, k=k, ts=ts: e.matmul(ps[0:12, :], lhsT=WIN[:, k, 640:652], rhs=HT[:, k, ts],
                                                                  start=(k == 0), stop=(k == 7)),
                      [WIN[:, k, 640:652], HT[:, k, ts]], [ps[0:12, :]])
                A('act', lambda e, ps=ps, ts=ts: e.activation(out=SGB[:, ts], in_=ps[0:12, :], func=AF.Sigmoid), [ps[0:12, :]], [SGB[:, ts]])
            proj_tm(652, 128, 0)
            P.phase = P.phase.split('/')[0] + '/cmp'
            P.dma('pool', W1[0:64, :, :], ck1_d[l].rearrange("(l d) h -> d l h", d=64), semkey='dma_w1')
            P.dma('pool', W1[64:128, :, :], cv1_d[l].rearrange("(l d) h -> d l h", d=64), semkey='dma_w1')
            for kv, w2d in enumerate((ck2_d, cv2_d)):
                for dup in range(2):
                    P.dma('pool', W2[:, kv, :, dup * 64:(dup + 1) * 64], w2d[l].rearrange("(a p) d -> p a d", p=128), semkey='dma_w2')
            A('dve', lambda e: e.tensor_copy(out=POSB[:, :], in_=PK[:, PK_POS + l * 32:PK_POS + (l + 1) * 32]),
              [PK[:, PK_POS + l * 32:PK_POS + (l + 1) * 32]], [POSB[:, :]])
            for kv in range(2):
                b0 = 64 * kv
                for half in range(2):
                    ps = nps()
                    ps2 = nps()
                    for li in range(32):
                        lw = W1[b0:b0 + 64, li, half * 128:(half + 1) * 128]
                        xs = KX[b0:b0 + 64, 0, li:li + 16 * 126 + 1:16]
                        A('pe', lambda e, ps=ps, lw=lw, xs=xs, li=li: e.matmul(ps[:, 0:127], lhsT=lw, rhs=xs, start=(li == 0), stop=(li == 31)),
                          [lw, xs], [ps[:, 0:127]])
                    for li in range(32):
                        lw = W1[b0:b0 + 64, li, half * 128:(half + 1) * 128]
                        pb = POSB[b0:b0 + 64, li:li + 1]
                        A('pe', lambda e, ps2=ps2, lw=lw, pb=pb, li=li: e.matmul(ps2[:, 0:1], lhsT=lw, rhs=pb, start=(li == 0), stop=(li == 31)),
                          [lw, pb], [ps2[:, 0:1]])
                    bc = B1[:, kv * 2 + half:kv * 2 + half + 1]
                    A('dve', lambda e, ps2=ps2, bc=bc: e.tensor_copy(out=bc, in_=ps2[:, 0:1]), [ps2[:, 0:1]], [bc])
                    x, u, v = FT[0][:, 0:127], FT[1][:, 0:127], FT[2][:, 0:127]
                    A('act', lambda e, ps=ps, bc=bc, x=x: e.activation(out=x, in_=ps[:, 0:127], func=AF.Identity, bias=bc), [ps[:, 0:127], bc], [x])
                    A('dve', lambda e, x=x, u=u: e.tensor_tensor(out=u, in0=x, in1=x, op=ALU.mult), [x], [u])
                    A('dve', lambda e, u=u: e.tensor_scalar(out=u, in0=u, scalar1=0.044715, scalar2=1.0, op0=ALU.mult, op1=ALU.add), [u], [u])
                    A('dve', lambda e, x=x, u=u, v=v: e.tensor_tensor(out=v, in0=u, in1=x, op=ALU.mult), [u, x], [v])
                    A('act', lambda e, v=v, u=u: e.activation(out=u, in_=v, func=AF.Tanh, scale=GELU_C), [v], [u])
                    A('dve', lambda e, u=u: e.tensor_scalar(out=u, in0=u, scalar1=1.0, scalar2=0.5, op0=ALU.add, op1=ALU.mult), [u], [u])
                    gl = GEL[:, kv * 2 + half, 0:127]
                    A('dve', lambda e, x=x, u=u, gl=gl: e.tensor_tensor(out=gl, in0=u, in1=x, op=ALU.mult), [u, x], [gl])
            ps = nps()
            for half in range(2):
                A('pe', lambda e, ps=ps, half=half: e.matmul(ps[:, 0:127], lhsT=W2[:, 0, half, :], rhs=GEL[:, half, 0:127], start=(half == 0), stop=(half == 1)),
                  [W2[:, 0, half, :], GEL[:, half, 0:127]], [ps[:, 0:127]])
            A('act', lambda e, ps=ps: e.activation(out=KC[0:64, 0, 0:127], in_=ps[0:64, 0:127], func=AF.Copy), [ps[0:64, 0:127]], [KC[0:64, 0, 0:127]])
            A('dve', lambda e, ps=ps: e.tensor_copy(out=KC[64:128, 1, 0:127], in_=ps[64:128, 0:127]), [ps[64:128, 0:127]], [KC[64:128, 1, 0:127]])
            ps = nps()
            for half in range(2):
                A('pe', lambda e, ps=ps, half=half: e.matmul(ps[0:127, 0:64], lhsT=GEL[:, 2 + half, 0:127], rhs=W2[:, 1, half, 0:64], start=(half == 0), stop=(half == 1)),
                  [GEL[:, 2 + half, 0:127], W2[:, 1, half, 0:64]], [ps[0:127, 0:64]])
            A('dve', lambda e, ps=ps: e.tensor_copy(out=VC[0:127, 0:64], in_=ps[0:127, 0:64]), [ps[0:127, 0:64]], [VC[0:127, 0:64]])
            after_proj()
            P.dma('pool', WO[:, :, :], w_out_d[l, 512:768, :].rearrange("(m p) c -> p m c", p=128), semkey='dma_wo')
            GC = av(GREG, [SEQ], BF16)
            GWs = [av(GREG + i * 2048, [1024], BF16) for i in range(2)]
            GNs = [av(GREG + 4096 + i * 1280, [640], BF16) for i in range(2)]
            IMPB = IMP[:, :, :].rearrange("p a b -> p (a b)").bitcast(BF16)
            GCs = [IMPB[:, i * 512:(i + 1) * 512] for i in range(2)]

            def cmp_terms(h, ts, gc=None):
                b0 = (h % 2) * 64
                g = GC[0:127, ts] if gc is None else gc[0:127, :]
                return [(KC[:, h % 2, 0:127], QT[:, h // 2, ts]), (IDENT[0:127, 0:127], g)]
            P.phase = P.phase.split('/')[0] + '/pass1'
            for h in range(4):
                load_g('c', h, GC[:, :], slot=0)
                for qt in range(4):
                    ts = slice(qt * 512, (qt + 1) * 512)
                    z = ZB[zi[0] % 3]
                    zi[0] += 1
                    pt = PT[pti[0] % 3]
                    pti[0] += 1
                    terms = cmp_terms(h, ts)
                    for i, (a, bb) in enumerate(terms):
                        A('pe', lambda e, a=a, bb=bb, i=i, z=z: e.matmul(z[0:127, :], lhsT=a, rhs=bb, start=(i == 0), stop=(i == 1)), [a, bb], [z[0:127, :]])
                    A('act', lambda e, z=z, pt=pt: e.activation(out=pt[0:127, :], in_=z[0:127, :], func=AF.Exp), [z[0:127, :]], [pt[0:127, :]])
                    for t4 in range(4):
                        tb = qt * 4 + t4
                        ps = mps()
                        A('pe', lambda e, ps=ps, pt=pt, t4=t4: e.matmul(ps[:, 0:33], lhsT=pt[0:127, t4 * 128:(t4 + 1) * 128], rhs=COVER[0:127, 0:33], start=True, stop=True),
                          [pt[0:127, t4 * 128:(t4 + 1) * 128], COVER[0:127, 0:33]], [ps[:, 0:33]])
                        A('dve', lambda e, ps=ps: e.tensor_scalar(out=IMR[:, :], in0=ps[:, 32:33], scalar1=1e-30, scalar2=None, op0=ALU.max), [ps[:, 32:33]], [IMR[:, :]])
                        A('dve', lambda e: e.reciprocal(out=IMR[:, :], in_=IMR[:, :]), [IMR[:, :]], [IMR[:, :]])
                        if h == 0:
                            A('dve', lambda e, ps=ps, tb=tb: e.tensor_scalar(out=IMP[:, tb, :], in0=ps[:, 0:32], scalar1=IMR[:, 0:1], scalar2=None, op0=ALU.mult),
                              [ps[:, 0:32], IMR[:, :]], [IMP[:, tb, :]])
                        else:
                            A('dve', lambda e, ps=ps, tb=tb: e.scalar_tensor_tensor(out=IMP[:, tb, :], in0=ps[:, 0:32], scalar=IMR[:, 0:1], in1=IMP[:, tb, :],
                                                                                    op0=ALU.mult, op1=ALU.add),
                              [ps[:, 0:32], IMR[:, :], IMP[:, tb, :]], [IMP[:, tb, :]])
            P.phase = P.phase.split('/')[0] + '/sel'
            for tb in range(16):
                tsl = slice(tb * 128, (tb + 1) * 128)
                oa, ob = 2 * tb, 2 * tb + 1
                A('pool', lambda e: e.memset(GATE[:, :], -1e30), [], [GATE[:, :]])
                if oa > 0:
                    A('dve', lambda e, tb=tb, oa=oa: e.tensor_copy(out=GATE[0:64, 0:oa], in_=IMP[0:64, tb, 0:oa]), [IMP[0:64, tb, 0:oa]], [GATE[0:64, 0:oa]])
                A('dve', lambda e, tb=tb, ob=ob: e.tensor_copy(out=GATE[64:128, 0:ob], in_=IMP[64:128, tb, 0:ob]), [IMP[64:128, tb, 0:ob]], [GATE[64:128, 0:ob]])
                A('dve', lambda e: e.max(out=TOP8[:, :], in_=GATE[:, :]), [GATE[:, :]], [TOP8[:, :]])
                A('dve', lambda e: e.tensor_scalar(out=MB[:, :], in0=GATE[:, :], scalar1=TOP8[:, 2:3], scalar2=-1.0, op0=ALU.is_ge, op1=ALU.add),
                  [GATE[:, :], TOP8[:, 2:3]], [MB[:, :]])
                A('dve', lambda e, oa=oa: e.memset(MB[0:64, oa:32], -1.0), [], [MB[0:64, oa:32]])
                A('dve', lambda e, oa=oa: e.memset(MB[0:64, oa:oa + 1], 0.0), [], [MB[0:64, oa:oa + 1]])
                A('dve', lambda e, ob=ob: e.memset(MB[64:128, ob:32], -1.0), [], [MB[64:128, ob:32]])
                A('dve', lambda e, ob=ob: e.memset(MB[64:128, ob:ob + 1], 0.0), [], [MB[64:128, ob:ob + 1]])
                ps = mps()
                A('pe', lambda e, ps=ps: e.matmul(ps[0:32, 0:128], lhsT=MB[:, :], rhs=BIGI[:, :], start=True, stop=True), [MB[:, :], BIGI[:, :]], [ps[0:32, 0:128]])
                A('act', lambda e, ps=ps, tsl=tsl: e.activation(out=MBT[64:96, tsl], in_=ps[0:32, 0:128], func=AF.Copy), [ps[0:32, 0:128]], [MBT[64:96, tsl]])
            P.phase = P.phase.split('/')[0] + '/pass2'
            EPDEF[0] = 1
            pipe = Pipe()
            obi = [0]

            def branch_ep(h, ts, br, O):
                def ep():
                    r, r2, t, acc = FT[0], FT[1], FT[2], FT[3]
                    recip_act(r[0:64, :], O[64:128, :])
                    ps = mps()
                    A('pe', lambda e: e.matmul(ps[0:64, :], lhsT=SELB[0:12, br * 4 + h, :], rhs=SGB[:, ts], start=True, stop=True),
                      [SELB[0:12, br * 4 + h, :], SGB[:, ts]], [ps[0:64, :]])
                    A('dve', lambda e: e.tensor_tensor(out=r2[0:64, :], in0=ps[0:64, :], in1=r[0:64, :], op=ALU.mult),
                      [ps[0:64, :], r[0:64, :]], [r2[0:64, :]])
                    dst = acc if br == 0 else t
                    A('dve', lambda e: e.tensor_tensor(out=dst[0:64, :], in0=O[0:64, :], in1=r2[0:64, :], op=ALU.mult),
                      [O[0:64, :], r2[0:64, :]], [dst[0:64, :]])
                    if br > 0:
                        A('pool', lambda e: e.tensor_tensor(out=acc[0:64, :], in0=acc[0:64, :], in1=t[0:64, :], op=ALU.add),
                          [acc[0:64, :], t[0:64, :]], [acc[0:64, :]])
                    if br == 2:
                        write_mix(mix_dst(h, ts), acc[0:64, :])
                return ep

            def next_o():
                o = OB[obi[0] % 3]
                obi[0] += 1
                return o
            def load_gc_slice(h, qt, dst, slot):
                src = bass.AP(scr_c[h].tensor, 2032 + qt * 512, [[L_C, 128], [1, 512]])
                P.add('pool', lambda e: e.dma_start(out=dst, in_=src), reads=[scr_c[h][:]], writes=[dst], dma=True,
                      semkey='dma_gcs%d' % slot)

            def load_head(h):
                load_g('n', 4 + h, GNs[h % 2][:, :], slot=2 + (h % 2))
                load_g('w', h, GWs[h % 2][:, :], slot=4 + (h % 2))
            load_head(0)
            gci = [0]
            load_gc_slice(0, 0, GCs[0], 0)
            for h in range(4):
                b0 = (h % 2) * 64
                GN = GNs[h % 2]
                GW = GWs[h % 2]
                if h == 0:
                    A('dve', lambda e: e.tensor_copy(out=MBT[0:64, :], in_=QT[0:64, 0, :]), [QT[0:64, 0, :]], [MBT[0:64, :]])
                if h + 1 < 4:
                    load_head(h + 1)
                for qt in range(4):
                    t0 = qt * 512
                    ts = slice(t0, t0 + 512)
                    gcs = GCs[gci[0] % 2]
                    gci[0] += 1
                    nh, nq = (h, qt + 1) if qt < 3 else (h + 1, 0)
                    if nh < 4:
                        load_gc_slice(nh, nq, GCs[gci[0] % 2], gci[0] % 2)
                    Oc = next_o()
                    sm_tile(pipe, cmp_terms(h, ts, gcs), 0, 512, 1.0, None, VC[0:127, :], Oc, True, True, nk=127,
                            epilogue=branch_ep(h, ts, 0, Oc))
                    blocks = causal_blocks(qt)
                    Os = next_o()
                    for bi, (kb, q0, N, D) in enumerate(blocks):
                        kT = OHS[:, kb * 128:(kb + 1) * 128]
                        qT = MBT[:, t0 + q0:t0 + q0 + N]
                        ext, cb = bias_terms(GN, 4 + h, q0, N, D)
                        last = (bi == len(blocks) - 1)
                        sm_tile(pipe, [(kT, qT)] + ext, q0, N, 1.0, cb, VT[:, kb, 0, :], Os, bi == 0, last,
                                epilogue=(branch_ep(h, ts, 1, Os) if last else None))
                    if qt == 3 and h < 3:
                        nb0 = ((h + 1) % 2) * 64
                        A('dve', lambda e, nb0=nb0, h=h: e.tensor_copy(out=MBT[0:64, :], in_=QT[nb0:nb0 + 64, (h + 1) // 2, :]),
                          [QT[nb0:nb0 + 64, (h + 1) // 2, :]], [MBT[0:64, :]])
                    wblocks = [(kb, q0, N, D) for (kb, q0, N, D) in blocks if D <= 512]
                    Ow = next_o()
                    for bi, (kb, q0, N, D) in enumerate(wblocks):
                        kT = (KT[:, 0, kb * 128:(kb + 1) * 128] if h % 2 == 0 else KX[:, 1, kb * 128:(kb + 1) * 128])
                        qT = QT[:, h // 2, t0 + q0:t0 + q0 + N]
                        last = (bi == len(wblocks) - 1)
                        sm_tile(pipe, [(kT, qT), (IDENT[:, :], GW[:, D:D + N])], q0, N, 1.0, None, VT[:, kb, 1, :], Ow, bi == 0, last,
                                epilogue=(branch_ep(h, ts, 2, Ow) if last else None))
                pipe.flush()
            FILL[0] = 0

        AFTER_PROJ = [None]

        def after_proj():
            if AFTER_PROJ[0]:
                AFTER_PROJ[0]()
                AFTER_PROJ[0] = None

        MIXERS = {0: mixer_sb, 1: mixer_moba, 2: mixer_nsa, 3: mixer_diff}

        for l in range(depth):
            P.phase = 'L%d norm' % l
            A('pool', lambda e: e.memset(VT[:, :, :, 64:128], 1.0), [], [VT[:, :, :, 64:128]])
            if mixers:
                for tt in range(4):
                    rmsnorm_tile(tt, PK_NA + l * 8, lambda c, tt=tt: (HT[:, c, tt * 512:(tt + 1) * 512], None))
            for mixer in mixers:
                P.phase = 'L%d mixer%d' % (l, mixer)
                mi = list(mixers).index(mixer)
                if mi == 0:
                    P.dma('pool', WIN[:, :, :], w_in_d[l, mixer].rearrange("(c p) f -> p c f", p=128), semkey='dma_win')
                if mi + 1 < len(mixers):
                    nxt = mixers[mi + 1]
                    AFTER_PROJ[0] = (lambda l=l, nxt=nxt: P.dma('pool', WIN[:, :, :], w_in_d[l, nxt].rearrange("(c p) f -> p c f", p=128), semkey='dma_win'))
                else:
                    AFTER_PROJ[0] = None
                if mixer != 2:
                    P.dma('pool', WO[:, :, :], w_out_d[l, mixer * 256:(mixer + 1) * 256, :].rearrange("(m p) c -> p m c", p=128), semkey='dma_wo')
                MIXERS[mixer](l)
                if dbg == (l, mixer) and cfg.get('dump'):
                    for nm in cfg['dump']:
                        src = {'QT': QT, 'KT': KT, 'KX': KX, 'HT0': HT[:, 0:2, :], 'HT1': HT[:, 2:4, :]}[nm]
                        dd = nc.dram_tensor("dump_" + nm, [128, 2 * SEQ] if nm != 'MBT' else [64, SEQ], BF16, kind="ExternalOutput").ap()
                        if nm == 'MBT':
                            P.dma('sp', dd[:, :], src[:, :], semkey='dma_out')
                        else:
                            P.dma('sp', dd.rearrange("p (a b) -> p a b", a=2), src, semkey='dma_out')
                if dbg == (l, mixer):
                    for m in range(2):
                        P.dma('sp', dbg_d[m * 128:(m + 1) * 128, :], MIXM[:, m, :], semkey='dma_out')
                P.phase = 'L%d wout%d' % (l, mixer)
                apply_wout(l, mixer)
            if do_mlp:
                P.phase = 'L%d mlp' % l
                for tt in range(4):
                    rmsnorm_tile(tt, PK_NM + l * 8, lambda c, tt=tt: (HT[:, c, tt * 512:(tt + 1) * 512], None))
                rli = 0
                for fg in range(8):
                    wu, wd, at = WUP[fg % 2], WDN[fg % 2], AT[fg % 2]
                    P.dma('pool', wu[:, :, :], w_up_d[l, :, fg * 512:(fg + 1) * 512].rearrange("(c p) f -> p c f", p=128), semkey='dma_wu%d' % (fg % 2))
                    P.dma('pool', wd[:, :, :], w_down_d[l, fg * 512:(fg + 1) * 512, :].rearrange("(m p) c -> p m c", p=128), semkey='dma_wd%d' % (fg % 2))
                    for tt in range(4):
                        ts = slice(tt * 512, (tt + 1) * 512)
                        for m in range(4):
                            ps = nps()
                            for k in range(8):
                                extra = []
                                if probe_wait:
                                    dd = DUM[:, k:k + 1]
                                    A('dve', lambda e, dd=dd: e.memset(dd, 0.0), [], [dd])
                                    extra = [dd]
                                pb0 = pb if (k % 2 == 0) else 0
                                A('pe', lambda e, ps=ps, k=k, m=m, wu=wu, ts=ts, pb0=pb0: e.matmul(
                                    ps[:, :], lhsT=wu[pb0:pb0 + pk, k, m * 128:(m + 1) * 128], rhs=HT[pb0:pb0 + pk, k, ts], start=(k == 0), stop=(k == 7)),
                                  [wu[:, k, m * 128:(m + 1) * 128], HT[:, k, ts]] + extra, [ps[:, :]])
                            rl = RL[rli % 2]
                            rli += 1
                            A('act', lambda e, ps=ps, rl=rl: e.activation(out=rl[:, :], in_=ps[:, :], func=AF.Relu), [ps[:, :]], [rl[:, :]])
                            sq_eng = 'dve' if (m % 2 == 0) else 'pool'
                            A(sq_eng, lambda e, rl=rl, at=at, m=m, ts=ts: e.tensor_tensor(out=at[:, m, ts], in0=rl[:, :], in1=rl[:, :], op=ALU.mult),
                              [rl[:, :]], [at[:, m, ts]])
                    for c in range(8):
                        for tt in range(4):
                            ts = slice(tt * 512, (tt + 1) * 512)
                            ps = nps()
                            for m in range(4):
                                A('pe', lambda e, ps=ps, m=m, c=c, wd=wd, at=at, ts=ts: e.matmul(
                                    ps[:, :], lhsT=wd[:, m, c * 128:(c + 1) * 128], rhs=at[:, m, ts], start=(m == 0), stop=(m == 3)),
                                  [wd[:, m, c * 128:(c + 1) * 128], at[:, m, ts]], [ps[:, :]])
                            A('dve', lambda e, ps=ps, c=c, ts=ts: e.tensor_tensor(out=XT[:, c, ts], in0=XT[:, c, ts], in1=ps[:, :], op=ALU.add),
                              [XT[:, c, ts], ps[:, :]], [XT[:, c, ts]])

        P.phase = 'final'
        oi = [0]
        for tt in range(4):
            ts = slice(tt * 512, (tt + 1) * 512)

            def dstf(c, ts=ts):
                ob = OUTB[oi[0] % 2]
                oi[0] += 1

                def post(ob=ob, c=c, ts=ts):
                    P.dma('sp', outT_d[c * 128:(c + 1) * 128, ts], ob[:, :], semkey='dma_out')
                return ob[:, :], post
            rmsnorm_tile(tt, PK_NF, dstf)
        P.final_dma_keys.append('dma_out')
        n = P.emit()
    _CACHE['prog'] = P
    return nc, n


def make_in_maps(inputs, nb=None):
    x = np.asarray(inputs['x'], np.float32)
    B = x.shape[0] if nb is None else nb
    pk = pack_small(inputs)
    hc = host_consts()
    shared = {
        'pk': pk,
        'w_in': permute_w_in(np.asarray(inputs['w_in'], np.float32)),
        'w_out': np.ascontiguousarray(inputs['w_out'], np.float32),
        'w_up': np.ascontiguousarray(inputs['w_up'], np.float32),
        'w_down': np.ascontiguousarray(inputs['w_down'], np.float32),
        'cmp_k_w1': np.ascontiguousarray(inputs['cmp_k_w1'], np.float32),
        'cmp_k_w2': np.ascontiguousarray(inputs['cmp_k_w2'], np.float32),
        'cmp_v_w1': np.ascontiguousarray(inputs['cmp_v_w1'], np.float32),
        'cmp_v_w2': np.ascontiguousarray(inputs['cmp_v_w2'], np.float32),
    }
    shared.update(hc)
    in_maps = []
    for b in range(B):
        m = dict(shared)
        m['xT'] = np.ascontiguousarray(x[b].T)
        in_maps.append(m)
    return in_maps


def kernel(**inputs):
    if 'nc' not in _CACHE:
        _CACHE['nc'] = build()[0]
    nc = _CACHE['nc']
    in_maps = make_in_maps(inputs)
    B = len(in_maps)
    res = run_bass_kernel_spmd(nc, in_maps, core_ids=list(range(B)))
    out = np.stack([np.ascontiguousarray(r['outT'].T) for r in res.results], axis=0)
    return out.astype(np.float32)
```

```python
import numpy as np
import concourse.bass as bass
import concourse.mybir as mybir
from concourse.bass_utils import run_bass_kernel_spmd

F32 = mybir.dt.float32
BF16 = mybir.dt.bfloat16
AF = mybir.ActivationFunctionType
ALU = mybir.AluOpType
AX = mybir.AxisListType

D_MODEL = 1024
SEQ = 2048
DEPTH = 2
D_FF = 4096
D_IN = 2956
NCORES = 8
EPS = 1e-6

_DTSIZE = {F32: 4, BF16: 2, mybir.dt.int32: 4, mybir.dt.uint32: 4, mybir.dt.uint16: 2,
           mybir.dt.int16: 2, mybir.dt.uint8: 1, mybir.dt.int8: 1, mybir.dt.float32r: 4,
           mybir.dt.float16: 2}


def _prod(xs):
    r = 1
    for v in xs:
        r *= int(v)
    return r


def region(ap):
    t = ap.tensor
    name = t.name
    esz = _DTSIZE[ap.dtype]
    off = int(ap.offset)
    dims = ap.ap
    space = str(ap.space)
    if 'DRAM' in space.upper() or 'HBM' in space.upper() or type(t).__name__.startswith('DRam'):
        lo = off
        hi = off
        for (st, cnt) in dims:
            if st >= 0:
                hi += (cnt - 1) * st
            else:
                lo += (cnt - 1) * st
        return (name, 0, 1, lo * esz, (hi + 1) * esz)
    tsz = _DTSIZE[t.dtype]
    pstride = _prod(list(t.shape)[1:]) * tsz // esz
    p0 = off // pstride
    f0 = off % pstride
    (pst, pcnt) = dims[0]
    assert pst % pstride == 0 or pcnt == 1, (name, dims, pstride)
    pstep = max(1, pst // pstride)
    p1 = p0 + (pcnt - 1) * pstep + 1
    lo = f0
    hi = f0
    for (st, cnt) in dims[1:]:
        if st >= 0:
            hi += (cnt - 1) * st
        else:
            lo += (cnt - 1) * st
    return (name, p0, p1, lo * esz, (hi + 1) * esz)


def _overlap(a, b):
    return a[1] < b[2] and b[1] < a[2] and a[3] < b[4] and b[3] < a[4]


def _covers(a, b):
    return a[1] <= b[1] and a[2] >= b[2] and a[3] <= b[3] and a[4] >= b[4]


_CACHE = {}


class _Op:
    __slots__ = ('idx', 'eng', 'fn', 'dma', 'semkey', 'deps', 'ordinal', 'waits', 'sig', 'semval', 'pe_group', 'phase')


class Prog:
    ENGS = ('pe', 'act', 'dve', 'pool', 'sp')

    def __init__(self, nc):
        self.nc = nc
        self.ops = []
        self.acc = {}
        self.final_dma_keys = []
        self.phase = ''

    def add(self, eng, fn, reads=(), writes=(), dma=False, semkey=None):
        op = _Op()
        op.idx = len(self.ops)
        op.eng = eng
        op.fn = fn
        op.dma = dma
        op.deps = set()
        op.phase = self.phase
        rregs = [region(a) for a in reads]
        wregs = [region(a) for a in writes]
        def _banks(r):
            return [(r[0], 0, 128, bk * 2048, (bk + 1) * 2048) for bk in range(r[3] // 2048, (r[4] - 1) // 2048 + 1)]
        ps_r = [x for r in rregs if r[0].startswith('PS') for x in _banks(r)]
        rregs = [r for r in rregs if not r[0].startswith('PS')]
        wregs = [x for w in wregs for x in (_banks(w) if w[0].startswith('PS') else [w])] + ps_r
        if dma:
            op.semkey = semkey if semkey is not None else ('dma_' + wregs[0][0])
        else:
            op.semkey = None
        stream = op.semkey if dma else eng
        for r in rregs:
            lst = self.acc.get(r[0], [])
            for rec in lst:
                if rec[1] and _overlap(rec[0], r):
                    op.deps.update(rec[2].values())
        for w in wregs:
            lst = self.acc.get(w[0], [])
            for rec in lst:
                if _overlap(rec[0], w):
                    op.deps.update(rec[2].values())
        for w in wregs:
            lst = self.acc.setdefault(w[0], [])
            lst[:] = [rec for rec in lst if not _covers(w, rec[0])]
            lst.append([w, True, {stream: op.idx}])
        for r in rregs:
            lst = self.acc.setdefault(r[0], [])
            done = False
            for rec in lst:
                if (not rec[1]) and rec[0] == r:
                    rec[2][stream] = op.idx
                    done = True
                    break
            if not done:
                lst.append([r, False, {stream: op.idx}])
        op.deps.discard(op.idx)
        self.ops.append(op)
        return op

    def dma(self, q, out, in_, semkey=None):
        return self.add(q, lambda e: e.dma_start(out=out, in_=in_), reads=[in_], writes=[out],
                        dma=True, semkey=semkey)

    def emit(self):
        nc = self.nc
        ops = self.ops
        cnt = {}
        for op in ops:
            s = op.semkey if op.dma else op.eng
            cnt[s] = cnt.get(s, 0) + 1
            op.ordinal = cnt[s]
            op.sig = op.dma
            op.waits = []
        waited = {e: {} for e in self.ENGS}
        import bisect
        dma_idx = {}
        for op in ops:
            if op.dma:
                dma_idx.setdefault(op.semkey, []).append(op.idx)
        for op in ops:
            need = {}
            for d in op.deps:
                a = ops[d]
                s = a.semkey if a.dma else a.eng
                if (not a.dma) and a.eng == op.eng and op.eng == 'pe' and not op.dma:
                    continue
                o = a.ordinal
                if a.dma:
                    o = bisect.bisect_left(dma_idx[s], op.idx)
                if o > need.get(s, 0):
                    need[s] = o
            w = waited[op.eng]
            for s, o in need.items():
                if w.get(s, 0) >= o:
                    continue
                w[s] = o
                op.waits.append((s, o))
        needed = set()
        for op in ops:
            for so in op.waits:
                needed.add(so)
        semvals = {}
        run = {}
        for op in ops:
            s = op.semkey if op.dma else op.eng
            if op.dma:
                run[s] = run.get(s, 0) + 16
                semvals[(s, op.ordinal)] = run[s]
            else:
                if (s, op.ordinal) in needed:
                    op.sig = True
                    run[s] = run.get(s, 0) + 1
                    semvals[(s, op.ordinal)] = run[s]
        final_vals = dict(run)
        streams = sorted(run.keys())
        from contextlib import ExitStack
        with ExitStack() as es:
            sems = {}
            for s in streams:
                sems[s] = es.enter_context(nc.semaphore('s_' + s))
            block = es.enter_context(nc.Block())
            per_eng = {e: [op for op in ops if op.eng == e] for e in self.ENGS}
            final_keys = list(self.final_dma_keys)

            def body(e, eops, is_last_waiter):
                for op in eops:
                    for (s, o) in op.waits:
                        e.wait_ge(sems[s], semvals[(s, o)])
                    ins = op.fn(e)
                    if op.sig:
                        s = op.semkey if op.dma else op.eng
                        ins.then_inc(sems[s], 16 if op.dma else 1)
                if is_last_waiter:
                    for k in final_keys:
                        e.wait_ge(sems[k], final_vals[k])

            @block.tensor
            def _(e):
                body(e, per_eng['pe'], False)

            @block.scalar
            def _(e):
                body(e, per_eng['act'], False)

            @block.vector
            def _(e):
                body(e, per_eng['dve'], False)

            @block.gpsimd
            def _(e):
                body(e, per_eng['pool'], False)

            @block.sync
            def _(e):
                body(e, per_eng['sp'], True)
        return len(ops)


import math
from contextlib import ExitStack

NEGV = -30000.0
BIGM = 240000.0
WINC = 784
GELU_C = math.sqrt(2.0 / math.pi)

PK_NA = 0
PK_NM = 16
PK_NF = 32
PK_SUBLN = 40
PK_TAB = 44
PK_LAM = 64
PK_POS = 320
NPK = 384

L_N, L_W, L_C = 768, 1152, 4096
OFF_N, OFF_W, OFF_C = 127, 127, 2063


def _t5_bucket(n):
    n = np.maximum(n, 0)
    nf = np.maximum(n, 1).astype(np.float32)
    large = 16 + (np.log(nf / np.float32(16)) / np.float32(math.log(128 / 16)) * np.float32(16)).astype(np.int32)
    large = np.minimum(large, 31)
    return np.where(n < 16, n, large)


def _onehot(L, off, win):
    oh = np.zeros((33, L), np.float32)
    d = np.arange(L) - off
    masked = d < 0
    if win:
        masked = masked | (d >= 512)
    b = _t5_bucket(d)
    for i in range(L):
        if masked[i]:
            oh[32, i] = 1.0
        else:
            oh[b[i], i] = 1.0
    return oh


def _cover():
    cmp_idx = np.arange(127)[:, None] * 16 + np.arange(32)[None, :]
    s_start = np.arange(32) * 64
    cover = np.clip(np.minimum(cmp_idx[:, -1][:, None], (s_start + 63)[None, :])
                    - np.maximum(cmp_idx[:, 0][:, None], s_start[None, :]) + 1, 0, None) / 32.0
    cv = np.zeros((128, 33), np.float32)
    cv[:127, :32] = cover
    cv[:127, 32] = 1.0
    return cv


def host_consts():
    return {'ohn': _onehot(L_N, OFF_N, False), 'ohw': _onehot(L_W, OFF_W, True),
            'ohc': _onehot(L_C, OFF_C, False), 'cover': _cover()}


def pack_small(inputs):
    pk = np.zeros((128, NPK), np.float32)
    for l in range(DEPTH):
        pk[:, PK_NA + l * 8:PK_NA + l * 8 + 8] = inputs['norm_attn'][l].reshape(8, 128).T
        pk[:, PK_NM + l * 8:PK_NM + l * 8 + 8] = inputs['norm_mlp'][l].reshape(8, 128).T
        pk[:, PK_SUBLN + l] = np.tile(inputs['diff_subln'][l], 2)
        pk[0, PK_LAM + l * 128:PK_LAM + (l + 1) * 128] = inputs['diff_lambda'][l].reshape(-1)
        pk[0:64, PK_POS + l * 32:PK_POS + (l + 1) * 32] = inputs['cmp_pos_k'][l].T
        pk[64:128, PK_POS + l * 32:PK_POS + (l + 1) * 32] = inputs['cmp_pos_v'][l].T
    pk[:, PK_NF:PK_NF + 8] = inputs['final_norm'].reshape(8, 128).T
    pk[0:32, PK_TAB:PK_TAB + 12] = inputs['rel_bias']
    return pk


def permute_w_in(w_in):
    L = w_in.shape[0]
    out = np.zeros((L, 4, D_MODEL, WINC), np.float32)
    out[:, 0, :, 0:768] = w_in[:, :, 0:768]
    out[:, 1, :, 0:768] = w_in[:, :, 768:1536]
    o = 1536
    ns = out[:, 2]
    ns[:, :, 0:256] = w_in[:, :, o:o + 256]
    ns[:, :, 256:320] = w_in[:, :, o + 256:o + 320]
    ns[:, :, 320:384] = w_in[:, :, o + 320:o + 384]
    ns[:, :, 384:448] = w_in[:, :, o + 384:o + 448]
    ns[:, :, 448:512] = w_in[:, :, o + 384:o + 448]
    ns[:, :, 512:576] = w_in[:, :, o + 512:o + 576]
    ns[:, :, 576:640] = w_in[:, :, o + 512:o + 576]
    ns[:, :, 640:652] = w_in[:, :, o + 640:o + 652]
    ns[:, :, 652:716] = w_in[:, :, o + 448:o + 512]
    ns[:, :, 716:780] = w_in[:, :, o + 576:o + 640]
    out[:, 3, :, 0:768] = w_in[:, :, 2188:2956]
    return out


def build(cfg=None):
    cfg = cfg or {}
    mixers = cfg.get('mixers', (0, 1, 2, 3))
    depth = cfg.get('depth', DEPTH)
    do_mlp = cfg.get('mlp', True)
    dbg = cfg.get('dbg', None)
    stop = cfg.get('stop', 99)
    filler = cfg.get('filler', 0)
    probe_wait = cfg.get('probe_wait', 0)
    pk = cfg.get('probe_k', 128)
    pb = cfg.get('probe_b', 0)
    nc = bass.Bass("TRN2", target_bir_lowering=False)

    def din(name, shape, dt=F32):
        return nc.dram_tensor(name, shape, dt, kind="ExternalInput").ap()
    xT_d = din("xT", [D_MODEL, SEQ])
    pk_d = din("pk", [128, NPK])
    w_in_d = din("w_in", [DEPTH, 4, D_MODEL, WINC])
    w_out_d = din("w_out", [DEPTH, D_MODEL, D_MODEL])
    w_up_d = din("w_up", [DEPTH, D_MODEL, D_FF])
    w_down_d = din("w_down", [DEPTH, D_FF, D_MODEL])
    ck1_d = din("cmp_k_w1", [DEPTH, 2048, 256])
    ck2_d = din("cmp_k_w2", [DEPTH, 256, 64])
    cv1_d = din("cmp_v_w1", [DEPTH, 2048, 256])
    cv2_d = din("cmp_v_w2", [DEPTH, 256, 64])
    ohn_d = din("ohn", [33, L_N])
    ohw_d = din("ohw", [33, L_W])
    ohc_d = din("ohc", [33, L_C])
    cover_d = din("cover", [128, 33])
    outT_d = nc.dram_tensor("outT", [D_MODEL, SEQ], F32, kind="ExternalOutput").ap()
    dbg_d = nc.dram_tensor("dbg", [256, SEQ], BF16, kind="ExternalOutput").ap() if dbg is not None else None
    scr_n = [nc.dram_tensor("scr_n%d" % h, [128 * (L_N + 1) + 8], F32, kind="Internal").ap() for h in range(12)]
    scr_w = [nc.dram_tensor("scr_w%d" % h, [128 * (L_W + 1) + 8], F32, kind="Internal").ap() for h in range(4)]
    scr_c = [nc.dram_tensor("scr_c%d" % h, [128 * (L_C + 16) + 8], F32, kind="Internal").ap() for h in range(4)]

    with ExitStack() as es:
        def sb(name, shape, dt):
            return es.enter_context(nc.sbuf_tensor(name, shape, dt))

        XT = sb("XT", [128, 8, SEQ], F32)
        ARENA_B = 106 * 1024
        ARENA = sb("ARENA", [128, ARENA_B // 2], BF16)

        def av(off, shape, dt):
            nbytes = _prod(shape) * _DTSIZE[dt]
            assert off % 4 == 0 and off + nbytes <= ARENA_B, (off, shape)
            a = ARENA[:, off // 2:(off + nbytes) // 2]
            if dt != BF16:
                a = a.bitcast(dt)
            if len(shape) == 2:
                a = a.rearrange("p (a b) -> p a b", a=shape[0])
            elif len(shape) == 3:
                a = a.rearrange("p (a b c) -> p a b c", a=shape[0], b=shape[1])
            return a
        KB = 1024
        HT = av(0, [8, SEQ], BF16)
        QT = av(32 * KB, [2, SEQ], BF16)
        KT = av(40 * KB, [2, SEQ], BF16)
        GTF = av(40 * KB, [SEQ], F32)
        KX = av(48 * KB, [2, SEQ], BF16)
        VT = av(56 * KB, [16, 4, 128], BF16)
        WIN = av(72 * KB, [8, WINC], BF16)
        W1 = av(72 * KB, [32, 256], BF16)
        WO = av(85 * KB, [2, D_MODEL], BF16)
        MIXM = av(89 * KB, [2, SEQ], BF16)
        GREG = 97 * KB
        WUP = [av(32 * KB + i * 8 * KB, [8, 512], BF16) for i in range(2)]
        WDN = [av(48 * KB + i * 8 * KB, [4, D_MODEL], BF16) for i in range(2)]
        AT = [av(64 * KB + i * 16 * KB, [4, SEQ], BF16) for i in range(2)]
        OHC = av(32 * KB, [L_C], F32)
        OHN = av(48 * KB, [L_N], F32)
        OHW = av(52 * KB, [L_W], F32)
        TB = av(57 * KB, [12 * 128], F32)
        RREP = av(64 * KB, [L_C], F32)

        PK = sb("PK", [128, NPK], F32)
        ONESM = sb("ONESM", [128, 128], BF16)
        ONES64 = sb("ONES64", [128, 64], BF16)
        ONE1 = sb("ONE1", [128, 128], BF16)
        IDENT = sb("IDENT", [128, 128], BF16)
        BIGI = sb("BIGI", [128, 128], BF16)
        NEGU = sb("NEGU", [128, 128], BF16)
        OHS = sb("OHS", [128, SEQ], BF16)
        MBT = sb("MBT", [128, SEQ], BF16)
        CBH = sb("CBH", [128, 12], F32)
        COVER = sb("COVER", [128, 33], BF16)
        RSTD = sb("RSTD", [128, 512], F32)
        EPSC = sb("EPSC", [128, 1], F32)
        ONEC = sb("ONEC", [128, 1], F32)
        TINYC = sb("TINYC", [128, 1], F32)
        LAMC = sb("LAMC", [64, 2], F32)
        LTMP = sb("LTMP", [1, 256], F32)
        SQ = [sb("SQ%d" % i, [128, 512], BF16) for i in range(2)]
        PT = [sb("PT%d" % i, [128, 512], BF16) for i in range(3)]
        FT = [sb("FT%d" % i, [128, 512], F32) for i in range(4)]
        SPB = [sb("SPB%d" % i, [128, 512], BF16) for i in range(2)]
        RL = FT[0:2]
        OUTB = FT[2:4]
        SELB = sb("SELB", [12, 12, 64], BF16)
        ONER = sb("ONER", [1, 64], F32)
        KM = sb("KM", [128, 2, 8], F32)
        KMB = sb("KMB", [128, 2, 8], BF16)
        GATE = sb("GATE", [128, 32], F32)
        TOP8 = sb("TOP8", [128, 8], F32)
        MB = sb("MB", [128, 32], BF16)
        MB8 = sb("MB8", [128, 8], BF16)
        GS_EXTRA = [(sb("MB8_%d" % i, [128, 8], BF16), sb("MB_%d" % i, [128, 32], BF16), sb("GATE_%d" % i, [128, 32], F32), sb("TOP8_%d" % i, [128, 8], F32)) for i in range(3)]
        IMP = sb("IMP", [128, 16, 32], F32)
        IMR = sb("IMR", [128, 1], F32)
        POSB = sb("POSB", [128, 32], BF16)
        B1 = sb("B1", [128, 4], F32)
        W2 = sb("W2", [128, 2, 2, 128], BF16)
        GEL = sb("GEL", [128, 4, 128], BF16)
        DUM = sb("DUM", [128, 8], F32)
        ZLH = sb("ZLH", [128, 128], BF16)
        KC = sb("KC", [128, 2, 128], BF16)
        VC = sb("VC", [128, 128], BF16)
        PS = [es.enter_context(nc.psum_tensor("PS%d" % i, [128, 512], F32)) for i in range(8)]

        P = Prog(nc)

        def A(eng, fn, reads, writes):
            return P.add(eng, fn, reads=reads, writes=writes)

        for c in range(8):
            P.dma('sp', XT[:, c, :], xT_d[c * 128:(c + 1) * 128, :])
        P.dma('sp', PK[:, :], pk_d[:, :])
        A('pool', lambda e: e.memset(ONESM[:, :], 1.0 / 1024.0), [], [ONESM[:, :]])
        A('pool', lambda e: e.memset(ONES64[:, :], 1.0 / 64.0), [], [ONES64[:, :]])
        A('pool', lambda e: e.memset(ONE1[:, :], 1.0), [], [ONE1[:, :]])
        A('pool', lambda e: e.memset(EPSC[:, :], EPS), [], [EPSC[:, :]])
        A('pool', lambda e: e.memset(ONEC[:, :], 1.0), [], [ONEC[:, :]])
        A('pool', lambda e: e.memset(ZLH[:, :], 0.0), [], [ZLH[:, :]])
        A('pool', lambda e: e.memset(TINYC[:, :], 1e-30), [], [TINYC[:, :]])
        A('pool', lambda e: e.affine_select(out=IDENT[:, :], in_=ONE1[:, :], pattern=[[1, 128]], compare_op=ALU.is_equal,
                                            fill=0.0, base=0, channel_multiplier=-1), [ONE1[:, :]], [IDENT[:, :]])
        A('pool', lambda e: e.tensor_scalar(out=BIGI[:, :], in0=IDENT[:, :], scalar1=BIGM, scalar2=None, op0=ALU.mult),
          [IDENT[:, :]], [BIGI[:, :]])
        A('pool', lambda e: e.memset(NEGU[:, :], -1.0), [], [NEGU[:, :]])
        A('pool', lambda e: e.affine_select(out=NEGU[:, :], in_=NEGU[:, :], pattern=[[-1, 128]], compare_op=ALU.is_gt,
                                            fill=0.0, base=0, channel_multiplier=1), [NEGU[:, :]], [NEGU[:, :]])
        OHTMP = av(80 * KB, [SEQ], BF16)
        for (T, w) in ((OHTMP[0:32, :], 64),):
            A('pool', lambda e, T=T: e.memset(T, 1.0), [], [T])
            A('pool', lambda e, T=T, w=w: e.affine_select(out=T, in_=T, pattern=[[1, SEQ]],
                                                          compare_op=ALU.is_ge, fill=0.0, base=0, channel_multiplier=-w),
              [T], [T])
            A('pool', lambda e, T=T, w=w: e.affine_select(out=T, in_=T, pattern=[[-1, SEQ]],
                                                          compare_op=ALU.is_ge, fill=0.0, base=w - 1, channel_multiplier=w),
              [T], [T])
        A('act', lambda e: e.activation(out=OHS[64:96, :], in_=OHTMP[0:32, :], func=AF.Copy), [OHTMP[0:32, :]], [OHS[64:96, :]])
        A('dve', lambda e: e.memset(OHS[96:128, :], 0.0), [], [OHS[96:128, :]])
        A('pool', lambda e: e.memset(VC[:, 64:128], 1.0), [], [VC[:, 64:128]])
        A('pool', lambda e: e.memset(SELB[:, :, :], 1.0), [], [SELB[:, :, :]])
        A('pool', lambda e: e.affine_select(out=SELB[:, :, :], in_=SELB[:, :, :], pattern=[[-1, 12], [0, 64]],
                                            compare_op=ALU.is_equal, fill=0.0, base=0, channel_multiplier=1),
          [SELB[:, :, :]], [SELB[:, :, :]])
        P.dma('pool', COVER[:, :], cover_d[:, :])
        A('pool', lambda e: e.memset(ONER[:, :], 1.0), [], [ONER[:, :]])

        P.dma('sp', OHN[0:33, :], ohn_d[:, :], semkey='dma_ohn')
        P.dma('sp', OHW[0:33, :], ohw_d[:, :], semkey='dma_ohw')
        P.dma('sp', OHC[0:33, :], ohc_d[:, :], semkey='dma_ohc')
        A('pool', lambda e: e.memset(TB[32:33, :], NEGV), [], [TB[32:33, :]])
        for h in range(12):
            A('dve', lambda e, h=h: e.tensor_copy(out=TB[0:32, h * 128:(h + 1) * 128],
                                                  in_=PK[0:32, PK_TAB + h:PK_TAB + h + 1].to_broadcast([32, 128])),
              [PK[0:32, PK_TAB + h:PK_TAB + h + 1]], [TB[0:32, h * 128:(h + 1) * 128]])
        psr = [0]

        def nps():
            p = PS[psr[0] % 8]
            psr[0] += 1
            return p

        def gen_bias(h, OH, L, scr, sk, inv_scale, want_cb):
            for j in range(0, L, 512):
                n = min(512, L - j)
                ps = nps()
                A('pe', lambda e, ps=ps, j=j, n=n: e.matmul(ps[:, 0:n], lhsT=TB[0:33, h * 128:(h + 1) * 128],
                                                            rhs=OH[0:33, j:j + n], start=True, stop=True),
                  [TB[0:33, h * 128:(h + 1) * 128], OH[0:33, j:j + n]], [ps[:, 0:n]])
                A('act', lambda e, ps=ps, j=j, n=n: e.activation(out=RREP[:, j:j + n], in_=ps[:, 0:n], func=AF.Copy,
                                                                 scale=inv_scale),
                  [ps[:, 0:n]], [RREP[:, j:j + n]])
                if want_cb and j == 0:
                    A('dve', lambda e, ps=ps: e.tensor_copy(out=CBH[:, h:h + 1], in_=ps[:, 400:401]),
                      [ps[:, 400:401]], [CBH[:, h:h + 1]])
            dst = bass.AP(scr.tensor, 0, [[L + sk, 128], [1, L]])
            P.add('sp', lambda e, dst=dst: e.dma_start(out=dst, in_=RREP[:, 0:L]), reads=[RREP[:, 0:L]], writes=[scr[:]],
                  dma=True, semkey='dma_' + scr.tensor.name)

        for h in range(12):
            gen_bias(h, OHN, L_N, scr_n[h], 1, (1.0 if h < 8 else math.sqrt(32.0)), True)
        for h in range(4):
            gen_bias(4 + h, OHW, L_W, scr_w[h], 1, 1.0, False)
            gen_bias(4 + h, OHC, L_C, scr_c[h], 16, 1.0, False)

        def load_g(kind, h, dst, slot=0):
            if kind == 'n':
                src = bass.AP(scr_n[h].tensor, OFF_N, [[L_N, 128], [1, 640]])
                full = scr_n[h]
            elif kind == 'w':
                src = bass.AP(scr_w[h].tensor, OFF_W, [[L_W, 128], [1, 1024]])
                full = scr_w[h]
            else:
                src = bass.AP(scr_c[h].tensor, 2032, [[L_C, 128], [1, 2048]])
                full = scr_c[h]
            P.add('pool', lambda e: e.dma_start(out=dst, in_=src), reads=[full[:]], writes=[dst], dma=True,
                  semkey='dma_greg%d' % slot)

        sqi = [0]

        def rmsnorm_tile(tt, gcol0, dst_fn):
            ts = slice(tt * 512, (tt + 1) * 512)
            ps = nps()
            for c in range(8):
                sq = SQ[sqi[0] % 2]
                sqi[0] += 1
                A('act', lambda e, sq=sq, c=c: e.activation(out=sq[:, :], in_=XT[:, c, ts], func=AF.Square),
                  [XT[:, c, ts]], [sq[:, :]])
                A('pe', lambda e, sq=sq, c=c: e.matmul(ps[:, :], lhsT=ONESM[:, :], rhs=sq[:, :], start=(c == 0), stop=(c == 7)),
                  [ONESM[:, :], sq[:, :]], [ps[:, :]])
            A('act', lambda e: e.activation(out=RSTD[:, :], in_=ps[:, :], func=AF.Sqrt, bias=EPSC[:, :]),
              [ps[:, :], EPSC[:, :]], [RSTD[:, :]])
            A('dve', lambda e: e.reciprocal(out=RSTD[:, :], in_=RSTD[:, :]), [RSTD[:, :]], [RSTD[:, :]])
            for c in range(8):
                dst, post = dst_fn(c)
                A('dve', lambda e, dst=dst, c=c: e.scalar_tensor_tensor(
                    out=dst, in0=XT[:, c, ts], scalar=PK[:, gcol0 + c:gcol0 + c + 1], in1=RSTD[:, :],
                    op0=ALU.mult, op1=ALU.mult),
                  [XT[:, c, ts], PK[:, gcol0 + c:gcol0 + c + 1], RSTD[:, :]], [dst])
                if post:
                    post()

        def proj_fm(col0, ncols, dst_fn, scale=1.0):
            for tt in range(4):
                ts = slice(tt * 512, (tt + 1) * 512)
                ps = nps()
                for k in range(8):
                    A('pe', lambda e, ps=ps, k=k, ts=ts: e.matmul(ps[0:ncols, :], lhsT=WIN[:, k, col0:col0 + ncols],
                                                                  rhs=HT[:, k, ts], start=(k == 0), stop=(k == 7)),
                      [WIN[:, k, col0:col0 + ncols], HT[:, k, ts]], [ps[0:ncols, :]])
                dst = dst_fn(tt)
                A('act', lambda e, ps=ps, dst=dst: e.activation(out=dst, in_=ps[0:ncols, :], func=AF.Copy, scale=scale),
                  [ps[0:ncols, :]], [dst])

        def proj_tm(col0, ncols, h0):
            nh = ncols // 64
            for tb in range(16):
                ps = nps()
                for k in range(8):
                    A('pe', lambda e, ps=ps, k=k, tb=tb: e.matmul(ps[:, 0:ncols], lhsT=HT[:, k, tb * 128:(tb + 1) * 128],
                                                                  rhs=WIN[:, k, col0:col0 + ncols], start=(k == 0), stop=(k == 7)),
                      [HT[:, k, tb * 128:(tb + 1) * 128], WIN[:, k, col0:col0 + ncols]], [ps[:, 0:ncols]])
                A('dve', lambda e, ps=ps, tb=tb: e.tensor_copy(out=VT[:, tb, h0:h0 + nh, 0:64],
                                                               in_=ps[:, 0:ncols].rearrange("p (h d) -> p h d", h=nh)),
                  [ps[:, 0:ncols]], [VT[:, tb, h0:h0 + nh, 0:64]])

        KZ = [KT[:, 0, :], KT[:, 1, :], KX[:, 0, :], KX[:, 1, :]]

        def proj_k_padded(colbase):
            for hh in range(4):
                ob = 64 - (hh % 2) * 64
                A('dve', lambda e, hh=hh, ob=ob: e.memset(KZ[hh][ob:ob + 64, :], 0.0), [], [KZ[hh][ob:ob + 64, :]])
            for c in range(2):
                for tt in range(4):
                    ts = slice(tt * 512, (tt + 1) * 512)
                    ps = nps()
                    for k in range(8):
                        A('pe', lambda e, ps=ps, k=k, ts=ts, c=c: e.matmul(ps[:, :], lhsT=WIN[:, k, colbase + c * 128:colbase + (c + 1) * 128],
                                                                            rhs=HT[:, k, ts], start=(k == 0), stop=(k == 7)),
                          [WIN[:, k, colbase + c * 128:colbase + (c + 1) * 128], HT[:, k, ts]], [ps[:, :]])
                    A('act', lambda e, ps=ps, c=c, ts=ts: e.activation(out=KZ[2 * c][0:64, ts], in_=ps[0:64, :], func=AF.Copy),
                      [ps[0:64, :]], [KZ[2 * c][0:64, ts]])
                    A('dve', lambda e, ps=ps, c=c, ts=ts: e.tensor_copy(out=KZ[2 * c + 1][64:128, ts], in_=ps[64:128, :]),
                      [ps[64:128, :]], [KZ[2 * c + 1][64:128, ts]])

        def vones(tb, c0):
            return VT[:, tb, c0 // 64, :]

        class Pipe:
            def __init__(self):
                self.e1 = None
                self.p0 = None
                self.eps = []

            def defer(self, fn, n=3):
                self.eps.append([n, fn])

            def _tick(self):
                for it in self.eps:
                    it[0] -= 1
                while self.eps and self.eps[0][0] <= 0:
                    self.eps.pop(0)[1]()

            def push(self, s, e, pv):
                self._tick()
                s()
                if self.e1:
                    self.e1[0]()
                if self.p0:
                    self.p0()
                self.p0 = self.e1[1] if self.e1 else None
                self.e1 = (e, pv)

            def flush(self):
                if self.e1:
                    self.e1[0]()
                if self.p0:
                    self.p0()
                if self.e1:
                    self.e1[1]()
                self.e1 = None
                self.p0 = None
                while self.eps:
                    self.eps.pop(0)[1]()

        zi = [0]
        pti = [0]
        FILL = [0]
        EPDEF = [3]
        ZB = [PS[0], PS[1], PS[2]]
        OB = [PS[3], PS[4], PS[5]]
        misc = [0]

        MISC = [[PS[6], PS[7]]]

        def mps():
            lst = MISC[0]
            p = lst[misc[0] % len(lst)]
            misc[0] += 1
            return p

        def recip_act(dst, den):
            A('act', lambda e: e.activation(out=dst, in_=den, func=AF.Ln, bias=TINYC[64:128, :]), [den, TINYC[64:128, :]], [dst])
            A('act', lambda e: e.activation(out=dst, in_=dst, func=AF.Exp, scale=-1.0), [dst], [dst])

        def sm_tile(pipe, terms, q0, N, act_scale, cbias, v_lhsT, o_ps, first, last, nk=128, extra_pv=None, epilogue=None):
            z = ZB[zi[0] % 3]
            zi[0] += 1
            pt = PT[pti[0] % 3]
            pti[0] += 1
            zs = z[0:nk, q0:q0 + N]
            pts = pt[0:nk, q0:q0 + N]

            def s():
                nf = FILL[0]
                if nf:
                    A('pe', lambda e: e.matmul(z[:, 0:nf], lhsT=ONE1[:, :], rhs=HT[:, 0, 0:nf], start=True, stop=True),
                      [ONE1[:, :], HT[:, 0, 0:nf]], [z[:, 0:nf]])
                for i, (a, b) in enumerate(terms):
                    A('pe', lambda e, a=a, b=b, i=i: e.matmul(zs, lhsT=a, rhs=b, start=(i == 0), stop=(i == len(terms) - 1)),
                      [a, b], [zs])

            def ex():
                if cbias is None:
                    A('act', lambda e: e.activation(out=pts, in_=zs, func=AF.Exp, scale=act_scale), [zs], [pts])
                else:
                    A('act', lambda e: e.activation(out=pts, in_=zs, func=AF.Exp, scale=act_scale, bias=cbias),
                      [zs, cbias], [pts])

            def pv():
                M = v_lhsT.shape[-1] if len(v_lhsT.shape) == 2 else 128
                osl = o_ps[0:M, q0:q0 + N]
                A('pe', lambda e: e.matmul(osl, lhsT=v_lhsT, rhs=pts, start=first, stop=last), [v_lhsT, pts], [osl])
                if extra_pv:
                    extra_pv(pt)
                if epilogue:
                    pipe.defer(epilogue, EPDEF[0])
            pipe.push(s, ex, pv)

        def causal_blocks(qt):
            out = []
            for kb in range(4 * qt + 4):
                i = kb - 4 * qt
                if i <= 0:
                    out.append((kb, 0, 512, (4 * qt - kb) * 128))
                else:
                    out.append((kb, 128 * i, 512 - 128 * i, 0))
            return out

        def bias_terms(G, bh, q0, N, D):
            if D >= 256:
                return [], CBH[:, bh:bh + 1]
            return [(IDENT[:, :], G[:, D:D + N])], None

        def write_mix(dst, src, scale=1.0):
            A('dve', lambda e: e.tensor_scalar(out=dst, in0=src, scalar1=scale, scalar2=None, op0=ALU.mult), [src], [dst])

        def apply_wout(l, mixer):
            for c in range(8):
                for tt in range(4):
                    ts = slice(tt * 512, (tt + 1) * 512)
                    ps = nps()
                    for m in range(2):
                        A('pe', lambda e, ps=ps, m=m, c=c, ts=ts: e.matmul(ps[:, :], lhsT=WO[:, m, c * 128:(c + 1) * 128],
                                                                            rhs=MIXM[:, m, ts], start=(m == 0), stop=(m == 1)),
                          [WO[:, m, c * 128:(c + 1) * 128], MIXM[:, m, ts]], [ps[:, :]])
                    A('dve', lambda e, ps=ps, c=c, ts=ts: e.tensor_tensor(out=XT[:, c, ts], in0=XT[:, c, ts], in1=ps[:, :], op=ALU.add),
                      [XT[:, c, ts], ps[:, :]], [XT[:, c, ts]])

        def mix_dst(h, ts):
            return MIXM[(h % 2) * 64:(h % 2) * 64 + 64, h // 2, ts]

        def mixer_diff(l):
            lam_init = 0.8 - 0.6 * math.exp(-0.3 * l)
            lv = PK[0:1, PK_LAM + l * 128:PK_LAM + (l + 1) * 128]
            A('dve', lambda e: e.tensor_tensor(out=LTMP[:, 0:32], in0=PK[0:1, PK_LAM + l * 128:PK_LAM + l * 128 + 32],
                                               in1=PK[0:1, PK_LAM + l * 128 + 32:PK_LAM + l * 128 + 64], op=ALU.mult),
              [lv], [LTMP[:, 0:32]])
            A('dve', lambda e: e.tensor_tensor(out=LTMP[:, 32:64], in0=PK[0:1, PK_LAM + l * 128 + 64:PK_LAM + l * 128 + 96],
                                               in1=PK[0:1, PK_LAM + l * 128 + 96:PK_LAM + l * 128 + 128], op=ALU.mult),
              [lv], [LTMP[:, 32:64]])
            A('dve', lambda e: e.reduce_sum(out=LTMP[:, 64:66], in_=LTMP[:, 0:64].rearrange("p (a b) -> p a b", a=2), axis=AX.X),
              [LTMP[:, 0:64]], [LTMP[:, 64:66]])
            A('act', lambda e: e.activation(out=LTMP[:, 66:68], in_=LTMP[:, 64:66], func=AF.Exp), [LTMP[:, 64:66]], [LTMP[:, 66:68]])
            A('dve', lambda e: e.scalar_tensor_tensor(out=LTMP[:, 68:69], in0=LTMP[:, 67:68], scalar=-lam_init, in1=LTMP[:, 66:67],
                                                      op0=ALU.add, op1=ALU.subtract),
              [LTMP[:, 66:68]], [LTMP[:, 68:69]])
            ps = mps()
            A('pe', lambda e: e.matmul(ps[0:64, 0:1], lhsT=ONER[0:1, 0:64], rhs=LTMP[0:1, 68:69], start=True, stop=True),
              [ONER[0:1, 0:64], LTMP[0:1, 68:69]], [ps[0:64, 0:1]])
            A('dve', lambda e: e.tensor_copy(out=LAMC[:, l:l + 1], in_=ps[0:64, 0:1]), [ps[0:64, 0:1]], [LAMC[:, l:l + 1]])
            if stop <= 1:
                return
            if stop <= 2:
                return
            A('dve', lambda e: e.memset(KT[:, :, :], 0.0), [], [KT[:, :, :]])
            for c in range(2):
                for tt in range(4):
                    ts = slice(tt * 512, (tt + 1) * 512)
                    ps = nps()
                    for k in range(8):
                        A('pe', lambda e, ps=ps, k=k, ts=ts, c=c: e.matmul(ps[:, :], lhsT=WIN[:, k, 256 + c * 128:256 + (c + 1) * 128],
                                                                            rhs=HT[:, k, ts], start=(k == 0), stop=(k == 7)),
                          [WIN[:, k, 256 + c * 128:256 + (c + 1) * 128], HT[:, k, ts]], [ps[:, :]])
                    for b0 in (0, 64):
                        A('act', lambda e, ps=ps, b0=b0, c=c, ts=ts: e.activation(out=KT[b0:b0 + 32, c, ts], in_=ps[b0:b0 + 32, :], func=AF.Copy),
                          [ps[b0:b0 + 32, :]], [KT[b0:b0 + 32, c, ts]])
                    A('dve', lambda e, ps=ps, c=c, ts=ts: e.tensor_copy(out=KX[:, c, ts], in_=ps[:, :]), [ps[:, :]], [KX[:, c, ts]])
                    for b0 in (0, 64):
                        A('dve', lambda e, b0=b0, c=c, ts=ts: e.memset(KX[b0:b0 + 32, c, ts], 0.0), [], [KX[b0:b0 + 32, c, ts]])
            if stop <= 3:
                return
            proj_tm(512, 256, 0)
            QW = av(72 * KB, [2, SEQ], BF16)
            QZ = [QT[:, 0, :], QT[:, 1, :], QW[:, 0, :], QW[:, 1, :]]
            for c in range(2):
                psl = []
                for tt in range(4):
                    ts = slice(tt * 512, (tt + 1) * 512)
                    ps = nps()
                    psl.append(ps)
                    for k in range(8):
                        A('pe', lambda e, ps=ps, k=k, ts=ts, c=c: e.matmul(ps[:, :], lhsT=WIN[:, k, c * 128:(c + 1) * 128], rhs=HT[:, k, ts],
                                                                            start=(k == 0), stop=(k == 7)),
                          [WIN[:, k, c * 128:(c + 1) * 128], HT[:, k, ts]], [ps[:, :]])
                for tt in range(4):
                    ts = slice(tt * 512, (tt + 1) * 512)
                    ps = psl[tt]
                    A('act', lambda e, ps=ps, ts=ts, c=c: e.activation(out=QZ[2 * c][0:64, ts], in_=ps[0:64, :], func=AF.Copy),
                      [ps[0:64, :]], [QZ[2 * c][0:64, ts]])
                    A('dve', lambda e, ps=ps, ts=ts, c=c: e.tensor_copy(out=QZ[2 * c + 1][64:128, ts], in_=ps[64:128, :]),
                      [ps[64:128, :]], [QZ[2 * c + 1][64:128, ts]])
            for hh in range(4):
                ob = 64 - (hh % 2) * 64
                A('dve', lambda e, hh=hh, ob=ob: e.memset(QZ[hh][ob:ob + 64, :], 0.0), [], [QZ[hh][ob:ob + 64, :]])
            after_proj()
            if stop <= 4:
                return
            G = [av(GREG + i * 1280, [640], BF16) for i in range(4)]
            for h in range(4):
                load_g('n', 8 + h, G[h][:, :], slot=h)
            if stop <= 5:
                return
            P.phase = P.phase.split('/')[0] + '/attn'
            MISC[0] = [PS[7]]
            EPDEF[0] = 1
            FILL[0] = 512 if filler else 0
            pipe = Pipe()
            sc = 1.0 / math.sqrt(32.0)
            for h in range(4):
                b0 = (h % 2) * 64
                for qt in range(4):
                    t0 = qt * 512
                    blocks = causal_blocks(qt)
                    gi = h * 4 + qt
                    O1, O2 = (PS[3], PS[4]) if gi % 2 == 0 else (PS[5], PS[6])
                    for half, (Kt, O) in enumerate(((KT, O1), (KX, O2))):
                        for bi, (kb, q0, N, D) in enumerate(blocks):
                            kT = Kt[:, h // 2, kb * 128:(kb + 1) * 128]
                            qT = QZ[h][:, t0 + q0:t0 + q0 + N]
                            ext, cb = bias_terms(G[h], 8 + h, q0, N, D)
                            last = (bi == len(blocks) - 1)
                            ep = None
                            if last and half == 1:
                                def ep(h=h, qt=qt, O1=O1, O2=O2):
                                    ts = slice(qt * 512, (qt + 1) * 512)
                                    r1, r2, o1, o2 = FT[0], FT[1], FT[2], FT[3]
                                    A('dve', lambda e: e.reciprocal(out=r1[0:64, :], in_=O1[64:128, :]), [O1[64:128, :]], [r1[0:64, :]])
                                    A('dve', lambda e: e.tensor_tensor(out=o1[0:64, :], in0=O1[0:64, :], in1=r1[0:64, :], op=ALU.mult),
                                      [O1[0:64, :], r1[0:64, :]], [o1[0:64, :]])
                                    recip_act(r2[0:64, :], O2[64:128, :])
                                    A('dve', lambda e: e.tensor_tensor(out=o2[0:64, :], in0=O2[0:64, :], in1=r2[0:64, :], op=ALU.mult),
                                      [O2[0:64, :], r2[0:64, :]], [o2[0:64, :]])
                                    A('dve', lambda e: e.scalar_tensor_tensor(out=o1[0:64, :], in0=o2[0:64, :], scalar=LAMC[:, l:l + 1],
                                                                              in1=o1[0:64, :], op0=ALU.mult, op1=ALU.add),
                                      [o2[0:64, :], LAMC[:, l:l + 1], o1[0:64, :]], [o1[0:64, :]])
                                    sq = SQ[sqi[0] % 2]
                                    sqi[0] += 1
                                    A('pool', lambda e: e.tensor_tensor(out=sq[0:64, :], in0=o1[0:64, :], in1=o1[0:64, :], op=ALU.mult), [o1[0:64, :]], [sq[0:64, :]])

                                    def ep2():
                                        ps = mps()
                                        A('pe', lambda e: e.matmul(ps[0:64, :], lhsT=ONES64[0:64, :], rhs=sq[0:64, :], start=True, stop=True),
                                          [ONES64[0:64, :], sq[0:64, :]], [ps[0:64, :]])
                                        A('act', lambda e: e.activation(out=r1[0:64, :], in_=ps[0:64, :], func=AF.Ln, bias=EPSC[0:64, :]),
                                          [ps[0:64, :], EPSC[0:64, :]], [r1[0:64, :]])
                                        A('act', lambda e: e.activation(out=r1[0:64, :], in_=r1[0:64, :], func=AF.Exp, scale=-0.5), [r1[0:64, :]], [r1[0:64, :]])
                                        A('dve', lambda e: e.scalar_tensor_tensor(out=o2[0:64, :], in0=o1[0:64, :], scalar=PK[0:64, PK_SUBLN + l:PK_SUBLN + l + 1],
                                                                                  in1=r1[0:64, :], op0=ALU.mult, op1=ALU.mult),
                                          [o1[0:64, :], PK[0:64, PK_SUBLN + l:PK_SUBLN + l + 1], r1[0:64, :]], [o2[0:64, :]])
                                        write_mix(mix_dst(h, ts), o2[0:64, :], scale=(1.0 - lam_init))
                                    pipe.defer(ep2, 9)
                            sm_tile(pipe, [(kT, qT)] + ext, q0, N, sc, cb, vones(kb, h * 64), O, bi == 0, last, epilogue=ep)
            pipe.flush()
            FILL[0] = 0
            MISC[0] = [PS[6], PS[7]]

        def mixer_moba(l):
            for c in range(2):
                proj_fm(c * 128, 128, lambda tt, c=c: QT[:, c, tt * 512:(tt + 1) * 512], scale=0.125)
            proj_k_padded(256)
            A('dve', lambda e: e.memset(MBT[96:128, :], 0.0), [], [MBT[96:128, :]])
            for hh in range(4):
                bb = (hh % 2) * 64
                A('dve', lambda e, hh=hh, bb=bb: e.reduce_sum(out=KM[bb:bb + 64, hh // 2, :],
                                                               in_=KZ[hh][bb:bb + 64, :].rearrange("p (b s) -> p b s", b=8), axis=AX.X),
                  [KZ[hh][bb:bb + 64, :]], [KM[bb:bb + 64, hh // 2, :]])
            for c in range(2):
                A('act', lambda e, c=c: e.activation(out=KMB[:, c, :], in_=KM[:, c, :], func=AF.Copy, scale=1.0 / 256.0),
                  [KM[:, c, :]], [KMB[:, c, :]])
            G = [av(GREG + i * 1280, [640], BF16) for i in range(4)]
            for h in range(4):
                load_g('n', h, G[h][:, :], slot=h)
            P.phase = P.phase.split('/')[0] + '/attn'
            EPDEF[0] = 3
            pipe = Pipe()
            gsets = [(MB8, MB, GATE, TOP8)] + GS_EXTRA
            gcnt = [0]

            def gating(h, tb):
                b0 = (h % 2) * 64
                own = tb // 2
                mb8, mb, gate, top8 = gsets[gcnt[0] % 4]
                gcnt[0] += 1
                tsl = slice(tb * 128, (tb + 1) * 128)
                msl = slice((h % 2) * 1024 + tb * 128 - 1024, (h % 2) * 1024 + (tb + 1) * 128 - 1024)
                A('pool', lambda e: e.memset(mb8[:, 0:8], -1.0), [], [mb8[:, 0:8]])
                A('pool', lambda e: e.memset(mb8[:, own:own + 1], 0.0), [], [mb8[:, own:own + 1]])
                ps = mps()
                A('pe', lambda e: e.matmul(ps[:, 0:8], lhsT=QT[b0:b0 + 64, h // 2, tsl], rhs=KMB[b0:b0 + 64, h // 2, :], start=True, stop=True),
                  [QT[b0:b0 + 64, h // 2, tsl], KMB[b0:b0 + 64, h // 2, :]], [ps[:, 0:8]])
                A('pool', lambda e: e.memset(gate[:, 0:8], -1e30), [], [gate[:, 0:8]])
                A('dve', lambda e: e.tensor_copy(out=gate[:, 0:own], in_=ps[:, 0:own]), [ps[:, 0:own]], [gate[:, 0:own]])
                A('dve', lambda e: e.max(out=top8[:, :], in_=gate[:, 0:8]), [gate[:, 0:8]], [top8[:, :]])
                A('dve', lambda e: e.tensor_scalar(out=mb8[:, 0:own], in0=gate[:, 0:own], scalar1=top8[:, 2:3], scalar2=-1.0,
                                                   op0=ALU.is_ge, op1=ALU.add),
                  [gate[:, 0:own], top8[:, 2:3]], [mb8[:, 0:own]])
                for r4 in range(4):
                    A('pool', lambda e, r4=r4: e.tensor_copy(out=mb[:, r4:32:4], in_=mb8[:, 0:8]), [mb8[:, 0:8]], [mb[:, r4:32:4]])

                def part2():
                    ps2 = mps()
                    A('pe', lambda e: e.matmul(ps2[0:32, 0:128], lhsT=mb[:, 0:32], rhs=BIGI[:, :], start=True, stop=True),
                      [mb[:, 0:32], BIGI[:, :]], [ps2[0:32, 0:128]])
                    A('dve', lambda e: e.tensor_copy(out=MBT[64:96, msl], in_=ps2[0:32, 0:128]), [ps2[0:32, 0:128]], [MBT[64:96, msl]])
                return part2

            def qcopy(hh):
                bb = (hh % 2) * 64
                dst = MBT[0:64, (hh % 2) * 1024:(hh % 2 + 1) * 1024]
                src = QT[bb:bb + 64, hh // 2, 1024:2048]
                A('dve', lambda e: e.tensor_copy(out=dst, in_=src), [src], [dst])

            def kcopy(hh):
                bb = (hh % 2) * 64
                src = KZ[hh][bb:bb + 64, :]
                A('dve', lambda e: e.tensor_copy(out=OHS[0:64, :], in_=src), [src], [OHS[0:64, :]])
            ogc = [0]
            qcopy(0)
            kcopy(0)
            pending = [(0, tb) for tb in range(8, 16)]
            p2s = []
            for (hh, tb) in pending:
                p2s.append(gating(hh, tb))
                if len(p2s) > 2:
                    p2s.pop(0)()
            while p2s:
                p2s.pop(0)()
            proj_tm(512, 256, 0)
            after_proj()
            for h in range(4):
                b0 = (h % 2) * 64
                pending = [(h + 1, tb) for tb in range(8, 16)] if h < 3 else []
                tcount = 0
                if h < 3:
                    qcopy(h + 1)
                for qt in (2, 3, 0, 1):
                    if qt == 0 and h < 3:
                        kcopy(h + 1)
                    t0 = qt * 512
                    blocks = causal_blocks(qt)
                    O = OB[ogc[0] % 3]
                    ogc[0] += 1
                    for bi, (kb, q0, N, D) in enumerate(blocks):
                        kT = KZ[h][:, kb * 128:(kb + 1) * 128]
                        qT = QT[:, h // 2, t0 + q0:t0 + q0 + N]
                        ext, cb = bias_terms(G[h], h, q0, N, D)
                        if qt >= 2:
                            m0 = (h % 2) * 1024 + t0 + q0 - 1024
                            kT = OHS[:, kb * 128:(kb + 1) * 128]
                            qT = MBT[:, m0:m0 + N]
                        last = (bi == len(blocks) - 1)
                        ep = None
                        if last:
                            def ep(h=h, qt=qt, O=O):
                                ts = slice(qt * 512, (qt + 1) * 512)
                                r1, o1 = FT[0], FT[2]
                                A('dve', lambda e: e.reciprocal(out=r1[0:64, :], in_=O[64:128, :]), [O[64:128, :]], [r1[0:64, :]])
                                A('dve', lambda e: e.tensor_tensor(out=o1[0:64, :], in0=O[0:64, :], in1=r1[0:64, :], op=ALU.mult),
                                  [O[0:64, :], r1[0:64, :]], [o1[0:64, :]])
                                write_mix(mix_dst(h, ts), o1[0:64, :])
                        sm_tile(pipe, [(kT, qT)] + ext, q0, N, 1.0, cb, vones(kb, h * 64), O, bi == 0, last, epilogue=ep)
                        tcount += 1
                        if tcount % 2 == 1 and (len(p2s) > 2 or (p2s and not pending)):
                            p2s.pop(0)()
                        if pending and tcount % 2 == 0:
                            hh, tb = pending.pop(0)
                            p2s.append(gating(hh, tb))
                while pending or p2s:
                    if pending:
                        hh, tb = pending.pop(0)
                        p2s.append(gating(hh, tb))
                    if p2s:
                        p2s.pop(0)()
            pipe.flush()
            FILL[0] = 0

        def mixer_sb(l):
            for c in range(2):
                proj_fm(c * 128, 128, lambda tt, c=c: QT[:, c, tt * 512:(tt + 1) * 512], scale=0.125)
            proj_k_padded(256)
            proj_tm(512, 256, 0)
            after_proj()
            P.phase = P.phase.split('/')[0] + '/attn'
            ZA = [PS[0], PS[1], PS[7]]
            WB = [PS[2], PS[3]]
            CBK = PS[4]
            OBK = [PS[5], PS[6]]
            SPB3 = [SPB[0], SPB[1], SQ[0], SQ[1]]
            FTS = [(FT[0], FT[1]), (FT[2], FT[3]), (MBT[:, 0:1024].bitcast(F32), MBT[:, 1024:2048].bitcast(F32))]
            blks = []
            for h in range(4):
                for qt in range(4):
                    blocks = list(reversed(causal_blocks(qt)))
                    for bi, (kb, q0, N, D) in enumerate(blocks):
                        blks.append((h, qt, bi, len(blocks), kb, q0, N))
            stA, stB1, stC1, stB2, stC2 = [], [], [], [], []
            for i, (h, qt, bi, nb, kb, q0, N) in enumerate(blks):
                b0 = (h % 2) * 64
                t0 = qt * 512
                O = OBK[(h * 4 + qt) % 2]
                diag = (kb >= 4 * qt)
                first = (bi == 0)
                last = (bi == nb - 1)
                za = ZA[i % 3][:, q0:q0 + N]
                wb = WB[i % 2][:, q0:q0 + N]
                cb = CBK[:, q0:q0 + N]
                t1t, t2t = FTS[i % 3]
                t1 = t1t[:, q0:q0 + N]
                t2 = t2t[:, q0:q0 + N]
                arg = t1
                spb_t = SPB3[i % 4]
                spb = spb_t[:, q0:q0 + N]
                pt_t = PT[i % 3]
                pts = pt_t[:, q0:q0 + N]
                kT = KZ[h][:, kb * 128:(kb + 1) * 128]
                qT = QT[:, h // 2, t0 + q0:t0 + q0 + N]

                def fA(za=za, kT=kT, qT=qT, t1=t1, t2=t2, diag=diag, spb_t=spb_t, t2t=t2t, q0=q0, N=N, spb=spb):
                    A('pe', lambda e: e.matmul(za, lhsT=kT, rhs=qT, start=True, stop=True), [kT, qT], [za])
                    A('act', lambda e: e.activation(out=t1, in_=za, func=AF.Exp), [za], [t1])
                    A('act', lambda e: e.activation(out=t2, in_=t1, func=AF.Ln, bias=ONEC[:, :]), [t1, ONEC[:, :]], [t2])
                    if diag:
                        d0 = spb_t[:, q0:q0 + 128]
                        s0 = t2t[:, q0:q0 + 128]
                        A('pool', lambda e: e.affine_select(out=d0, in_=s0, pattern=[[1, 128]], compare_op=ALU.is_gt,
                                                            fill=0.0, base=0, channel_multiplier=-1), [s0], [d0])
                        if N > 128:
                            d1 = spb_t[:, q0 + 128:q0 + N]
                            s1 = t2t[:, q0 + 128:q0 + N]
                            A('pool', lambda e: e.tensor_copy(out=d1, in_=s1), [s1], [d1])
                    else:
                        A('pool', lambda e: e.tensor_copy(out=spb, in_=t2), [t2], [spb])

                def fB1(wb=wb, za=za, spb=spb, t2=t2, arg=arg):
                    A('pe', lambda e: e.matmul(wb, lhsT=NEGU[:, :], rhs=spb, start=True, stop=True), [NEGU[:, :], spb], [wb])
                    A('dve', lambda e: e.tensor_tensor(out=arg, in0=za, in1=t2, op=ALU.subtract), [za, t2], [arg])
                    A('dve', lambda e: e.tensor_tensor(out=arg, in0=arg, in1=wb, op=ALU.add), [arg, wb], [arg])

                def fC1(cb=cb, spb=spb, first=first, last=last):
                    if first:
                        A('pe', lambda e: e.matmul(CBK[:, :], lhsT=ZLH[:, :], rhs=HT[:, 0, 0:512], start=True, stop=True, skip_group_check=True),
                          [ZLH[:, :], HT[:, 0, 0:512]], [CBK[:, :]])
                    if not last:
                        A('pe', lambda e: e.matmul(cb, lhsT=ONE1[:, :], rhs=spb, start=False, stop=True, skip_group_check=True), [ONE1[:, :], spb], [cb])

                def fB2(diag=diag, q0=q0, first=first, t1t=t1t, arg=arg, pts=pts, pt_t=pt_t):
                    c0 = q0 + 128 if diag else q0
                    if not first and c0 < 512:
                        cbs = CBK[:, c0:512]
                        args = t1t[:, c0:512]
                        A('dve', lambda e: e.tensor_tensor(out=args, in0=args, in1=cbs, op=ALU.subtract), [args, cbs], [args])
                    A('act', lambda e: e.activation(out=pts, in_=arg, func=AF.Exp), [arg], [pts])
                    if diag:
                        a0 = pt_t[:, q0:q0 + 128]
                        A('pool', lambda e: e.affine_select(out=a0, in_=a0, pattern=[[1, 128]], compare_op=ALU.is_gt,
                                                            fill=0.0, base=0, channel_multiplier=-1), [a0], [a0])

                def fC2(O=O, q0=q0, N=N, kb=kb, h=h, qt=qt, pts=pts, first=first, last=last):
                    vl = VT[:, kb, h, :]
                    osl = O[:, q0:q0 + N]
                    if first:
                        A('pe', lambda e: e.matmul(O[:, :], lhsT=ZLH[:, :], rhs=HT[:, 0, 0:512], start=True, stop=True, skip_group_check=True),
                          [ZLH[:, :], HT[:, 0, 0:512]], [O[:, :]])
                    A('pe', lambda e: e.matmul(osl, lhsT=vl, rhs=pts, start=False, stop=True, skip_group_check=True), [vl, pts], [osl])
                    if last:
                        ts = slice(qt * 512, (qt + 1) * 512)
                        write_mix(mix_dst(h, ts), O[0:64, :])
                stA.append(fA)
                stB1.append(fB1)
                stC1.append(fC1)
                stB2.append(fB2)
                stC2.append(fC2)
            nblk = len(blks)
            for step in range(-3, nblk):
                if 0 <= step + 3 < nblk:
                    stA[step + 3]()
                if 0 <= step + 1 < nblk:
                    stB1[step + 1]()
                if 0 <= step < nblk:
                    stC1[step]()
                if 0 <= step + 1 < nblk:
                    stB2[step + 1]()
                if 0 <= step < nblk:
                    stC2[step]()

        def mixer_nsa(l):
            for c in range(2):
                proj_fm(c * 128, 128, lambda tt, c=c: QT[:, c, tt * 512:(tt + 1) * 512], scale=0.125)
            proj_fm(256, 128, lambda tt: KX[:, 0, tt * 512:(tt + 1) * 512])
            proj_fm(384, 64, lambda tt: OHS[0:64, tt * 512:(tt + 1) * 512])
            A('dve', lambda e: e.memset(KT[64:128, 0, :], 0.0), [], [KT[64:128, 0, :]])
            A('dve', lambda e: e.memset(KX[0:64, 1, :], 0.0), [], [KX[0:64, 1, :]])
            A('dve', lambda e: e.memset(MBT[96:128, :], 0.0), [], [MBT[96:128, :]])
            A('dve', lambda e: e.memset(KC[:, :, :], 0.0), [], [KC[:, :, :]])
            for tt in range(4):
                ts = slice(tt * 512, (tt + 1) * 512)
                ps = nps()
                for k in range(8):
                    A('pe', lambda e, ps=ps, k=k, ts=ts: e.matmul(ps[:, :], lhsT=WIN[:, k, 512:640], rhs=HT[:, k, ts], start=(k == 0), stop=(k == 7)),
                      [WIN[:, k, 512:640], HT[:, k, ts]], [ps[:, :]])
                A('act', lambda e, ps=ps, ts=ts: e.activation(out=KT[0:64, 0, ts], in_=ps[0:64, :], func=AF.Copy), [ps[0:64, :]], [KT[0:64, 0, ts]])
                A('dve', lambda e, ps=ps, ts=ts: e.tensor_copy(out=KX[64:128, 1, ts], in_=ps[64:128, :]), [ps[64:128, :]], [KX[64:128, 1, ts]])
            SGB = KT[0:12, 1, :]
            for tt in range(4):
                ts = slice(tt * 512, (tt + 1) * 512)
                ps = nps()
                for k in range(8):
                    A('pe', lambda e, ps=ps, k=k, ts=ts: e.matmul(ps[0:12, :], lhsT=WIN[:, k, 640:652], rhs=HT[:, k, ts],
                                                                  start=(k == 0), stop=(k == 7)),
                      [WIN[:, k, 640:652], HT[:, k, ts]], [ps[0:12, :]])
                A('act', lambda e, ps=ps, ts=ts: e.activation(out=SGB[:, ts], in_=ps[0:12, :], func=AF.Sigmoid), [ps[0:12, :]], [SGB[:, ts]])
            proj_tm(652, 128, 0)
            P.phase = P.phase.split('/')[0] + '/cmp'
            P.dma('pool', W1[0:64, :, :], ck1_d[l].rearrange("(l d) h -> d l h", d=64), semkey='dma_w1')
            P.dma('pool', W1[64:128, :, :], cv1_d[l].rearrange("(l d) h -> d l h", d=64), semkey='dma_w1')
            for kv, w2d in enumerate((ck2_d, cv2_d)):
                for dup in range(2):
                    P.dma('pool', W2[:, kv, :, dup * 64:(dup + 1) * 64], w2d[l].rearrange("(a p) d -> p a d", p=128), semkey='dma_w2')
            A('dve', lambda e: e.tensor_copy(out=POSB[:, :], in_=PK[:, PK_POS + l * 32:PK_POS + (l + 1) * 32]),
              [PK[:, PK_POS + l * 32:PK_POS + (l + 1) * 32]], [POSB[:, :]])
            for kv in range(2):
                b0 = 64 * kv
                for half in range(2):
                    ps = nps()
                    ps2 = nps()
                    for li in range(32):
                        lw = W1[b0:b0 + 64, li, half * 128:(half + 1) * 128]
                        xs = KX[b0:b0 + 64, 0, li:li + 16 * 126 + 1:16]
                        A('pe', lambda e, ps=ps, lw=lw, xs=xs, li=li: e.matmul(ps[:, 0:127], lhsT=lw, rhs=xs, start=(li == 0), stop=(li == 31)),
                          [lw, xs], [ps[:, 0:127]])
                    for li in range(32):
                        lw = W1[b0:b0 + 64, li, half * 128:(half + 1) * 128]
                        pb = POSB[b0:b0 + 64, li:li + 1]
                        A('pe', lambda e, ps2=ps2, lw=lw, pb=pb, li=li: e.matmul(ps2[:, 0:1], lhsT=lw, rhs=pb, start=(li == 0), stop=(li == 31)),
                          [lw, pb], [ps2[:, 0:1]])
                    bc = B1[:, kv * 2 + half:kv * 2 + half + 1]
                    A('dve', lambda e, ps2=ps2, bc=bc: e.tensor_copy(out=bc, in_=ps2[:, 0:1]), [ps2[:, 0:1]], [bc])
                    x, u, v = FT[0][:, 0:127], FT[1][:, 0:127], FT[2][:, 0:127]
                    A('act', lambda e, ps=ps, bc=bc, x=x: e.activation(out=x, in_=ps[:, 0:127], func=AF.Identity, bias=bc), [ps[:, 0:127], bc], [x])
                    A('dve', lambda e, x=x, u=u: e.tensor_tensor(out=u, in0=x, in1=x, op=ALU.mult), [x], [u])
                    A('dve', lambda e, u=u: e.tensor_scalar(out=u, in0=u, scalar1=0.044715, scalar2=1.0, op0=ALU.mult, op1=ALU.add), [u], [u])
                    A('dve', lambda e, x=x, u=u, v=v: e.tensor_tensor(out=v, in0=u, in1=x, op=ALU.mult), [u, x], [v])
                    A('act', lambda e, v=v, u=u: e.activation(out=u, in_=v, func=AF.Tanh, scale=GELU_C), [v], [u])
                    A('dve', lambda e, u=u: e.tensor_scalar(out=u, in0=u, scalar1=1.0, scalar2=0.5, op0=ALU.add, op1=ALU.mult), [u], [u])
                    gl = GEL[:, kv * 2 + half, 0:127]
                    A('dve', lambda e, x=x, u=u, gl=gl: e.tensor_tensor(out=gl, in0=u, in1=x, op=ALU.mult), [u, x], [gl])
            ps = nps()
            for half in range(2):
                A('pe', lambda e, ps=ps, half=half: e.matmul(ps[:, 0:127], lhsT=W2[:, 0, half, :], rhs=GEL[:, half, 0:127], start=(half == 0), stop=(half == 1)),
                  [W2[:, 0, half, :], GEL[:, half, 0:127]], [ps[:, 0:127]])
            A('act', lambda e, ps=ps: e.activation(out=KC[0:64, 0, 0:127], in_=ps[0:64, 0:127], func=AF.Copy), [ps[0:64, 0:127]], [KC[0:64, 0, 0:127]])
            A('dve', lambda e, ps=ps: e.tensor_copy(out=KC[64:128, 1, 0:127], in_=ps[64:128, 0:127]), [ps[64:128, 0:127]], [KC[64:128, 1, 0:127]])
            ps = nps()
            for half in range(2):
                A('pe', lambda e, ps=ps, half=half: e.matmul(ps[0:127, 0:64], lhsT=GEL[:, 2 + half, 0:127], rhs=W2[:, 1, half, 0:64], start=(half == 0), stop=(half == 1)),
                  [GEL[:, 2 + half, 0:127], W2[:, 1, half, 0:64]], [ps[0:127, 0:64]])
            A('dve', lambda e, ps=ps: e.tensor_copy(out=VC[0:127, 0:64], in_=ps[0:127, 0:64]), [ps[0:127, 0:64]], [VC[0:127, 0:64]])
            after_proj()
            P.dma('pool', WO[:, :, :], w_out_d[l, 512:768, :].rearrange("(m p) c -> p m c", p=128), semkey='dma_wo')
            GC = av(GREG, [SEQ], BF16)
            GWs = [av(GREG + i * 2048, [1024], BF16) for i in range(2)]
            GNs = [av(GREG + 4096 + i * 1280, [640], BF16) for i in range(2)]
            IMPB = IMP[:, :, :].rearrange("p a b -> p (a b)").bitcast(BF16)
            GCs = [IMPB[:, i * 512:(i + 1) * 512] for i in range(2)]

            def cmp_terms(h, ts, gc=None):
                b0 = (h % 2) * 64
                g = GC[0:127, ts] if gc is None else gc[0:127, :]
                return [(KC[:, h % 2, 0:127], QT[:, h // 2, ts]), (IDENT[0:127, 0:127], g)]
            P.phase = P.phase.split('/')[0] + '/pass1'
            for h in range(4):
                load_g('c', h, GC[:, :], slot=0)
                for qt in range(4):
                    ts = slice(qt * 512, (qt + 1) * 512)
                    z = ZB[zi[0] % 3]
                    zi[0] += 1
                    pt = PT[pti[0] % 3]
                    pti[0] += 1
                    terms = cmp_terms(h, ts)
                    for i, (a, bb) in enumerate(terms):
                        A('pe', lambda e, a=a, bb=bb, i=i, z=z: e.matmul(z[0:127, :], lhsT=a, rhs=bb, start=(i == 0), stop=(i == 1)), [a, bb], [z[0:127, :]])
                    A('act', lambda e, z=z, pt=pt: e.activation(out=pt[0:127, :], in_=z[0:127, :], func=AF.Exp), [z[0:127, :]], [pt[0:127, :]])
                    for t4 in range(4):
                        tb = qt * 4 + t4
                        ps = mps()
                        A('pe', lambda e, ps=ps, pt=pt, t4=t4: e.matmul(ps[:, 0:33], lhsT=pt[0:127, t4 * 128:(t4 + 1) * 128], rhs=COVER[0:127, 0:33], start=True, stop=True),
                          [pt[0:127, t4 * 128:(t4 + 1) * 128], COVER[0:127, 0:33]], [ps[:, 0:33]])
                        A('dve', lambda e, ps=ps: e.tensor_scalar(out=IMR[:, :], in0=ps[:, 32:33], scalar1=1e-30, scalar2=None, op0=ALU.max), [ps[:, 32:33]], [IMR[:, :]])
                        A('dve', lambda e: e.reciprocal(out=IMR[:, :], in_=IMR[:, :]), [IMR[:, :]], [IMR[:, :]])
                        if h == 0:
                            A('dve', lambda e, ps=ps, tb=tb: e.tensor_scalar(out=IMP[:, tb, :], in0=ps[:, 0:32], scalar1=IMR[:, 0:1], scalar2=None, op0=ALU.mult),
                              [ps[:, 0:32], IMR[:, :]], [IMP[:, tb, :]])
                        else:
                            A('dve', lambda e, ps=ps, tb=tb: e.scalar_tensor_tensor(out=IMP[:, tb, :], in0=ps[:, 0:32], scalar=IMR[:, 0:1], in1=IMP[:, tb, :],
                                                                                    op0=ALU.mult, op1=ALU.add),
                              [ps[:, 0:32], IMR[:, :], IMP[:, tb, :]], [IMP[:, tb, :]])
            P.phase = P.phase.split('/')[0] + '/sel'
            for tb in range(16):
                tsl = slice(tb * 128, (tb + 1) * 128)
                oa, ob = 2 * tb, 2 * tb + 1
                A('pool', lambda e: e.memset(GATE[:, :], -1e30), [], [GATE[:, :]])
                if oa > 0:
                    A('dve', lambda e, tb=tb, oa=oa: e.tensor_copy(out=GATE[0:64, 0:oa], in_=IMP[0:64, tb, 0:oa]), [IMP[0:64, tb, 0:oa]], [GATE[0:64, 0:oa]])
                A('dve', lambda e, tb=tb, ob=ob: e.tensor_copy(out=GATE[64:128, 0:ob], in_=IMP[64:128, tb, 0:ob]), [IMP[64:128, tb, 0:ob]], [GATE[64:128, 0:ob]])
                A('dve', lambda e: e.max(out=TOP8[:, :], in_=GATE[:, :]), [GATE[:, :]], [TOP8[:, :]])
                A('dve', lambda e: e.tensor_scalar(out=MB[:, :], in0=GATE[:, :], scalar1=TOP8[:, 2:3], scalar2=-1.0, op0=ALU.is_ge, op1=ALU.add),
                  [GATE[:, :], TOP8[:, 2:3]], [MB[:, :]])
                A('dve', lambda e, oa=oa: e.memset(MB[0:64, oa:32], -1.0), [], [MB[0:64, oa:32]])
                A('dve', lambda e, oa=oa: e.memset(MB[0:64, oa:oa + 1], 0.0), [], [MB[0:64, oa:oa + 1]])
                A('dve', lambda e, ob=ob: e.memset(MB[64:128, ob:32], -1.0), [], [MB[64:128, ob:32]])
                A('dve', lambda e, ob=ob: e.memset(MB[64:128, ob:ob + 1], 0.0), [], [MB[64:128, ob:ob + 1]])
                ps = mps()
                A('pe', lambda e, ps=ps: e.matmul(ps[0:32, 0:128], lhsT=MB[:, :], rhs=BIGI[:, :], start=True, stop=True), [MB[:, :], BIGI[:, :]], [ps[0:32, 0:128]])
                A('act', lambda e, ps=ps, tsl=tsl: e.activation(out=MBT[64:96, tsl], in_=ps[0:32, 0:128], func=AF.Copy), [ps[0:32, 0:128]], [MBT[64:96, tsl]])
            P.phase = P.phase.split('/')[0] + '/pass2'
            EPDEF[0] = 1
            pipe = Pipe()
            obi = [0]

            def branch_ep(h, ts, br, O):
                def ep():
                    r, r2, t, acc = FT[0], FT[1], FT[2], FT[3]
                    recip_act(r[0:64, :], O[64:128, :])
                    ps = mps()
                    A('pe', lambda e: e.matmul(ps[0:64, :], lhsT=SELB[0:12, br * 4 + h, :], rhs=SGB[:, ts], start=True, stop=True),
                      [SELB[0:12, br * 4 + h, :], SGB[:, ts]], [ps[0:64, :]])
                    A('dve', lambda e: e.tensor_tensor(out=r2[0:64, :], in0=ps[0:64, :], in1=r[0:64, :], op=ALU.mult),
                      [ps[0:64, :], r[0:64, :]], [r2[0:64, :]])
                    dst = acc if br == 0 else t
                    A('dve', lambda e: e.tensor_tensor(out=dst[0:64, :], in0=O[0:64, :], in1=r2[0:64, :], op=ALU.mult),
                      [O[0:64, :], r2[0:64, :]], [dst[0:64, :]])
                    if br > 0:
                        A('pool', lambda e: e.tensor_tensor(out=acc[0:64, :], in0=acc[0:64, :], in1=t[0:64, :], op=ALU.add),
                          [acc[0:64, :], t[0:64, :]], [acc[0:64, :]])
                    if br == 2:
                        write_mix(mix_dst(h, ts), acc[0:64, :])
                return ep

            def next_o():
                o = OB[obi[0] % 3]
                obi[0] += 1
                return o
            def load_gc_slice(h, qt, dst, slot):
                src = bass.AP(scr_c[h].tensor, 2032 + qt * 512, [[L_C, 128], [1, 512]])
                P.add('pool', lambda e: e.dma_start(out=dst, in_=src), reads=[scr_c[h][:]], writes=[dst], dma=True,
                      semkey='dma_gcs%d' % slot)

            def load_head(h):
                load_g('n', 4 + h, GNs[h % 2][:, :], slot=2 + (h % 2))
                load_g('w', h, GWs[h % 2][:, :], slot=4 + (h % 2))
            load_head(0)
            gci = [0]
            load_gc_slice(0, 0, GCs[0], 0)
            for h in range(4):
                b0 = (h % 2) * 64
                GN = GNs[h % 2]
                GW = GWs[h % 2]
                if h == 0:
                    A('dve', lambda e: e.tensor_copy(out=MBT[0:64, :], in_=QT[0:64, 0, :]), [QT[0:64, 0, :]], [MBT[0:64, :]])
                if h + 1 < 4:
                    load_head(h + 1)
                for qt in range(4):
                    t0 = qt * 512
                    ts = slice(t0, t0 + 512)
                    gcs = GCs[gci[0] % 2]
                    gci[0] += 1
                    nh, nq = (h, qt + 1) if qt < 3 else (h + 1, 0)
                    if nh < 4:
                        load_gc_slice(nh, nq, GCs[gci[0] % 2], gci[0] % 2)
                    Oc = next_o()
                    sm_tile(pipe, cmp_terms(h, ts, gcs), 0, 512, 1.0, None, VC[0:127, :], Oc, True, True, nk=127,
                            epilogue=branch_ep(h, ts, 0, Oc))
                    blocks = causal_blocks(qt)
                    Os = next_o()
                    for bi, (kb, q0, N, D) in enumerate(blocks):
                        kT = OHS[:, kb * 128:(kb + 1) * 128]
                        qT = MBT[:, t0 + q0:t0 + q0 + N]
                        ext, cb = bias_terms(GN, 4 + h, q0, N, D)
                        last = (bi == len(blocks) - 1)
                        sm_tile(pipe, [(kT, qT)] + ext, q0, N, 1.0, cb, VT[:, kb, 0, :], Os, bi == 0, last,
                                epilogue=(branch_ep(h, ts, 1, Os) if last else None))
                    if qt == 3 and h < 3:
                        nb0 = ((h + 1) % 2) * 64
                        A('dve', lambda e, nb0=nb0, h=h: e.tensor_copy(out=MBT[0:64, :], in_=QT[nb0:nb0 + 64, (h + 1) // 2, :]),
                          [QT[nb0:nb0 + 64, (h + 1) // 2, :]], [MBT[0:64, :]])
                    wblocks = [(kb, q0, N, D) for (kb, q0, N, D) in blocks if D <= 512]
                    Ow = next_o()
                    for bi, (kb, q0, N, D) in enumerate(wblocks):
                        kT = (KT[:, 0, kb * 128:(kb + 1) * 128] if h % 2 == 0 else KX[:, 1, kb * 128:(kb + 1) * 128])
                        qT = QT[:, h // 2, t0 + q0:t0 + q0 + N]
                        last = (bi == len(wblocks) - 1)
                        sm_tile(pipe, [(kT, qT), (IDENT[:, :], GW[:, D:D + N])], q0, N, 1.0, None, VT[:, kb, 1, :], Ow, bi == 0, last,
                                epilogue=(branch_ep(h, ts, 2, Ow) if last else None))
                pipe.flush()
            FILL[0] = 0

        AFTER_PROJ = [None]

        def after_proj():
            if AFTER_PROJ[0]:
                AFTER_PROJ[0]()
                AFTER_PROJ[0] = None

        MIXERS = {0: mixer_sb, 1: mixer_moba, 2: mixer_nsa, 3: mixer_diff}

        for l in range(depth):
            P.phase = 'L%d norm' % l
            A('pool', lambda e: e.memset(VT[:, :, :, 64:128], 1.0), [], [VT[:, :, :, 64:128]])
            if mixers:
                for tt in range(4):
                    rmsnorm_tile(tt, PK_NA + l * 8, lambda c, tt=tt: (HT[:, c, tt * 512:(tt + 1) * 512], None))
            for mixer in mixers:
                P.phase = 'L%d mixer%d' % (l, mixer)
                mi = list(mixers).index(mixer)
                if mi == 0:
                    P.dma('pool', WIN[:, :, :], w_in_d[l, mixer].rearrange("(c p) f -> p c f", p=128), semkey='dma_win')
                if mi + 1 < len(mixers):
                    nxt = mixers[mi + 1]
                    AFTER_PROJ[0] = (lambda l=l, nxt=nxt: P.dma('pool', WIN[:, :, :], w_in_d[l, nxt].rearrange("(c p) f -> p c f", p=128), semkey='dma_win'))
                else:
                    AFTER_PROJ[0] = None
                if mixer != 2:
                    P.dma('pool', WO[:, :, :], w_out_d[l, mixer * 256:(mixer + 1) * 256, :].rearrange("(m p) c -> p m c", p=128), semkey='dma_wo')
                MIXERS[mixer](l)
                if dbg == (l, mixer) and cfg.get('dump'):
                    for nm in cfg['dump']:
                        src = {'QT': QT, 'KT': KT, 'KX': KX, 'HT0': HT[:, 0:2, :], 'HT1': HT[:, 2:4, :]}[nm]
                        dd = nc.dram_tensor("dump_" + nm, [128, 2 * SEQ] if nm != 'MBT' else [64, SEQ], BF16, kind="ExternalOutput").ap()
                        if nm == 'MBT':
                            P.dma('sp', dd[:, :], src[:, :], semkey='dma_out')
                        else:
                            P.dma('sp', dd.rearrange("p (a b) -> p a b", a=2), src, semkey='dma_out')
                if dbg == (l, mixer):
                    for m in range(2):
                        P.dma('sp', dbg_d[m * 128:(m + 1) * 128, :], MIXM[:, m, :], semkey='dma_out')
                P.phase = 'L%d wout%d' % (l, mixer)
                apply_wout(l, mixer)
            if do_mlp:
                P.phase = 'L%d mlp' % l
                for tt in range(4):
                    rmsnorm_tile(tt, PK_NM + l * 8, lambda c, tt=tt: (HT[:, c, tt * 512:(tt + 1) * 512], None))
                rli = 0
                for fg in range(8):
                    wu, wd, at = WUP[fg % 2], WDN[fg % 2], AT[fg % 2]
                    P.dma('pool', wu[:, :, :], w_up_d[l, :, fg * 512:(fg + 1) * 512].rearrange("(c p) f -> p c f", p=128), semkey='dma_wu%d' % (fg % 2))
                    P.dma('pool', wd[:, :, :], w_down_d[l, fg * 512:(fg + 1) * 512, :].rearrange("(m p) c -> p m c", p=128), semkey='dma_wd%d' % (fg % 2))
                    for tt in range(4):
                        ts = slice(tt * 512, (tt + 1) * 512)
                        for m in range(4):
                            ps = nps()
                            for k in range(8):
                                extra = []
                                if probe_wait:
                                    dd = DUM[:, k:k + 1]
                                    A('dve', lambda e, dd=dd: e.memset(dd, 0.0), [], [dd])
                                    extra = [dd]
                                pb0 = pb if (k % 2 == 0) else 0
                                A('pe', lambda e, ps=ps, k=k, m=m, wu=wu, ts=ts, pb0=pb0: e.matmul(
                                    ps[:, :], lhsT=wu[pb0:pb0 + pk, k, m * 128:(m + 1) * 128], rhs=HT[pb0:pb0 + pk, k, ts], start=(k == 0), stop=(k == 7)),
                                  [wu[:, k, m * 128:(m + 1) * 128], HT[:, k, ts]] + extra, [ps[:, :]])
                            rl = RL[rli % 2]
                            rli += 1
                            A('act', lambda e, ps=ps, rl=rl: e.activation(out=rl[:, :], in_=ps[:, :], func=AF.Relu), [ps[:, :]], [rl[:, :]])
                            sq_eng = 'dve' if (m % 2 == 0) else 'pool'
                            A(sq_eng, lambda e, rl=rl, at=at, m=m, ts=ts: e.tensor_tensor(out=at[:, m, ts], in0=rl[:, :], in1=rl[:, :], op=ALU.mult),
                              [rl[:, :]], [at[:, m, ts]])
                    for c in range(8):
                        for tt in range(4):
                            ts = slice(tt * 512, (tt + 1) * 512)
                            ps = nps()
                            for m in range(4):
                                A('pe', lambda e, ps=ps, m=m, c=c, wd=wd, at=at, ts=ts: e.matmul(
                                    ps[:, :], lhsT=wd[:, m, c * 128:(c + 1) * 128], rhs=at[:, m, ts], start=(m == 0), stop=(m == 3)),
                                  [wd[:, m, c * 128:(c + 1) * 128], at[:, m, ts]], [ps[:, :]])
                            A('dve', lambda e, ps=ps, c=c, ts=ts: e.tensor_tensor(out=XT[:, c, ts], in0=XT[:, c, ts], in1=ps[:, :], op=ALU.add),
                              [XT[:, c, ts], ps[:, :]], [XT[:, c, ts]])

        P.phase = 'final'
        oi = [0]
        for tt in range(4):
            ts = slice(tt * 512, (tt + 1) * 512)

            def dstf(c, ts=ts):
                ob = OUTB[oi[0] % 2]
                oi[0] += 1

                def post(ob=ob, c=c, ts=ts):
                    P.dma('sp', outT_d[c * 128:(c + 1) * 128, ts], ob[:, :], semkey='dma_out')
                return ob[:, :], post
            rmsnorm_tile(tt, PK_NF, dstf)
        P.final_dma_keys.append('dma_out')
        n = P.emit()
    _CACHE['prog'] = P
    return nc, n


def make_in_maps(inputs, nb=None):
    x = np.asarray(inputs['x'], np.float32)
    B = x.shape[0] if nb is None else nb
    pk = pack_small(inputs)
    hc = host_consts()
    shared = {
        'pk': pk,
        'w_in': permute_w_in(np.asarray(inputs['w_in'], np.float32)),
        'w_out': np.ascontiguousarray(inputs['w_out'], np.float32),
        'w_up': np.ascontiguousarray(inputs['w_up'], np.float32),
        'w_down': np.ascontiguousarray(inputs['w_down'], np.float32),
        'cmp_k_w1': np.ascontiguousarray(inputs['cmp_k_w1'], np.float32),
        'cmp_k_w2': np.ascontiguousarray(inputs['cmp_k_w2'], np.float32),
        'cmp_v_w1': np.ascontiguousarray(inputs['cmp_v_w1'], np.float32),
        'cmp_v_w2': np.ascontiguousarray(inputs['cmp_v_w2'], np.float32),
    }
    shared.update(hc)
    in_maps = []
    for b in range(B):
        m = dict(shared)
        m['xT'] = np.ascontiguousarray(x[b].T)
        in_maps.append(m)
    return in_maps


def kernel(**inputs):
    if 'nc' not in _CACHE:
        _CACHE['nc'] = build()[0]
    nc = _CACHE['nc']
    in_maps = make_in_maps(inputs)
    B = len(in_maps)
    res = run_bass_kernel_spmd(nc, in_maps, core_ids=list(range(B)))
    out = np.stack([np.ascontiguousarray(r['outT'].T) for r in res.results], axis=0)
    return out.astype(np.float32)
```
